# Optimizing a Trainium2 kernel written in Bass

```python
import jax
import jax.numpy as jnp
from jax import lax
import numpy as np

D_MODEL = 1024
BATCH = 4
SEQ = 8192
DEPTH = 1
DEC_BATCH = 32
DEC_SEQ = 8
PAST_LEN = 16384
PAGE_SIZE = 128

D_CONV = D_MODEL // 2
CONV_K = 31
N_HEADS = 8
N_KV_HEADS = 2
HEAD_DIM = 64
GROUP = N_HEADS // N_KV_HEADS
Q_W = N_HEADS * HEAD_DIM
KV_W = N_KV_HEADS * HEAD_DIM
BLOCK = 64
N_SEL = 16
WINDOW = 512
ROPE_THETA = 10000.0
D_FF = ((8 * D_MODEL + 3 * 256 - 1) // (3 * 256)) * 256
Q_BLOCK = 128
EPS = 1e-6
FORCED_BONUS = 2.0 * GROUP
OFF_Q = 2 * D_CONV
OFF_KV = OFF_Q + Q_W
OFF_NG = OFF_KV + 6 * KV_W
OFF_MG = OFF_NG + 3 * N_HEADS
D_IN = OFF_MG + 2 * D_MODEL

kernel_name = 'hybrid_conformer_nsa_adaln_decoder_step'


def rms_norm(x, g):
    xf = x.astype(jnp.float32)
    y = xf * lax.rsqrt(jnp.mean(xf * xf, axis=-1, keepdims=True) + EPS)
    return (y * g.astype(jnp.float32)).astype(x.dtype)


def layer_norm(x, g, b):
    xf = x.astype(jnp.float32)
    mu = jnp.mean(xf, axis=-1, keepdims=True)
    var = jnp.mean(jnp.square(xf - mu), axis=-1, keepdims=True)
    y = (xf - mu) * lax.rsqrt(var + EPS)
    return (y * g.astype(jnp.float32) + b.astype(jnp.float32)).astype(x.dtype)


def rope(x, pos):
    half = HEAD_DIM // 2
    inv = jnp.power(ROPE_THETA, -jnp.arange(half, dtype=jnp.float32) / half)
    ang = pos.astype(jnp.float32)[:, None] * inv[None, :]
    cos = jnp.cos(ang)[None, :, None, :]
    sin = jnp.sin(ang)[None, :, None, :]
    xf = x.astype(jnp.float32)
    x1, x2 = xf[..., :half], xf[..., half:]
    return jnp.concatenate([x1 * cos - x2 * sin, x2 * cos + x1 * sin], axis=-1).astype(x.dtype)


def masked_softmax(s, mask):
    s = jnp.where(mask, s, -jnp.inf)
    m = jnp.max(s, axis=-1, keepdims=True)
    m = jnp.where(jnp.isfinite(m), m, 0.0)
    e = jnp.where(mask, jnp.exp(s - m), 0.0)
    return e / jnp.maximum(jnp.sum(e, axis=-1, keepdims=True), 1e-30)


def compress(rows, pos_mod, w):
    B, L, H, D = rows.shape
    blocks = rows.reshape(B, L // BLOCK, BLOCK, H, D)
    summ = jnp.mean((blocks * (1.0 + pos_mod)[None, None, :, None, :]).astype(jnp.float32), axis=2)
    return jnp.einsum('bnhd,de->bnhe', summ.astype(rows.dtype), w)


def front(x, c, pos, p):
    B, T, _ = x.shape
    mod = jax.nn.silu(c) @ p['w_ada'] + p['b_ada']
    sh1, sc1, g1, sh2, sc2, g2 = jnp.split(mod[:, None, :], 6, axis=-1)
    h = rms_norm(x, p['mix_norm_g']) * (1.0 + sc1) + sh1
    z = h @ p['w_in']
    glu, q, kv, ng, mg = jnp.split(z, [OFF_Q, OFF_KV, OFF_NG, OFF_MG], axis=-1)
    glu_a, glu_b = jnp.split(glu, 2, axis=-1)
    u = glu_a * jax.nn.sigmoid(glu_b)
    q = rms_norm(q.reshape(B, T, N_HEADS, HEAD_DIM), p['q_norm_g'])
    q_plain = q.reshape(B, T, N_KV_HEADS, GROUP, HEAD_DIM)
    q_rope = rope(q, pos).reshape(B, T, N_KV_HEADS, GROUP, HEAD_DIM)
    kv = kv.reshape(B, T, 6, N_KV_HEADS, HEAD_DIM)
    kng = p['k_norm_g']
    g_conv, g_nsa = jnp.split(jax.nn.sigmoid(mg), 2, axis=-1)
    return dict(
        g1=g1, sh2=sh2, sc2=sc2, g2=g2, u=u, q_plain=q_plain, q_rope=q_rope,
        k_cmp=rms_norm(kv[:, :, 0], kng[0]), v_cmp=kv[:, :, 1],
        k_sel=rope(rms_norm(kv[:, :, 2], kng[1]), pos), v_sel=kv[:, :, 3],
        k_win=rope(rms_norm(kv[:, :, 4], kng[2]), pos), v_win=kv[:, :, 5],
        branch_g=jax.nn.sigmoid(ng).reshape(B, T, N_KV_HEADS, GROUP, 3),
        g_conv=g_conv, g_nsa=g_nsa)


def conv_branch(u, u_prev, p):
    ext = jnp.concatenate([u_prev.astype(u.dtype), u], axis=1)
    y = lax.conv_general_dilated(ext, p['w_dw'][:, None, :].astype(u.dtype), window_strides=(1,),
                                 padding='VALID', dimension_numbers=('NWC', 'WIO', 'NWC'),
                                 feature_group_count=D_CONV) + p['b_dw']
    y = jax.nn.silu(layer_norm(y, p['conv_ln_g'], p['conv_ln_b']))
    return y @ p['w_pw2'], ext[:, -(CONV_K - 1):]


def nsa_attend(q_plain, q_rope, qpos, kc, vc, k_pool, v_pool, block_table, kw, vw, kwpos, branch_g):
    B, Q = q_plain.shape[:2]
    NB = kc.shape[1]
    scale = HEAD_DIM ** -0.5
    blk = jnp.arange(NB, dtype=jnp.int32)
    cur = qpos // BLOCK
    s_cmp = jnp.einsum('bqhgd,bnhd->bqhgn', q_plain, kc).astype(jnp.float32) * scale
    cmp_ok = (blk[None, :] + 1) * BLOCK <= qpos[:, None] + 1
    p_cmp = masked_softmax(s_cmp, cmp_ok[None, :, None, None, :])
    o_cmp = jnp.einsum('bqhgn,bnhd->bqhgd', p_cmp.astype(vc.dtype), vc)
    imp = jnp.sum(p_cmp, axis=3)
    visible = blk[None, :] <= cur[:, None]
    forced = (blk[None, :] == 0) | (blk[None, :] == cur[:, None]) | (blk[None, :] == cur[:, None] - 1)
    bonus = jnp.where(forced, FORCED_BONUS, 0.0)
    score = jnp.where(visible[None, :, None, :], imp + bonus[None, :, None, :], -1.0)
    _, idx = lax.top_k(score, min(N_SEL, NB))
    n_top = idx.shape[-1]
    phys = jax.vmap(lambda bt, i: bt[i])(block_table, idx)
    hidx = jnp.arange(N_KV_HEADS)[None, None, :, None]
    kg = k_pool[phys, :, hidx]
    vg = v_pool[phys, :, hidx]
    kpos = idx[..., None] * BLOCK + jnp.arange(BLOCK, dtype=jnp.int32)
    sel_ok = (kpos <= qpos[None, :, None, None, None]).reshape(B, Q, N_KV_HEADS, 1, n_top * BLOCK)
    s_sel = jnp.einsum('bqhgd,bqhnkd->bqhgnk', q_rope, kg).astype(jnp.float32)
    s_sel = s_sel.reshape(B, Q, N_KV_HEADS, GROUP, n_top * BLOCK) * scale
    p_sel = masked_softmax(s_sel, sel_ok)
    o_sel = jnp.einsum('bqhgm,bqhmd->bqhgd', p_sel.astype(vg.dtype),
                       vg.reshape(B, Q, N_KV_HEADS, n_top * BLOCK, HEAD_DIM))
    s_win = jnp.einsum('bqhgd,blhd->bqhgl', q_rope, kw).astype(jnp.float32) * scale
    win_ok = ((kwpos[None, :] <= qpos[:, None]) & (kwpos[None, :] > qpos[:, None] - WINDOW)
              & (kwpos[None, :] >= 0))
    p_win = masked_softmax(s_win, win_ok[None, :, None, None, :])
    o_win = jnp.einsum('bqhgl,blhd->bqhgd', p_win.astype(vw.dtype), vw)
    g = branch_g
    return g[..., 0:1] * o_cmp + g[..., 1:2] * o_sel + g[..., 2:3] * o_win


def back(x, f, u_out, o, p):
    B, T = x.shape[:2]
    nsa = o.reshape(B, T, Q_W) @ p['w_nsa_o']
    mix = (f['g_conv'] * u_out + f['g_nsa'] * nsa) @ p['w_out']
    x = x + f['g1'] * mix
    h = rms_norm(x, p['ffn_norm_g']) * (1.0 + f['sc2']) + f['sh2']
    ffn = (jax.nn.silu(h @ p['w_gate']) * (h @ p['w_up'])) @ p['w_down']
    return x + f['g2'] * ffn


def prompt_layer(x, c, p):
    B, T, _ = x.shape
    pos = jnp.arange(T, dtype=jnp.int32)
    f = front(x, c, pos, p)
    u_out, conv_state = conv_branch(f['u'], jnp.zeros((B, CONV_K - 1, D_CONV), x.dtype), p)
    kc = compress(f['k_cmp'], p['cmp_mod_k'], p['cmp_w_k'])
    vc = compress(f['v_cmp'], p['cmp_mod_v'], p['cmp_w_v'])
    nb = T // BLOCK
    k_pool = f['k_sel'].reshape(B * nb, BLOCK, N_KV_HEADS, HEAD_DIM)
    v_pool = f['v_sel'].reshape(B * nb, BLOCK, N_KV_HEADS, HEAD_DIM)
    block_table = jnp.arange(B * nb, dtype=jnp.int32).reshape(B, nb)
    pad = ((0, 0), (WINDOW, 0), (0, 0), (0, 0))
    kw_pad = jnp.pad(f['k_win'], pad)
    vw_pad = jnp.pad(f['v_win'], pad)

    def one_block(i):
        s0 = i * Q_BLOCK
        qpos = s0 + jnp.arange(Q_BLOCK, dtype=jnp.int32)
        sl = lambda a: lax.dynamic_slice_in_dim(a, s0, Q_BLOCK, axis=1)
        kw = lax.dynamic_slice_in_dim(kw_pad, s0, WINDOW + Q_BLOCK, axis=1)
        vw = lax.dynamic_slice_in_dim(vw_pad, s0, WINDOW + Q_BLOCK, axis=1)
        kwpos = s0 - WINDOW + jnp.arange(WINDOW + Q_BLOCK, dtype=jnp.int32)
        return nsa_attend(sl(f['q_plain']), sl(f['q_rope']), qpos, kc, vc, k_pool, v_pool,
                          block_table, kw, vw, kwpos, sl(f['branch_g']))

    o = lax.map(one_block, jnp.arange(T // Q_BLOCK, dtype=jnp.int32))
    o = jnp.moveaxis(o, 0, 1).reshape(B, T, N_KV_HEADS, GROUP, HEAD_DIM)
    y = back(x, f, u_out, o, p)
    keep = min(WINDOW, T)
    return y, (f['k_cmp'], f['v_cmp'], f['k_sel'], f['v_sel'],
               f['k_win'][:, -keep:], f['v_win'][:, -keep:], conv_state)


def sample_layer(x, c, cache_cmp_k, cache_cmp_v, cache_sel_k, cache_sel_v,
                 cache_win_k, cache_win_v, state_conv, page_table, p):
    B, T, _ = x.shape
    n_pages = page_table.shape[1]
    past_len = n_pages * PAGE_SIZE
    pos = past_len + jnp.arange(T, dtype=jnp.int32)
    f = front(x, c, pos, p)
    u_out, conv_state = conv_branch(f['u'], state_conv, p)
    n_new = -(-T // BLOCK)
    new_pad = ((0, 0), (0, n_new * BLOCK - T), (0, 0), (0, 0))
    past_k = cache_cmp_k[page_table].reshape(B, past_len, N_KV_HEADS, HEAD_DIM)
    past_v = cache_cmp_v[page_table].reshape(B, past_len, N_KV_HEADS, HEAD_DIM)
    kc = jnp.concatenate([compress(past_k, p['cmp_mod_k'], p['cmp_w_k']),
                          compress(jnp.pad(f['k_cmp'], new_pad), p['cmp_mod_k'], p['cmp_w_k'])], axis=1)
    vc = jnp.concatenate([compress(past_v, p['cmp_mod_v'], p['cmp_w_v']),
                          compress(jnp.pad(f['v_cmp'], new_pad), p['cmp_mod_v'], p['cmp_w_v'])], axis=1)
    bpp = PAGE_SIZE // BLOCK
    n_pool_blocks = cache_sel_k.shape[0] * bpp
    k_pool = jnp.concatenate([cache_sel_k.reshape(n_pool_blocks, BLOCK, N_KV_HEADS, HEAD_DIM),
                              jnp.pad(f['k_sel'], new_pad).reshape(B * n_new, BLOCK, N_KV_HEADS, HEAD_DIM)], axis=0)
    v_pool = jnp.concatenate([cache_sel_v.reshape(n_pool_blocks, BLOCK, N_KV_HEADS, HEAD_DIM),
                              jnp.pad(f['v_sel'], new_pad).reshape(B * n_new, BLOCK, N_KV_HEADS, HEAD_DIM)], axis=0)
    past_tbl = (page_table[:, :, None] * bpp + jnp.arange(bpp, dtype=jnp.int32)).reshape(B, n_pages * bpp)
    new_tbl = n_pool_blocks + jnp.arange(B * n_new, dtype=jnp.int32).reshape(B, n_new)
    block_table = jnp.concatenate([past_tbl, new_tbl], axis=1)
    buf = cache_win_k.shape[1]
    kw = jnp.concatenate([cache_win_k.astype(x.dtype), f['k_win']], axis=1)
    vw = jnp.concatenate([cache_win_v.astype(x.dtype), f['v_win']], axis=1)
    kwpos = past_len - buf + jnp.arange(buf + T, dtype=jnp.int32)
    o = nsa_attend(f['q_plain'], f['q_rope'], pos, kc, vc, k_pool, v_pool, block_table,
                   kw, vw, kwpos, f['branch_g'])
    y = back(x, f, u_out, o, p)
    keep = min(WINDOW, buf + T)
    return y, (f['k_cmp'], f['v_cmp'], f['k_sel'], f['v_sel'], kw[:, -keep:], vw[:, -keep:], conv_state)


def setup_inputs(seed: int = 0) -> dict:
    key = jax.random.key(seed)
    ks = iter(jax.random.split(key, 48))
    nrm = lambda shape, s: s * jax.random.normal(next(ks), shape, jnp.float32)
    n_pages = PAST_LEN // PAGE_SIZE
    used = DEC_BATCH * n_pages
    n_pool = used + max(1, used // 4)
    win_buf = min(WINDOW, PAST_LEN)
    L = DEPTH
    kv_pool = (L, n_pool, PAGE_SIZE, N_KV_HEADS, HEAD_DIM)
    kv_win = (L, DEC_BATCH, win_buf, N_KV_HEADS, HEAD_DIM)
    inputs = {}
    inputs['x_prompt'] = nrm((BATCH, SEQ, D_MODEL), 1.0)
    inputs['x_sample'] = nrm((DEC_BATCH, DEC_SEQ, D_MODEL), 1.0)
    inputs['cache_cmp_k'] = nrm(kv_pool, 1.0)
    inputs['cache_cmp_v'] = nrm(kv_pool, 1.0)
    inputs['cache_sel_k'] = nrm(kv_pool, 1.0)
    inputs['cache_sel_v'] = nrm(kv_pool, 1.0)
    inputs['cache_win_k'] = nrm(kv_win, 1.0)
    inputs['cache_win_v'] = nrm(kv_win, 1.0)
    inputs['state_conv'] = nrm((L, DEC_BATCH, CONV_K - 1, D_CONV), 0.5)
    inputs['page_table'] = jax.random.permutation(next(ks), n_pool)[:used].reshape(DEC_BATCH, n_pages).astype(jnp.int32)
    inputs['c_prompt'] = nrm((BATCH, D_MODEL), 1.0)
    inputs['c_sample'] = nrm((DEC_BATCH, D_MODEL), 1.0)
    inputs['w_ada'] = nrm((L, D_MODEL, 6 * D_MODEL), 0.5 * D_MODEL ** -0.5)
    inputs['b_ada'] = nrm((L, 6 * D_MODEL), 0.01)
    inputs['mix_norm_g'] = 1.0 + nrm((L, D_MODEL), 0.01)
    inputs['w_in'] = nrm((L, D_MODEL, D_IN), D_MODEL ** -0.5)
    inputs['q_norm_g'] = 1.0 + nrm((L, HEAD_DIM), 0.01)
    inputs['k_norm_g'] = 1.0 + nrm((L, 3, HEAD_DIM), 0.01)
    inputs['w_dw'] = nrm((L, CONV_K, D_CONV), CONV_K ** -0.5)
    inputs['b_dw'] = nrm((L, D_CONV), 0.01)
    inputs['conv_ln_g'] = 1.0 + nrm((L, D_CONV), 0.01)
    inputs['conv_ln_b'] = nrm((L, D_CONV), 0.01)
    inputs['w_pw2'] = nrm((L, D_CONV, D_MODEL), D_CONV ** -0.5)
    inputs['cmp_mod_k'] = nrm((L, BLOCK, HEAD_DIM), 0.1)
    inputs['cmp_w_k'] = nrm((L, HEAD_DIM, HEAD_DIM), HEAD_DIM ** -0.5)
    inputs['cmp_mod_v'] = nrm((L, BLOCK, HEAD_DIM), 0.1)
    inputs['cmp_w_v'] = nrm((L, HEAD_DIM, HEAD_DIM), HEAD_DIM ** -0.5)
    inputs['w_nsa_o'] = nrm((L, Q_W, D_MODEL), Q_W ** -0.5)
    inputs['w_out'] = nrm((L, D_MODEL, D_MODEL), D_MODEL ** -0.5)
    inputs['ffn_norm_g'] = 1.0 + nrm((L, D_MODEL), 0.01)
    inputs['w_gate'] = nrm((L, D_MODEL, D_FF), D_MODEL ** -0.5)
    inputs['w_up'] = nrm((L, D_MODEL, D_FF), D_MODEL ** -0.5)
    inputs['w_down'] = nrm((L, D_FF, D_MODEL), D_FF ** -0.5)
    return inputs


def reference(x_prompt, x_sample, cache_cmp_k, cache_cmp_v, cache_sel_k, cache_sel_v,
              cache_win_k, cache_win_v, state_conv, page_table, c_prompt, c_sample,
              w_ada, b_ada, mix_norm_g, w_in, q_norm_g, k_norm_g, w_dw, b_dw, conv_ln_g,
              conv_ln_b, w_pw2, cmp_mod_k, cmp_w_k, cmp_mod_v, cmp_w_v, w_nsa_o, w_out,
              ffn_norm_g, w_gate, w_up, w_down):
    yp, ys = x_prompt, x_sample
    st_p, st_s = [], []
    for l in range(DEPTH):
        p = dict(w_ada=w_ada[l], b_ada=b_ada[l], mix_norm_g=mix_norm_g[l], w_in=w_in[l],
                 q_norm_g=q_norm_g[l], k_norm_g=k_norm_g[l], w_dw=w_dw[l], b_dw=b_dw[l],
                 conv_ln_g=conv_ln_g[l], conv_ln_b=conv_ln_b[l], w_pw2=w_pw2[l],
                 cmp_mod_k=cmp_mod_k[l], cmp_w_k=cmp_w_k[l], cmp_mod_v=cmp_mod_v[l],
                 cmp_w_v=cmp_w_v[l], w_nsa_o=w_nsa_o[l], w_out=w_out[l],
                 ffn_norm_g=ffn_norm_g[l], w_gate=w_gate[l], w_up=w_up[l], w_down=w_down[l])
        yp, sp = prompt_layer(yp, c_prompt, p)
        ys, ss = sample_layer(ys, c_sample, cache_cmp_k[l], cache_cmp_v[l], cache_sel_k[l],
                              cache_sel_v[l], cache_win_k[l], cache_win_v[l], state_conv[l],
                              page_table, p)
        st_p.append(sp)
        st_s.append(ss)
    stk = lambda lst, j: jnp.stack([s[j] for s in lst])
    return (yp, ys,
            stk(st_p, 0), stk(st_p, 1), stk(st_p, 2), stk(st_p, 3), stk(st_p, 4), stk(st_p, 5), stk(st_p, 6),
            stk(st_s, 0), stk(st_s, 1), stk(st_s, 2), stk(st_s, 3), stk(st_s, 4), stk(st_s, 5), stk(st_s, 6))
```

```python
import numpy as np
import ml_dtypes
from contextlib import ExitStack
import concourse.bass as bass
import concourse.mybir as mybir
from concourse.bass_utils import run_bass_kernel_spmd

F32 = mybir.dt.float32
BF = mybir.dt.bfloat16
I32 = mybir.dt.int32
ALU = mybir.AluOpType
AF = mybir.ActivationFunctionType
AX = mybir.AxisListType

D = 1024
NT = 32
OFF_Q = 1024
OFF_KV = 1536
OFF_NG = 2304
OFF_MG = 2328
D_IN = 4376
D_FF = 2816
NFF = 22
EPS = 1e-6
BIG = 30000.0
SCALE = 0.125
CONV_K = 31


class Op:
    __slots__ = ("eng", "fn", "deps", "signal", "val", "sem", "is_dma", "idx")


class Prog:
    ENGS = ("pe", "act", "dve", "pool", "sp")

    def __init__(self, nc, es):
        self.nc = nc
        self.es = es
        self.ops = {e: [] for e in self.ENGS}
        self.lastw = {}
        self.readers = {}
        self.dma_sems = {}
        self.n = 0
        self.bar = None

    def add(self, eng, fn, r=(), w=(), sem=None):
        op = Op()
        op.eng, op.fn, op.signal, op.val, op.sem = eng, fn, False, None, None
        op.is_dma = sem is not None
        op.idx = self.n
        self.n += 1
        deps = []
        for k in list(r) + list(w):
            if k not in self.lastw and k not in self.readers and self.bar is not None:
                deps.append((self.bar, True))
        for k in r:
            d = self.lastw.get(k)
            if d is not None:
                deps.append((d, True))
        for k in w:
            d = self.lastw.get(k)
            if d is not None:
                deps.append((d, False))
            for d in self.readers.get(k, ()):
                deps.append((d, False))
        seen = set()
        op.deps = []
        for d, raw in sorted(deps, key=lambda t: not t[1]):
            if id(d) in seen or d is op:
                continue
            if d.eng == eng and not d.is_dma and not op.is_dma:
                if eng == "pe" or not raw:
                    continue
            seen.add(id(d))
            if d.is_dma:
                if sem is not None and d.sem == sem:
                    continue
                op.deps.append((d.sem, self.dma_sems[d.sem][1]))
            else:
                d.signal = True
                op.deps.append(d)
        if op.is_dma:
            if sem not in self.dma_sems:
                self.dma_sems[sem] = [self.es.enter_context(self.nc.semaphore("d_" + sem)), 0]
            self.dma_sems[sem][1] += 16
            op.sem = sem
            op.val = self.dma_sems[sem][1]
        for k in r:
            self.readers.setdefault(k, []).append(op)
        for k in w:
            self.lastw[k] = op
            self.readers[k] = []
        self.ops[eng].append(op)
        return op

    def barrier(self, fn):
        keys = set(self.lastw) | set(self.readers)
        op = self.add("dve", fn, r=[], w=sorted(keys))
        self.bar = op
        return op

    def emit(self, final_sems=()):
        nc = self.nc
        esem = {e: self.es.enter_context(nc.semaphore("e_" + e)) for e in self.ENGS}
        for e in self.ENGS:
            c = 0
            for op in self.ops[e]:
                if op.signal and not op.is_dma:
                    c += 1
                    op.val = c
        block = self.es.enter_context(nc.Block())
        prog = self

        def run(e, eng):
            waited = {}
            for op in prog.ops[e]:
                for d in op.deps:
                    if isinstance(d, tuple):
                        key, h, v = "d_" + d[0], prog.dma_sems[d[0]][0], d[1]
                    else:
                        key, h, v = "e_" + d.eng, esem[d.eng], d.val
                    if waited.get(key, 0) >= v:
                        continue
                    waited[key] = v
                    eng.wait_ge(h, v)
                ins = op.fn(eng)
                if op.is_dma:
                    ins.then_inc(prog.dma_sems[op.sem][0], 16)
                elif op.signal:
                    ins.then_inc(esem[e], 1)
            if e == "sp":
                for s in final_sems:
                    h, v = prog.dma_sems[s]
                    eng.wait_ge(h, v)

        @block.tensor
        def _(eng):
            run("pe", eng)

        @block.scalar
        def _(eng):
            run("act", eng)

        @block.vector
        def _(eng):
            run("dve", eng)

        @block.gpsimd
        def _(eng):
            run("pool", eng)

        @block.sync
        def _(eng):
            run("sp", eng)


def build(n_own=NT, do_sample=True):
    nc = bass.Bass("TRN2", target_bir_lowering=False)
    es = ExitStack()
    P = Prog(nc, es)

    def din(name, shape, dt=F32):
        return nc.dram_tensor(name, list(shape), dt, kind="ExternalInput").ap()

    def dout(name, shape, dt=F32):
        return nc.dram_tensor(name, list(shape), dt, kind="ExternalOutput").ap()

    def sb(name, shape, dt=F32):
        return es.enter_context(nc.sbuf_tensor("s_" + name, list(shape), dt))

    xo = din("xo", [NT, 128, D])
    xt = din("xt", [NT, 128, D])
    xh = din("xh", [NT, 32, D])
    cs_o = din("cs_o", [NT, 128, 64])
    cs_t = din("cs_t", [NT, 128, 64])
    hv = din("hv", [128, NT])
    m_cmp = din("m_cmp", [NT, 128, 128])
    m_add = din("m_add", [NT, 128, 128])
    m_cmpT = din("m_cmpT", [NT, 128, 128])
    masks_d = din("masks", [128, 4, 128])
    eall_d = din("eall", [128, 64 * 128])
    identf_d = din("identf", [128, 128])
    sel2_d = din("sel2", [128, 2])
    selP_d = din("selP", [5, 128])
    c5T_d = din("c5T", [128, 40])
    w_ada = din("w_ada", [D, 6 * D])
    b_ada = din("b_ada", [1, 6 * D])
    gmix = din("mix_norm_g", [1, D])
    gffn = din("ffn_norm_g", [1, D])
    w_in = din("w_in", [D, D_IN])
    gq_d = din("q_norm_g", [1, 64])
    gk_d = din("k_norm_g", [1, 192])
    wdwT_d = din("wdwT", [128, 4, CONV_K])
    cvec_d = din("cvec", [128, 3, 4])
    w_pw2 = din("w_pw2", [512, D])
    modk_d = din("modk", [128, 64])
    modv_d = din("modv", [128, 64])
    wkbd_d = din("wkbd", [128, 128])
    wvbd_d = din("wvbd", [128, 128])
    w_nsa_o = din("w_nsa_o", [512, D])
    w_out = din("w_out", [D, D])
    w_gate = din("w_gate", [D, D_FF])
    w_up = din("w_up", [D, D_FF])
    w_down = din("w_down", [D_FF, D])

    xs_d = din("xs", [32, D])
    cs_s = din("cs_s", [32, 64])
    selS_d = din("selS", [5, 32])
    cwk_d = din("cwk", [4, 512, 128])
    cwv_d = din("cwv", [4, 512, 128])
    sconv_d = din("sconv", [4, 30, 512])
    pt_d = din("pt", [1, 512], I32)
    ccmpk = din("cache_cmp_k", [5120 * 128, 128])
    ccmpv = din("cache_cmp_v", [5120 * 128, 128])
    cselk = din("cache_sel_k", [5120 * 128, 128])
    cselv = din("cache_sel_v", [5120 * 128, 128])
    madds_d = din("madd_s", [1, 256])
    mwin_d = din("mwin_s", [128, 8])
    mnew_d = din("mnew_s", [32, 32])
    ys_o = dout("ys_o", [32, D])
    skv_o = dout("skv_o", [32, 768])
    swin_o = dout("swin_o", [4, 2, 512, 128])
    sconv_o = dout("sconv_o", [4, 30, 512])
    y_o = dout("y_o", [NT, 128, D])
    kv_o = dout("kv_o", [NT, 128, 768])
    conv_o = dout("conv_o", [30, 512])
    x1_d = nc.dram_tensor("x1_scr", [NT + 1, 128, D], F32, kind="Internal").ap()
    o_d = nc.dram_tensor("o_scr", [NT + 1, 128, 512], BF, kind="Internal").ap()

    ps = [es.enter_context(nc.psum_tensor(f"ps{i}", [128, 512], F32)) for i in range(8)]

    def psb(i):
        return ps[i][:].bitcast(BF)

    identb = sb("identb", [128, 128], BF)
    identf = sb("identf", [128, 128])
    masks = sb("masks", [128, 4, 128], BF)
    sel2 = sb("sel2", [128, 2], BF)
    selP = sb("selP", [5, 128])
    selPb = sb("selPb", [5, 128], BF)
    selS = sb("selS", [5, 32])
    selSb = sb("selSb", [5, 32], BF)
    A1s = sb("A1s", [32, D])
    G1s = sb("G1s", [32, D])
    pm1 = sb("pm1", [128, 2, 64])
    gk_bc = sb("gk_bc", [128, 192])
    gq_bc = sb("gq_bc", [128, 64])
    wkbd = sb("wkbd", [128, 128], BF)
    wvbd = sb("wvbd", [128, 128], BF)
    hv_sb = sb("hv_sb", [128, NT])
    dummy = sb("dummy", [128, 1])
    nhalf = sb("nhalf", [128, 128])
    wdwT = sb("wdwT", [128, 4, CONV_K])
    cvec = sb("cvec", [128, 3, 4])
    A1 = sb("A1", [128, D])
    G1 = sb("G1", [128, D])
    zb1 = sb("zb1", [5, D_IN], BF)
    sh2T = sb("sh2T", [128, 8, 5], BF)
    a2row = sb("a2row", [5, D])
    g2row = sb("g2row", [5, D])

    def ld(dst, src, key, sem, eng="sp", extra_w=()):
        P.add(eng, lambda e, d=dst, s=src: e.dma_start(out=d, in_=s), r=[], w=[key] + list(extra_w), sem=sem)

    ld(identf[:], identf_d[:, :], "identf", "c0")
    ld(identb[:], identf_d[:, :], "identb", "c1", eng="pool")
    ld(masks[:], masks_d[:, :, :], "masks", "c3", eng="pool")
    ld(sel2[:], sel2_d[:, :], "sel2", "c4", eng="pool")
    ld(selP[:], selP_d[:, :], "selP", "c5")
    ld(selPb[:], selP_d[:, :], "selPb", "c6", eng="pool")
    ld(selS[:], selS_d[:, :], "selS", "c15")
    ld(selSb[:], selS_d[:, :], "selSb", "c16", eng="pool")
    ld(pm1[:, 0, :], modk_d[:, :], "pm1", "c7")
    ld(pm1[:, 1, :], modv_d[:, :], "pm1", "c7")
    ld(gk_bc[:], gk_d[0:1, :].to_broadcast([128, 192]), "gk_bc", "c8")
    ld(gq_bc[:], gq_d[0:1, :].to_broadcast([128, 64]), "gq_bc", "c9")
    ld(wkbd[:], wkbd_d[:, :], "wkbd", "c10", eng="pool")
    ld(wvbd[:], wvbd_d[:, :], "wvbd", "c11", eng="pool")
    ld(hv_sb[:], hv[:, :], "hv", "c12")
    ld(wdwT[:], wdwT_d[:, :, :], "wdwT", "c13")
    ld(cvec[:], cvec_d[:, :, :], "cvec", "c14")
    P.add("pool", lambda e: e.memset(nhalf[:], -0.5), w=["nhalf"])
    P.add("dve", lambda e: e.tensor_scalar(out=pm1[:], in0=pm1[:], scalar1=1.0, scalar2=None, op0=ALU.add),
          r=["pm1"], w=["pm1"])

    w_in_v = w_in.rearrange("(kc p) n -> p kc n", p=128)
    def rsqrt_pool(out_ap, in_ap, inv_n, ncol, rk, wk):
        P.add("pool", lambda e: e.tensor_scalar(out=out_ap, in0=in_ap, scalar1=float(inv_n), scalar2=float(EPS), op0=ALU.mult, op1=ALU.add),
              r=[rk], w=[wk])
        P.add("pool", lambda e: e.tensor_tensor(out=out_ap, in0=out_ap, in1=nhalf[0:out_ap.shape[0], 0:ncol], op=ALU.pow),
              r=[wk, "nhalf"], w=[wk])

    with ExitStack() as es0:
        def sb0(name, shape, dt=F32):
            return es0.enter_context(nc.sbuf_tensor("s0_" + name, list(shape), dt))
        c5T = sb0("c5T", [128, 40])
        scT = sb0("scT", [128, 40])
        tmp40 = sb0("tmp40", [128, 40])
        mod5 = sb0("mod5", [5, 6 * D])
        bada = [sb0(f"bada{i}", [5, 512]) for i in range(2)]
        wz = [sb0(f"wz{i}", [128, 8, 512], BF) for i in range(2)]
        g5 = sb0("g5", [5, 2, D])
        a5 = sb0("a5", [5, 2, D])
        wa = [sb0(f"wa{i}", [128, 8, 512]) for i in range(2)]
        sh1T = sb0("sh1T", [128, 8, 5], BF)

        ld(c5T[:], c5T_d[:, :], "c5T", "p0")
        ld(g5[:, 0, :], gmix[0:1, :].to_broadcast([5, D]), "g5", "p2")
        ld(g5[:, 1, :], gffn[0:1, :].to_broadcast([5, D]), "g5", "p2")
        P.add("act", lambda e: e.activation(out=tmp40[:], in_=c5T[:], func=AF.Tanh, scale=0.5), r=["c5T"], w=["tmp40"])
        P.add("dve", lambda e: e.tensor_scalar(out=tmp40[:], in0=tmp40[:], scalar1=0.5, scalar2=0.5, op0=ALU.mult, op1=ALU.add),
              r=["tmp40"], w=["tmp40"])
        P.add("dve", lambda e: e.tensor_tensor(out=scT[:], in0=c5T[:], in1=tmp40[:], op=ALU.mult), r=["tmp40", "c5T"], w=["scT"])
        wa_v = w_ada.rearrange("(kc p) n -> p kc n", p=128)
        for cg in range(12):
            wb = wa[cg % 2]
            wk = f"wa{cg % 2}"
            for kc in range(8):
                ld(wb[:, kc, :], wa_v[:, kc, cg * 512:(cg + 1) * 512], wk, wk)
            ld(bada[cg % 2][:], b_ada[0:1, cg * 512:(cg + 1) * 512].to_broadcast([5, 512]), f"bada{cg % 2}", f"bada{cg % 2}")

            def mm(e, wb=wb):
                for kc in range(8):
                    ins = e.matmul(ps[0][0:5, :], lhsT=scT[:, kc * 5:(kc + 1) * 5], rhs=wb[:, kc, :], start=(kc == 0), stop=(kc == 7))
                return ins
            P.add("pe", mm, r=["scT", wk], w=["ps0"])
            P.add("dve", lambda e, cg=cg: e.tensor_tensor(out=mod5[:, cg * 512:(cg + 1) * 512], in0=ps[0][0:5, :],
                                                         in1=bada[cg % 2][:], op=ALU.add),
                  r=["ps0", f"bada{cg % 2}"], w=["mod5"])
        P.add("dve", lambda e: e.scalar_tensor_tensor(out=a5[:, 0, :], in0=mod5[:, D:2 * D], scalar=1.0, in1=g5[:, 0, :], op0=ALU.add, op1=ALU.mult),
              r=["mod5", "g5"], w=["a5"])
        P.add("dve", lambda e: e.scalar_tensor_tensor(out=a5[:, 1, :], in0=mod5[:, 4 * D:5 * D], scalar=1.0, in1=g5[:, 1, :], op0=ALU.add, op1=ALU.mult),
              r=["mod5", "g5"], w=["a5"])
        P.add("pool", lambda e: e.tensor_copy(out=a2row[:], in_=a5[:, 1, :]), r=["a5"], w=["a2row"])
        P.add("pool", lambda e: e.tensor_copy(out=g2row[:], in_=mod5[:, 5 * D:6 * D]), r=["mod5"], w=["g2row"])
        for dst, key, src in ((A1, "A1", a5[:, 0, :]), (G1, "G1", mod5[:, 2 * D:3 * D])):
            for half in range(2):
                P.add("pe", lambda e, src=src, half=half: e.matmul(ps[1][:, :], lhsT=selP[:, :], rhs=src[:, half * 512:(half + 1) * 512], start=True, stop=True),
                      r=["selP", "a5", "mod5"], w=["ps1"])
                P.add("act", lambda e, dst=dst, half=half: e.copy(out=dst[:, half * 512:(half + 1) * 512], in_=ps[1][:, :]),
                      r=["ps1"], w=[key])
        for dst, key, src in ((A1s, "A1s", a5[:, 0, :]), (G1s, "G1s", mod5[:, 2 * D:3 * D])):
            for half in range(2):
                P.add("pe", lambda e, src=src, half=half: e.matmul(ps[1][0:32, :], lhsT=selS[:, :], rhs=src[:, half * 512:(half + 1) * 512], start=True, stop=True),
                      r=["selS", "a5", "mod5"], w=["ps1"])
                P.add("act", lambda e, dst=dst, half=half: e.copy(out=dst[:, half * 512:(half + 1) * 512], in_=ps[1][0:32, :]),
                      r=["ps1"], w=[key])
        for dst, key, off in ((sh1T, "sh1T", 0), (sh2T, "sh2T", 3 * D)):
            def tr(e, off=off):
                for kc in range(8):
                    ins = e.transpose(out=ps[2][:, kc * 5:(kc + 1) * 5], in_=mod5[0:5, off + kc * 128: off + (kc + 1) * 128], identity=identf[0:5, 0:5])
                return ins
            P.add("pe", tr, r=["mod5", "identf"], w=["ps2"])
            P.add("act", lambda e, dst=dst: e.copy(out=dst[:].rearrange("p a b -> p (a b)"), in_=ps[2][:, 0:40]), r=["ps2"], w=[key])
        for ci, c0 in enumerate(range(0, D_IN, 512)):
            c1 = min(c0 + 512, D_IN)
            wzb = wz[ci % 2]
            wzk = f"wz{ci % 2}"
            for kc in range(8):
                ld(wzb[:, kc, 0:c1 - c0], w_in_v[:, kc, c0:c1], wzk, wzk, eng="pool")

            def mm(e, c0=c0, c1=c1, wzb=wzb):
                for kc in range(8):
                    ins = e.matmul(ps[3][0:5, 0:c1 - c0], lhsT=sh1T[:, kc, :], rhs=wzb[:, kc, 0:c1 - c0], start=(kc == 0), stop=(kc == 7))
                return ins
            P.add("pe", mm, r=["sh1T", wzk], w=["ps3"])
            P.add("act", lambda e, c0=c0, c1=c1: e.copy(out=zb1[:, c0:c1], in_=ps[3][0:5, 0:c1 - c0]), r=["ps3"], w=["zb1"])
    P.barrier(lambda e: e.memset(dummy[:], 1.0))

    utok = sb("utok", [128, 512])
    xbuf = [sb(f"xbuf{i}", [128, D]) for i in range(2)]
    csb = [sb(f"csb{i}", [128, 64]) for i in range(2)]
    junk = sb("junk", [128, D], BF)
    ss = sb("ss", [128, 1])
    rr = sb("rr", [128, 1])
    hbf = sb("hbf", [128, D], BF)
    hT = sb("hT", [128, 8, 128], BF)
    rt = [sb(f"rt{i}", [128, 8, 32]) for i in range(4)]

    esA = ExitStack()

    def sbA(name, shape, dt=F32):
        return esA.enter_context(nc.sbuf_tensor("sA_" + name, list(shape), dt))
    w_glu_bf = sbA("w_glu_bf", [128, 8, 1024], BF)
    for kc in range(8):
        ld(w_glu_bf[:, kc, :], w_in_v[:, kc, 0:1024], "w_glu_bf", "w1", eng="pool")
    eall = sbA("eall", [128, 64 * 128], BF)
    ld(eall[:], eall_d[:, :], "eall", "c2", eng="pool")
    NA = OFF_MG - OFF_Q
    w_in_bf = sbA("w_in_bf", [128, 8, NA], BF)
    for kc in range(8):
        ld(w_in_bf[:, kc, :], w_in_v[:, kc, OFF_Q:OFF_MG], "w_in_bf", "w0", eng="pool")
    zkv = sbA("zkv", [128, 768])
    sqk = sbA("sqk", [128, 3, 128])
    ssk = sbA("ssk", [128, 6])
    rk = sbA("rk", [128, 6])
    kn = sbA("kn", [128, 3, 128])
    kvout = [sbA(f"kvout{i}", [128, 768]) for i in range(2)]
    kbf = sbA("kbf", [128, 2, 128], BF)
    cmod = sbA("cmod", [128, 2, 128], BF)
    KTsel = sbA("KTsel", [128, 64 * 128], BF)
    KTwin = sbA("KTwin", [128, 6 * 128], BF)
    Vsel = sbA("Vsel", [128, 64, 2, 65], BF)
    Vwin = sbA("Vwin", [128, 6, 2, 65], BF)
    summT = sbA("summT", [128, 2, 128], BF)
    kcT = sbA("kcT", [128, 128], BF)
    Vcmp = sbA("Vcmp", [128, 2, 65], BF)
    zeros_bf = sbA("zeros_bf", [128, 260], BF)
    P.add("pool", lambda e: e.memset(zeros_bf[:], 0.0), w=["zeros_bf"])
    zq = sbA("zq", [128, 512])
    sqq = sbA("sqq", [128, 512])
    ssq = sbA("ssq", [128, 8])
    rq = sbA("rq", [128, 8])
    gate24 = sbA("gate24", [128, 24])
    qpb = sbA("qpb", [128, 4, 2, 64], BF)
    qrb = sbA("qrb", [128, 4, 2, 64], BF)
    QT = sbA("QT", [128, 8, 128], BF)
    mcmp_t = sbA("mcmp_t", [128, 128])
    madd_t = sbA("madd_t", [128, 128])
    mcT_t = sbA("mcT_t", [128, 128], BF)
    sbias = sbA("sbias", [128, 4, 128])
    ecmp = sbA("ecmp", [128, 4, 128])
    bnc_w = sbias
    bnc_c = ecmp[0:32].rearrange("p a b -> p (a b)")
    l4 = sbA("l4", [128, 4])
    rl4 = sbA("rl4", [128, 4])
    score = sbA("score", [128, 128])
    sc2 = sbA("sc2", [128, 128])
    mx8 = sbA("mx8", [128, 16])
    selb = sbA("selb", [128, 128], BF)
    selbT = sbA("selbT", [128, 128], BF)
    PT = [sbA(f"PT{i}", [128, 512], BF) for i in range(2)]
    Obr = sbA("Obr", [128, 3, 4, 65])
    l3 = sbA("l3", [128, 3, 4])
    coef = sbA("coef", [128, 3, 4])
    otmp = sbA("otmp", [128, 4, 64])
    otok = sbA("otok", [128, 8, 64])
    obf = sbA("obf", [128, 512], BF)

    P.add("pool", lambda e: e.memset(Vsel[:].rearrange("p a b c -> p (a b c)"), 1.0), w=["Vsel"])
    P.add("pool", lambda e: e.memset(Vwin[:].rearrange("p a b c -> p (a b c)"), 1.0), w=["Vwin"])
    P.add("pool", lambda e: e.memset(Vcmp[:].rearrange("p a b -> p (a b)"), 1.0), w=["Vcmp"])
    P.add("pool", lambda e: e.memset(summT[:].rearrange("p a b -> p (a b)"), 0.0), w=["summT"])

    state = {"xi": 0}

    def rope_ops(src4, dst1, dst2, cs, nh, rkeys, wkey, N=128):
        x1, x2 = src4
        shp = list(x1.shape)
        cos = cs[0:N, 0:32]
        sin = cs[0:N, 32:64]
        for _ in range(len(shp) - 2):
            cos = cos.unsqueeze(1)
            sin = sin.unsqueeze(1)
        cos = cos.to_broadcast(shp)
        sin = sin.to_broadcast(shp)
        t = [rt[j][0:N, 0:nh, :] if len(shp) == 3 else rt[j][0:N, 0:nh, :].rearrange("p (a b) d -> p a b d", a=shp[1]) for j in range(4)]

        P.add("dve", lambda e: e.tensor_tensor(out=t[0], in0=x1, in1=cos, op=ALU.mult), r=rkeys, w=["rt0"])
        P.add("dve", lambda e: e.tensor_tensor(out=t[1], in0=x2, in1=sin, op=ALU.mult), r=rkeys, w=["rt1"])
        P.add("dve", lambda e: e.tensor_tensor(out=t[2], in0=x2, in1=cos, op=ALU.mult), r=rkeys, w=["rt2"])
        P.add("dve", lambda e: e.tensor_tensor(out=t[3], in0=x1, in1=sin, op=ALU.mult), r=rkeys, w=["rt3"])
        P.add("dve", lambda e: e.tensor_tensor(out=dst1, in0=t[0], in1=t[1], op=ALU.subtract), r=["rt0", "rt1"], w=[wkey])
        P.add("dve", lambda e: e.tensor_tensor(out=dst2, in0=t[2], in1=t[3], op=ALU.add), r=["rt2", "rt3"], w=[wkey])

    def front_norm(xsrc, cs_src, N, At, Ak, src_key=None):
        ib = state["xi"] % 2
        state["xi"] += 1
        xb = xbuf[ib]
        xk = f"xbuf{ib}"
        P.add("sp", lambda e: e.dma_start(out=xb[0:N, :], in_=xsrc), r=([src_key] if src_key else []), w=[xk], sem=xk)
        ld(csb[ib][0:N, :], cs_src, f"csb{ib}", f"csb{ib}")
        P.add("act", lambda e: e.activation(out=junk[0:N, :], in_=xb[0:N, :], func=AF.Square, accum_out=ss[0:N, :]), r=[xk], w=["junk", "ss"])
        rsqrt_pool(rr[0:N, :], ss[0:N, :], 1.0 / D, 1, "ss", "rr")
        P.add("dve", lambda e: e.scalar_tensor_tensor(out=hbf[0:N, :], in0=xb[0:N, :], scalar=rr[0:N, 0:1], in1=At[0:N, :], op0=ALU.mult, op1=ALU.mult),
              r=[xk, "rr", Ak], w=["hbf"])

        def tr(e):
            for kc in range(8):
                ins = e.transpose(out=psb(0)[:, kc * 128:kc * 128 + N], in_=hbf[0:N, kc * 128:(kc + 1) * 128], identity=identb[0:N, 0:N])
            return ins
        P.add("pe", tr, r=["hbf", "identb"], w=["ps0"])
        P.add("act", lambda e: e.copy(out=hT[:, :, 0:N], in_=psb(0)[:, :].rearrange("p (a b) -> p a b", a=8)[:, :, 0:N]), r=["ps0"], w=["hT"])
        return ib

    def proj_tok(bank, c0, n, N, selb, selk, wt=None, wk="w_in_bf", woff=OFF_Q):
        wt = w_in_bf if wt is None else wt

        def mm(e):
            for kc in range(8):
                e.matmul(ps[bank][0:N, 0:n], lhsT=hT[:, kc, 0:N], rhs=wt[:, kc, c0 - woff:c0 - woff + n], start=(kc == 0), stop=False)
            return e.matmul(ps[bank][0:N, 0:n], lhsT=selb[:, 0:N], rhs=zb1[:, c0:c0 + n], start=False, stop=True)
        P.add("pe", mm, r=["hT", wk, "zb1", selk], w=[f"ps{bank}"])

    def f1(xsrc, cs_src, N, At, Ak, selb, selk, slot, wslot, kv_dst, kvi, glu_dst=None):
        ib = front_norm(xsrc, cs_src, N, At, Ak)
        cs = csb[ib]
        csk = f"csb{ib}"
        proj_tok(1, OFF_KV, 512, N, selb, selk)
        proj_tok(2, OFF_KV + 512, 256, N, selb, selk)
        P.add("act", lambda e: e.copy(out=zkv[0:N, 0:512], in_=ps[1][0:N, :]), r=["ps1"], w=["zkv"])
        P.add("act", lambda e: e.copy(out=zkv[0:N, 512:768], in_=ps[2][0:N, 0:256]), r=["ps2"], w=["zkv"])
        zk = zkv[0:N, :].rearrange("p (k v c) -> p k v c", k=3, v=2, c=128)
        P.add("dve", lambda e: e.tensor_tensor(out=sqk[0:N], in0=zk[:, :, 0, :], in1=zk[:, :, 0, :], op=ALU.mult), r=["zkv"], w=["sqk"])
        P.add("dve", lambda e: e.tensor_reduce(out=ssk[0:N], in_=sqk[0:N].rearrange("p k (h d) -> p (k h) d", h=2), axis=AX.X, op=ALU.add),
              r=["sqk"], w=["ssk"])
        rsqrt_pool(rk[0:N], ssk[0:N], 1.0 / 64, 6, "ssk", "rk")
        gkv = gk_bc[0:N].rearrange("p (k d) -> p k d", k=3).unsqueeze(2).to_broadcast([N, 3, 2, 64])
        kn4 = kn[0:N].rearrange("p k (h d) -> p k h d", h=2)
        P.add("dve", lambda e: e.tensor_tensor(out=kn4, in0=zk[:, :, 0, :].rearrange("p k (h d) -> p k h d", h=2), in1=gkv, op=ALU.mult),
              r=["zkv", "gk_bc"], w=["kn"])
        kn6 = kn[0:N].rearrange("p k (h d) -> p (k h) d", h=2)
        P.add("dve", lambda e: e.tensor_tensor(out=kn6, in0=kn6, in1=rk[0:N].unsqueeze(2).to_broadcast([N, 6, 64]), op=ALU.mult),
              r=["kn", "rk"], w=["kn"])
        kvo = kvout[kvi % 2]
        kvk = f"kvout{kvi % 2}"
        kvo4 = kvo[0:N].rearrange("p (k v c) -> p k v c", k=3, v=2, c=128)
        P.add("pool", lambda e: e.tensor_copy(out=kvo4[:, 0, 0, :], in_=kn[0:N, 0, :]), r=["kn"], w=[kvk])
        P.add("pool", lambda e: e.tensor_copy(out=kvo4[:, :, 1, :], in_=zk[:, :, 1, :]), r=["zkv"], w=[kvk])
        src = kn[0:N, 1:3, :].rearrange("p k (h f d) -> p k h f d", h=2, f=2, d=32)
        dstv = kvo4[:, 1:3, 0, :].rearrange("p k (h f d) -> p k h f d", h=2, f=2, d=32)
        rope_ops((src[:, :, :, 0, :], src[:, :, :, 1, :]), dstv[:, :, :, 0, :], dstv[:, :, :, 1, :], cs, 4, ["kn", csk], kvk, N)
        if slot is not None:
            P.add("pool", lambda e: e.tensor_copy(out=kbf[:], in_=kvo4[:, 1:3, 0, :]), r=[kvk], w=["kbf"])
            P.add("pool", lambda e: e.tensor_copy(out=Vsel[:, slot, :, 0:64], in_=zk[:, 1, 1, :].rearrange("p (h d) -> p h d", h=2)),
                  r=["zkv"], w=["Vsel"])
            P.add("pool", lambda e: e.tensor_copy(out=Vwin[:, wslot, :, 0:64], in_=zk[:, 2, 1, :].rearrange("p (h d) -> p h d", h=2)),
                  r=["zkv"], w=["Vwin"])
            pm1k = pm1[:, 0, :].unsqueeze(1).to_broadcast([128, 2, 64])
            pm1v = pm1[:, 1, :].unsqueeze(1).to_broadcast([128, 2, 64])
            P.add("pool", lambda e: e.tensor_tensor(out=cmod[:, 0, :].rearrange("p (h d) -> p h d", h=2), in0=kn[:, 0, :].rearrange("p (h d) -> p h d", h=2),
                                                   in1=pm1k, op=ALU.mult), r=["kn", "pm1"], w=["cmod"])
            P.add("pool", lambda e: e.tensor_tensor(out=cmod[:, 1, :].rearrange("p (h d) -> p h d", h=2), in0=zk[:, 0, 1, :].rearrange("p (h d) -> p h d", h=2),
                                                   in1=pm1v, op=ALU.mult), r=["zkv", "pm1"], w=["cmod"])

            def tr(e):
                e.transpose(out=psb(3)[:, 0:128], in_=kbf[:, 0, :], identity=identb[:])
                return e.transpose(out=psb(3)[:, 128:256], in_=kbf[:, 1, :], identity=identb[:])
            P.add("pe", tr, r=["kbf", "identb"], w=["ps3"])
            P.add("act", lambda e: e.copy(out=KTsel[:, slot * 128:(slot + 1) * 128], in_=psb(3)[:, 0:128]), r=["ps3"], w=["KTsel"])
            P.add("act", lambda e: e.copy(out=KTwin[:, wslot * 128:(wslot + 1) * 128], in_=psb(3)[:, 128:256]), r=["ps3"], w=["KTwin"])

            def mmc(e):
                e.matmul(ps[4][:, 0:2], lhsT=cmod[:, 0, :], rhs=sel2[:, :], start=True, stop=True)
                return e.matmul(ps[4][:, 2:4], lhsT=cmod[:, 1, :], rhs=sel2[:, :], start=True, stop=True)
            P.add("pe", mmc, r=["cmod", "sel2"], w=["ps4"])
            P.add("act", lambda e: e.copy(out=summT[:, :, 2 * slot:2 * slot + 2], in_=ps[4][:, 0:4].rearrange("p (a b) -> p a b", a=2)),
                  r=["ps4"], w=["summT"])
        if kv_dst is not None:
            P.add("sp", lambda e: e.dma_start(out=kv_dst, in_=kvo[0:N, :]), r=[kvk], w=[], sem=f"st_{kvk}")
        if glu_dst is not None:
            proj_tok(5, 0, 512, N, selb, selk, wt=w_glu_bf, wk="w_glu_bf", woff=0)
            proj_tok(6, 512, 512, N, selb, selk, wt=w_glu_bf, wk="w_glu_bf", woff=0)
            P.add("act", lambda e: e.activation(out=utok[0:N, :], in_=ps[6][0:N, :], func=AF.Tanh, scale=0.5), r=["ps6"], w=["utok"])
            P.add("dve", lambda e: e.tensor_scalar(out=utok[0:N, :], in0=utok[0:N, :], scalar1=0.5, scalar2=0.5, op0=ALU.mult, op1=ALU.add),
                  r=["utok"], w=["utok"])
            P.add("dve", lambda e: e.tensor_tensor(out=utok[0:N, :], in0=utok[0:N, :], in1=ps[5][0:N, :], op=ALU.mult), r=["utok", "ps5"], w=["utok"])
            for (dap, r0, r1) in glu_dst:
                P.add("sp", lambda e, dap=dap, r0=r0, r1=r1: e.dma_start(out=dap, in_=utok[r0:r1, :]), r=["utok"], w=[], sem="st_utok")
        return ib

    def update_cmp():
        def mm(e):
            e.matmul(ps[4][:, 0:128], lhsT=wkbd[:], rhs=summT[:, 0, :], start=True, stop=True)
            return e.matmul(ps[4][:, 128:256], lhsT=summT[:, 1, :], rhs=wvbd[:], start=True, stop=True)
        P.add("pe", mm, r=["summT", "wkbd", "wvbd"], w=["ps4"])
        P.add("act", lambda e: e.copy(out=kcT[:], in_=ps[4][:, 0:128]), r=["ps4"], w=["kcT"])
        P.add("act", lambda e: e.copy(out=Vcmp[:, :, 0:64], in_=ps[4][:, 128:256].rearrange("p (h d) -> p h d", h=2)), r=["ps4"], w=["Vcmp"])

    def qfront(N, selb_, selk, cs, csk):
        proj_tok(1, OFF_Q, 512, N, selb_, selk)
        proj_tok(2, OFF_NG, 24, N, selb_, selk)
        P.add("act", lambda e: e.copy(out=zq[0:N, :], in_=ps[1][0:N, :]), r=["ps1"], w=["zq"])
        P.add("act", lambda e: e.activation(out=gate24[0:N, :], in_=ps[2][0:N, 0:24], func=AF.Tanh, scale=0.5), r=["ps2"], w=["gate24"])
        P.add("dve", lambda e: e.tensor_scalar(out=gate24[0:N, :], in0=gate24[0:N, :], scalar1=0.5, scalar2=0.5, op0=ALU.mult, op1=ALU.add),
              r=["gate24"], w=["gate24"])
        P.add("dve", lambda e: e.tensor_tensor(out=sqq[0:N, :], in0=zq[0:N, :], in1=zq[0:N, :], op=ALU.mult), r=["zq"], w=["sqq"])
        P.add("dve", lambda e: e.tensor_reduce(out=ssq[0:N, :], in_=sqq[0:N, :].rearrange("p (h d) -> p h d", h=8), axis=AX.X, op=ALU.add),
              r=["sqq"], w=["ssq"])
        rsqrt_pool(rq[0:N, :], ssq[0:N, :], 1.0 / 64, 8, "ssq", "rq")
        zq3 = zq[0:N, :].rearrange("p (h d) -> p h d", h=8)
        P.add("dve", lambda e: e.tensor_tensor(out=zq3, in0=zq3, in1=gq_bc[0:N, :].unsqueeze(1).to_broadcast([N, 8, 64]), op=ALU.mult),
              r=["zq", "gq_bc"], w=["zq"])
        P.add("dve", lambda e: e.tensor_tensor(out=zq3, in0=zq3, in1=rq[0:N, :].unsqueeze(2).to_broadcast([N, 8, 64]), op=ALU.mult),
              r=["zq", "rq"], w=["zq"])
        P.add("pool", lambda e: e.tensor_copy(out=qpb[0:N].rearrange("p g k d -> p k g d"), in_=zq[0:N, :].rearrange("p (k g d) -> p k g d", k=2, g=4)),
              r=["zq"], w=["qpb"])
        src = zq[0:N, :].rearrange("p (k g f d) -> p k g f d", k=2, g=4, f=2)
        dstv = qrb[0:N].rearrange("p g k (f d) -> p k g f d", f=2)
        rope_ops((src[:, :, :, 0, :], src[:, :, :, 1, :]), dstv[:, :, :, 0, :], dstv[:, :, :, 1, :], cs, 8, ["zq", csk], "qrb", N)

        def tr(e):
            for v, qx in enumerate((qpb, qrb)):
                for g in range(4):
                    ins = e.transpose(out=psb(3)[:, (v * 4 + g) * 128:(v * 4 + g) * 128 + N],
                                      in_=qx[0:N, g, :, :].rearrange("p k d -> p (k d)"), identity=identb[0:N, 0:N])
            return ins
        P.add("pe", tr, r=["qpb", "qrb", "identb"], w=["ps3"])
        P.add("act", lambda e: e.copy(out=QT[:, :, 0:N], in_=psb(3)[:, :].rearrange("p (a b) -> p a b", a=8)[:, :, 0:N]), r=["ps3"], w=["QT"])

    st_att = {"s": 0, "first7": True}

    def s_tile(N, h, var, KT_ap, kkeys, bias_list):
        hp = slice(64 * h, 64 * h + 64)
        b = 5 + (st_att["s"] % 2)
        pt = PT[st_att["s"] % 2]
        ptk = f"PT{st_att['s'] % 2}"
        st_att["s"] += 1

        def mm(e):
            last = len(bias_list) == 0
            ins = e.matmul(ps[b][:, 0:4 * N], lhsT=KT_ap, rhs=QT[hp, var * 4:var * 4 + 4, 0:N], start=True, stop=last)
            for bi, (l_ap, r_ap) in enumerate(bias_list):
                ins = e.matmul(ps[b][:, 0:4 * N], lhsT=l_ap, rhs=r_ap.unsqueeze(1).to_broadcast([128, 4, N]), start=False,
                               stop=(bi == len(bias_list) - 1))
            return ins
        P.add("pe", mm, r=["QT"] + list(kkeys), w=[f"ps{b}"])
        P.add("act", lambda e: e.activation(out=pt[:, 0:4 * N], in_=ps[b][:, 0:4 * N], func=AF.Exp, scale=SCALE), r=[f"ps{b}"], w=[ptk])
        return pt, ptk

    def pv(N, pt, ptk, V_ap, vkeys, first):
        def mm(e):
            if first:
                e.matmul(ps[7][0:N, 0:260], lhsT=zeros_bf[:, 0:N], rhs=zeros_bf[:, 0:260], start=True, stop=False, skip_group_check=True)
            for g in range(4):
                ins = e.matmul(ps[7][0:N, g * 65:(g + 1) * 65], lhsT=pt[:, g * N:(g + 1) * N], rhs=V_ap,
                               start=False, stop=False, skip_group_check=True)
            return ins
        P.add("pe", mm, r=[ptk, "zeros_bf"] + list(vkeys), w=["ps7"])

    def attention(i, N=128):
        ld(mcmp_t[:], m_cmp[i, :, :], "mcmp_t", "mcmp_t")
        ld(madd_t[:], m_add[i, :, :], "madd_t", "madd_t")
        ld(mcT_t[:], m_cmpT[i, :, :], "mcT_t", "mcT_t", eng="pool")
        for h in range(2):
            hp = slice(64 * h, 64 * h + 64)
            def mmA(e, hp=hp):
                for g in range(4):
                    ins = e.matmul(ps[4][0:N, g * 128:(g + 1) * 128], lhsT=QT[hp, g, 0:N], rhs=kcT[hp, :], start=True, stop=True)
                return ins
            P.add("pe", mmA, r=["QT", "kcT"], w=["ps4"])
            P.add("dve", lambda e: e.tensor_tensor(out=sbias[0:N], in0=ps[4][0:N, :].rearrange("p (g b) -> p g b", g=4),
                                                  in1=mcmp_t[0:N, :].unsqueeze(1).to_broadcast([N, 4, 128]), op=ALU.add),
                  r=["ps4", "mcmp_t"], w=["sbias"])

            def expA(e):
                for g in range(4):
                    ins = e.activation(out=ecmp[0:N, g, :], in_=sbias[0:N, g, :], func=AF.Exp, scale=SCALE, accum_out=l4[0:N, g:g + 1])
                return ins
            P.add("act", expA, r=["sbias"], w=["ecmp", "l4"])

            P.add("dve", lambda e: e.tensor_scalar(out=rl4[0:N, :], in0=l4[0:N, :], scalar1=1e-30, scalar2=None, op0=ALU.max), r=["l4"], w=["rl4"])
            P.add("dve", lambda e: e.reciprocal(out=rl4[0:N, :], in_=rl4[0:N, :]), r=["rl4"], w=["rl4"])
            P.add("dve", lambda e: e.scalar_tensor_tensor(out=score[0:N, :], in0=ecmp[0:N, 0, :], scalar=rl4[0:N, 0:1], in1=madd_t[0:N, :], op0=ALU.mult, op1=ALU.add),
                  r=["ecmp", "rl4", "madd_t"], w=["score"])
            for g in range(1, 4):
                P.add("dve", lambda e, g=g: e.scalar_tensor_tensor(out=score[0:N, :], in0=ecmp[0:N, g, :], scalar=rl4[0:N, g:g + 1], in1=score[0:N, :], op0=ALU.mult, op1=ALU.add),
                      r=["ecmp", "rl4", "score"], w=["score"])
            P.add("dve", lambda e: e.max(out=mx8[0:N, 0:8], in_=score[0:N, :]), r=["score"], w=["mx8a"])
            P.add("dve", lambda e: e.match_replace(out=sc2[0:N, :], in_to_replace=mx8[0:N, 0:8], in_values=score[0:N, :], imm_value=-1e9),
                  r=["score", "mx8a"], w=["sc2"])
            P.add("dve", lambda e: e.max(out=mx8[0:N, 8:16], in_=sc2[0:N, :]), r=["sc2"], w=["mx8b"])
            P.add("dve", lambda e: e.tensor_scalar(out=selb[0:N, :], in0=score[0:N, :], scalar1=mx8[0:N, 15:16], scalar2=-BIG, op0=ALU.is_lt, op1=ALU.mult),
                  r=["score", "mx8b"], w=["selb"])
            P.add("pe", lambda e: e.transpose(out=psb(3)[:, 0:N], in_=selb[0:N, :], identity=identb[0:N, 0:N]), r=["selb", "identb"], w=["ps3"])
            P.add("act", lambda e: e.copy(out=selbT[:, 0:N], in_=psb(3)[:, 0:N]), r=["ps3"], w=["selbT"])

            pt, ptk = s_tile(N, h, 0, kcT[hp, :], ["kcT"], [(identb[:, :], mcT_t[:, 0:N])] )
            pv(N, pt, ptk, Vcmp[:, h, :], ["Vcmp"], True)
            P.add("act", lambda e: e.copy(out=Obr[0:N, 0].rearrange("p g c -> p (g c)"), in_=ps[7][0:N, 0:260]), r=["ps7"], w=["Obr"])
            tiles = []
            for d in (2, 1, 0):
                if i - d >= 0:
                    w = (i - d) % 3
                    mk = {2: 1, 1: None, 0: 0}[d]
                    tiles.append((w, mk))
            for d in (2, 1, 0):
                if i - d >= 0:
                    w = 3 + (i - d) % 3
                    mk = {2: 3, 1: None, 0: 2}[d]
                    tiles.append((w, mk))
            for ti, (w, mk) in enumerate(tiles):
                bl = [] if mk is None else [(identb[:, :], masks[:, mk, 0:N])]
                pt, ptk = s_tile(N, h, 1, KTwin[hp, w * 128:(w + 1) * 128], ["KTwin", "masks"], bl)
                pv(N, pt, ptk, Vwin[:, w, h, :], ["Vwin"], ti == 0)
            P.add("act", lambda e: e.copy(out=Obr[0:N, 2].rearrange("p g c -> p (g c)"), in_=ps[7][0:N, 0:260]), r=["ps7"], w=["Obr"])
            tiles = [(j, 0 if j == i else None) for j in range(i + 1)] + [(32 + j, 2 if j == i else None) for j in range(i + 1)]
            for ti, (sl, mk) in enumerate(tiles):
                bl = [(eall[:, sl * 128:(sl + 1) * 128], selbT[:, 0:N])]
                if mk is not None:
                    bl.append((identb[:, :], masks[:, mk, 0:N]))
                pt, ptk = s_tile(N, h, 1, KTsel[hp, sl * 128:(sl + 1) * 128], ["KTsel", "eall", "selbT", "masks"], bl)
                pv(N, pt, ptk, Vsel[:, sl, h, :], ["Vsel"], ti == 0)
            P.add("act", lambda e: e.copy(out=Obr[0:N, 1].rearrange("p g c -> p (g c)"), in_=ps[7][0:N, 0:260]), r=["ps7"], w=["Obr"])
            combine(N, h)
        P.add("pool", lambda e: e.tensor_copy(out=obf[0:N, :], in_=otok[0:N].rearrange("p h d -> p (h d)")), r=["otok"], w=["obf"])

    def combine(N, h):
        g3 = gate24[0:N, :].rearrange("p (k g b) -> p k b g", k=2, g=4, b=3)[:, h, :, :]

        P.add("dve", lambda e: e.tensor_scalar(out=l3[0:N], in0=Obr[0:N, :, :, 64], scalar1=1e-30, scalar2=None, op0=ALU.max), r=["Obr"], w=["l3"])
        P.add("dve", lambda e: e.reciprocal(out=l3[0:N], in_=l3[0:N]), r=["l3"], w=["l3"])
        P.add("dve", lambda e: e.tensor_tensor(out=coef[0:N], in0=l3[0:N], in1=g3, op=ALU.mult), r=["l3", "gate24"], w=["coef"])
        oh = otok[0:N, 4 * h:4 * h + 4, :]
        P.add("dve", lambda e: e.tensor_tensor(out=oh, in0=Obr[0:N, 0, :, 0:64], in1=coef[0:N, 0, :].unsqueeze(2).to_broadcast([N, 4, 64]), op=ALU.mult),
              r=["Obr", "coef"], w=["otok"])
        for br in (1, 2):
            P.add("dve", lambda e, br=br: e.tensor_tensor(out=otmp[0:N], in0=Obr[0:N, br, :, 0:64], in1=coef[0:N, br, :].unsqueeze(2).to_broadcast([N, 4, 64]), op=ALU.mult),
                  r=["Obr", "coef"], w=["otmp"])
            P.add("dve", lambda e: e.tensor_tensor(out=oh, in0=oh, in1=otmp[0:N], op=ALU.add), r=["otok", "otmp"], w=["otok"])

    ptb = sbA("ptb", [128, 512], I32)
    ptf = ptb[:].bitcast(F32)
    idx = ptb
    riota = sbA("riota", [128, 1], I32)
    riof = sbA("riof", [128, 1])
    pgk = [sbA(f"pgk{i}", [128, 128]) for i in range(2)]
    pgv = [sbA(f"pgv{i}", [128, 128]) for i in range(2)]
    pgkb = [sbA(f"pgkb{i}", [128, 128], BF) for i in range(2)]
    pgvb = [sbA(f"pgvb{i}", [128, 2, 65], BF) for i in range(2)]
    KTp = [sbA(f"KTp{i}", [128, 128], BF) for i in range(2)]
    summTs = sbA("summTs", [128, 2, 256], BF)
    kcTs = sbA("kcTs", [128, 256], BF)
    Vcs = sbA("Vcs", [128, 2, 2, 65], BF)
    madds = sbA("madds", [8, 256])
    mwin = sbA("mwin", [128, 8], BF)
    mnew = sbA("mnew", [32, 32], BF)
    ecs = xbuf[1][0:8, :].rearrange("p (a b) -> p a b", a=4)
    scs = sbA("scs", [8, 256])
    sc2s = sbA("sc2s", [8, 256])
    selbs = sbA("selbs", [8, 256], BF)
    knew = sbA("knew", [32, 2, 128], BF)
    KTnew = sbA("KTnew", [128, 2, 32], BF)
    Vnew = sbA("Vnew", [32, 2, 2, 65], BF)
    for i_ in range(2):
        P.add("pool", lambda e, i_=i_: e.memset(pgvb[i_][:].rearrange("p a b -> p (a b)"), 1.0), w=[f"pgvb{i_}"])
    P.add("pool", lambda e: e.memset(Vcs[:].rearrange("p a b c -> p (a b c)"), 1.0), w=["Vcs"])
    P.add("pool", lambda e: e.memset(Vnew[:].rearrange("p a b c -> p (a b c)"), 1.0), w=["Vnew"])
    ld(ptb[:], pt_d[0:1, :].to_broadcast([128, 512]), "ptb", "ptb")
    ld(madds[:], madds_d[0:1, :].to_broadcast([8, 256]), "madds", "madds")
    ld(mwin[:], mwin_d[:, :], "mwin", "mwin", eng="pool")
    ld(mnew[:], mnew_d[:, :], "mnew", "mnew", eng="pool")
    P.add("pool", lambda e: e.iota(out=riota[:], pattern=[[0, 1]], base=0, channel_multiplier=1), w=["riota"])
    P.add("pool", lambda e: e.tensor_copy(out=riof[:], in_=riota[:]), r=["riota"], w=["riof"])
    P.add("pool", lambda e: e.tensor_copy(out=ptf, in_=ptb[:]), r=["ptb"], w=["ptb"])
    P.add("pool", lambda e: e.tensor_scalar(out=ptf, in0=ptf, scalar1=128.0, scalar2=riof[:, 0:1], op0=ALU.mult, op1=ALU.add), r=["ptb", "riof"], w=["ptb"])
    P.add("pool", lambda e: e.tensor_copy(out=ptb[:], in_=ptf), r=["ptb"], w=["idx"])

    pg_state = {"n": 0}

    def gather(dst, dkey, cache, col):
        P.add("pool", lambda e: e.indirect_dma_start(out=dst[:, :], out_offset=None, in_=cache[:, :],
                                                    in_offset=bass.IndirectOffsetOnAxis(ap=idx[:, col:col + 1], axis=0)),
              r=["idx"], w=[dkey], sem=dkey)

    wkb_t = sbA("wkb_t", [128, 4, 128], BF)
    wvb_t = sbA("wvb_t", [128, 4, 2, 65], BF)
    wKT = sbA("wKT", [128, 4, 128], BF)
    ObrS1 = sbA("ObrS1", [8, 3, 4, 65])
    gate8 = sbA("gate8", [8, 4, 24])
    obf8 = sbA("obf8", [8, 512], BF)
    selbT2 = sbA("selbT2", [128, 2, 2, 8], BF)
    P.add("pool", lambda e: e.memset(wvb_t[:].rearrange("p a b c -> p (a b c)"), 1.0), w=["wvb_t"])

    def sample_attention():
        kvo4 = kvout[0][0:32].rearrange("p (k v c) -> p k v c", k=3, v=2, c=128)
        P.add("pool", lambda e: e.tensor_copy(out=knew[:], in_=kvo4[:, 1:3, 0, :]), r=["kvout0"], w=["knew"])
        P.add("pool", lambda e: e.tensor_copy(out=Vnew[:, :, :, 0:64], in_=kvo4[:, 1:3, 1, :].rearrange("p k (h d) -> p k h d", h=2)), r=["kvout0"], w=["Vnew"])

        def trn(e):
            e.transpose(out=psb(3)[:, 0:32], in_=knew[:, 0, :], identity=identb[0:32, 0:32])
            return e.transpose(out=psb(3)[:, 32:64], in_=knew[:, 1, :], identity=identb[0:32, 0:32])
        P.add("pe", trn, r=["knew", "identb"], w=["ps3"])
        P.add("act", lambda e: e.copy(out=KTnew[:].rearrange("p a b -> p (a b)"), in_=psb(3)[:, 0:64]), r=["ps3"], w=["KTnew"])
        for sq in range(4):
            P.add("sp", lambda e, sq=sq: e.dma_start(out=gate8[:, sq, :], in_=gate24[8 * sq:8 * sq + 8, :]), r=["gate24"], w=["gate8"], sem="gate8")
        for sq in range(4):
            cs0 = 8 * sq

            def s_tile_s(h, var, KT_ap, nkeys, kkeys, bias_list, cs0=cs0):
                hp = slice(64 * h, 64 * h + 64)
                b = 5 + (st_att["s"] % 2)
                pt = PT[st_att["s"] % 2]
                ptk = f"PT{st_att['s'] % 2}"
                st_att["s"] += 1

                def mm(e):
                    last = len(bias_list) == 0
                    ins = e.matmul(ps[b][0:nkeys, 0:32], lhsT=KT_ap, rhs=QT[hp, var * 4:var * 4 + 4, cs0:cs0 + 8], start=True, stop=last)
                    for bi, (l_ap, r_ap) in enumerate(bias_list):
                        ins = e.matmul(ps[b][0:nkeys, 0:32], lhsT=l_ap, rhs=r_ap.unsqueeze(1).to_broadcast([r_ap.shape[0], 4, 8]), start=False,
                                       stop=(bi == len(bias_list) - 1))
                    return ins
                P.add("pe", mm, r=["QT"] + list(kkeys), w=[f"ps{b}"])
                P.add("act", lambda e: e.activation(out=pt[0:nkeys, 0:32], in_=ps[b][0:nkeys, 0:32], func=AF.Exp, scale=SCALE), r=[f"ps{b}"], w=[ptk])
                return pt, ptk

            def pv_s(pt, ptk, nkeys, V_ap, vkeys, first, bank=7):
                def mm(e):
                    if first:
                        e.matmul(ps[bank][0:8, 0:260], lhsT=zeros_bf[:, 0:8], rhs=zeros_bf[:, 0:260], start=True, stop=False, skip_group_check=True)
                    for g in range(4):
                        ins = e.matmul(ps[bank][0:8, g * 65:(g + 1) * 65], lhsT=pt[0:nkeys, g * 8:(g + 1) * 8], rhs=V_ap,
                                       start=False, stop=False, skip_group_check=True)
                    return ins
                P.add("pe", mm, r=[ptk, "zeros_bf"] + list(vkeys), w=[f"ps{bank}"])

            pm1k = pm1[:, 0, :].unsqueeze(1).to_broadcast([128, 2, 64])
            pm1v = pm1[:, 1, :].unsqueeze(1).to_broadcast([128, 2, 64])
            for p in range(128):
                col = sq * 128 + p
                ib_ = pg_state["n"] % 2
                pg_state["n"] += 1
                gather(pgk[ib_], f"pgk{ib_}", ccmpk, col)
                gather(pgv[ib_], f"pgv{ib_}", ccmpv, col)
                P.add("dve", lambda e, ib_=ib_: e.tensor_tensor(out=cmod[:, 0, :].rearrange("p (h d) -> p h d", h=2), in0=pgk[ib_][:, :].rearrange("p (h d) -> p h d", h=2),
                                                             in1=pm1k, op=ALU.mult), r=[f"pgk{ib_}", "pm1"], w=["cmod"])
                P.add("dve", lambda e, ib_=ib_: e.tensor_tensor(out=cmod[:, 1, :].rearrange("p (h d) -> p h d", h=2), in0=pgv[ib_][:, :].rearrange("p (h d) -> p h d", h=2),
                                                             in1=pm1v, op=ALU.mult), r=[f"pgv{ib_}", "pm1"], w=["cmod"])

                def mmc(e):
                    e.matmul(ps[4][:, 0:2], lhsT=cmod[:, 0, :], rhs=sel2[:, :], start=True, stop=True)
                    return e.matmul(ps[4][:, 2:4], lhsT=cmod[:, 1, :], rhs=sel2[:, :], start=True, stop=True)
                P.add("pe", mmc, r=["cmod", "sel2"], w=["ps4"])
                P.add("act", lambda e, p=p: e.copy(out=summTs[:, :, 2 * p:2 * p + 2], in_=ps[4][:, 0:4].rearrange("p (a b) -> p a b", a=2)),
                      r=["ps4"], w=["summTs"])

            def mmk(e):
                e.matmul(ps[4][:, 0:256], lhsT=wkbd[:], rhs=summTs[:, 0, :], start=True, stop=True)
                e.matmul(ps[3][:, 0:128], lhsT=summTs[:, 1, 0:128], rhs=wvbd[:], start=True, stop=True)
                return e.matmul(ps[3][:, 128:256], lhsT=summTs[:, 1, 128:256], rhs=wvbd[:], start=True, stop=True)
            P.add("pe", mmk, r=["summTs", "wkbd", "wvbd"], w=["ps4", "ps3"])
            P.add("act", lambda e: e.copy(out=kcTs[:], in_=ps[4][:, 0:256]), r=["ps4"], w=["kcTs"])
            P.add("act", lambda e: e.copy(out=Vcs[:, :, :, 0:64], in_=ps[3][:, 0:256].rearrange("p (c h d) -> p c h d", c=2, h=2)), r=["ps3"], w=["Vcs"])
            ld(wkb_t[:], cwk_d[sq, :, :].rearrange("(w p) c -> p w c", p=128), "wkb_t", "wkb_t", eng="pool")
            for w_ in range(4):
                ld(wvb_t[:, w_, :, 0:64], cwv_d[sq, w_ * 128:(w_ + 1) * 128, :].rearrange("p (h d) -> p h d", h=2), "wvb_t", "wvb_t", eng="pool")

            def trw(e):
                for w in range(4):
                    ins = e.transpose(out=psb(3)[:, 128 * w:128 * w + 128], in_=wkb_t[:, w, :], identity=identb[:])
                return ins
            P.add("pe", trw, r=["wkb_t", "identb"], w=["ps3"])
            P.add("act", lambda e: e.copy(out=wKT[:].rearrange("p a b -> p (a b)"), in_=psb(3)[:, 0:512]), r=["ps3"], w=["wKT"])

            for h in range(2):
                hp = slice(64 * h, 64 * h + 64)
                def mmA(e, hp=hp, cs0=cs0):
                    for g in range(4):
                        ins = e.matmul(ps[4 - g // 2][0:8, (g % 2) * 256:(g % 2) * 256 + 256], lhsT=QT[hp, g, cs0:cs0 + 8], rhs=kcTs[hp, :], start=True, stop=True)
                    return ins
                P.add("pe", mmA, r=["QT", "kcTs"], w=["ps4", "ps3"])

                def expA(e):
                    for g in range(4):
                        ins = e.activation(out=ecs[:, g, :], in_=ps[4 - g // 2][0:8, (g % 2) * 256:(g % 2) * 256 + 256], func=AF.Exp, scale=SCALE,
                                           accum_out=l4[0:8, g:g + 1])
                    return ins
                P.add("act", expA, r=["ps4", "ps3"], w=["xbuf1", "l4"])
                P.add("dve", lambda e: e.tensor_scalar(out=rl4[0:8, :], in0=l4[0:8, :], scalar1=1e-30, scalar2=None, op0=ALU.max), r=["l4"], w=["rl4"])
                P.add("dve", lambda e: e.reciprocal(out=rl4[0:8, :], in_=rl4[0:8, :]), r=["rl4"], w=["rl4"])
                P.add("dve", lambda e: e.scalar_tensor_tensor(out=scs[:, :], in0=ecs[:, 0, :], scalar=rl4[0:8, 0:1], in1=madds[:, :], op0=ALU.mult, op1=ALU.add),
                      r=["xbuf1", "rl4", "madds"], w=["scs"])
                for g in range(1, 4):
                    P.add("dve", lambda e, g=g: e.scalar_tensor_tensor(out=scs[:, :], in0=ecs[:, g, :], scalar=rl4[0:8, g:g + 1], in1=scs[:, :], op0=ALU.mult, op1=ALU.add),
                          r=["xbuf1", "rl4", "scs"], w=["scs"])
                P.add("dve", lambda e: e.max(out=mx8[0:8, 0:8], in_=scs[:, :]), r=["scs"], w=["mx8a"])
                P.add("dve", lambda e: e.match_replace(out=sc2s[:, :], in_to_replace=mx8[0:8, 0:8], in_values=scs[:, :], imm_value=-1e9), r=["scs", "mx8a"], w=["sc2s"])
                P.add("dve", lambda e: e.max(out=mx8[0:8, 8:16], in_=sc2s[:, :]), r=["sc2s"], w=["mx8b"])
                P.add("dve", lambda e: e.tensor_scalar(out=selbs[:, :], in0=scs[:, :], scalar1=mx8[0:8, 14:15], scalar2=-BIG, op0=ALU.is_lt, op1=ALU.mult),
                      r=["scs", "mx8b"], w=["selbs"])

                def trsb(e):
                    e.transpose(out=psb(3)[:, 0:8], in_=selbs[:, 0:128], identity=identb[0:8, 0:8])
                    return e.transpose(out=psb(3)[:, 8:16], in_=selbs[:, 128:256], identity=identb[0:8, 0:8])
                P.add("pe", trsb, r=["selbs", "identb"], w=["ps3"])
                P.add("act", lambda e, h=h: e.copy(out=selbT2[:, h].rearrange("p a b -> p (a b)"), in_=psb(3)[:, 0:16]), r=["ps3"], w=["selbT2"])
                for c in range(2):
                    pt, ptk = s_tile_s(h, 0, kcTs[hp, c * 128:(c + 1) * 128], 128, ["kcTs"], [])
                    pv_s(pt, ptk, 128, Vcs[:, c, h, :], ["Vcs"], c == 0)
                P.add("act", lambda e, h=h: e.copy(out=(Obr[0:8] if h == 0 else ObrS1[:])[:, 0].rearrange("p g c -> p (g c)"), in_=ps[7][0:8, 0:260]), r=["ps7"], w=["ObrS", "Obr"])
                for w in range(4):
                    bl = [(identb[:, :], mwin[:, :])] if w == 0 else []
                    pt, ptk = s_tile_s(h, 1, wKT[hp, w, :], 128, ["wKT", "mwin"], bl)
                    pv_s(pt, ptk, 128, wvb_t[:, w, h, :], ["wvb_t"], w == 0)
                pt, ptk = s_tile_s(h, 1, KTnew[hp, 1, :], 32, ["KTnew", "mnew"], [(identb[0:32, 0:32], mnew[:, cs0:cs0 + 8])])
                pv_s(pt, ptk, 32, Vnew[:, 1, h, :], ["Vnew"], False)
                P.add("act", lambda e, h=h: e.copy(out=(Obr[0:8] if h == 0 else ObrS1[:])[:, 2].rearrange("p g c -> p (g c)"), in_=ps[7][0:8, 0:260]), r=["ps7"], w=["ObrS", "Obr"])
            for p in range(128):
                col = sq * 128 + p
                ib_ = pg_state["n"] % 2
                pg_state["n"] += 1
                gather(pgk[ib_], f"pgk{ib_}", cselk, col)
                gather(pgv[ib_], f"pgv{ib_}", cselv, col)
                P.add("dve", lambda e, ib_=ib_: e.tensor_copy(out=pgkb[ib_][:], in_=pgk[ib_][:]), r=[f"pgk{ib_}"], w=[f"pgkb{ib_}"])
                P.add("pe", lambda e, ib_=ib_: e.transpose(out=psb(3)[:, 256 * ib_:256 * ib_ + 128], in_=pgkb[ib_][:], identity=identb[:]),
                      r=[f"pgkb{ib_}", "identb"], w=["ps3"])
                P.add("act", lambda e, ib_=ib_: e.copy(out=KTp[ib_][:], in_=psb(3)[:, 256 * ib_:256 * ib_ + 128]), r=["ps3"], w=[f"KTp{ib_}"])
                P.add("dve", lambda e, ib_=ib_: e.tensor_copy(out=pgvb[ib_][:, :, 0:64], in_=pgv[ib_][:, :].rearrange("p (h d) -> p h d", h=2)),
                      r=[f"pgv{ib_}"], w=[f"pgvb{ib_}"])
                for h in range(2):
                    hp = slice(64 * h, 64 * h + 64)
                    bl = [(eall[:, (p % 64) * 128:(p % 64 + 1) * 128], selbT2[:, h, p // 64, :])]
                    pt, ptk = s_tile_s(h, 1, KTp[ib_][hp, :], 128, [f"KTp{ib_}", "eall", "selbT2"], bl)
                    pv_s(pt, ptk, 128, pgvb[ib_][:, h, :], [f"pgvb{ib_}"], p == 0, bank=(7 if h == 0 else 2))
            for h in range(2):
                hp = slice(64 * h, 64 * h + 64)
                pt, ptk = s_tile_s(h, 1, KTnew[hp, 0, :], 32, ["KTnew", "mnew"], [(identb[0:32, 0:32], mnew[:, cs0:cs0 + 8])])
                pv_s(pt, ptk, 32, Vnew[:, 0, h, :], ["Vnew"], False, bank=(7 if h == 0 else 2))
                bk = 7 if h == 0 else 2
                P.add("act", lambda e, h=h, bk=bk: e.copy(out=(Obr[0:8] if h == 0 else ObrS1[:])[:, 1].rearrange("p g c -> p (g c)"), in_=ps[bk][0:8, 0:260]), r=[f"ps{bk}"], w=["ObrS", "Obr"])
            for h in range(2):
                combine_s(sq, h)
            P.add("sp", lambda e, sq=sq: e.dma_start(out=o_d[NT, 8 * sq:8 * sq + 8, :], in_=obf8[:, :]), r=["obf8"], w=[f"o_d{NT}"], sem="st_obf8")

    def combine_s(sq, h):
        g3 = gate8[:, sq, :].rearrange("p (k g b) -> p k b g", k=2, g=4, b=3)[:, h, :, :]
        Ob = Obr[0:8] if h == 0 else ObrS1[:]
        P.add("dve", lambda e: e.tensor_scalar(out=l3[0:8], in0=Ob[:, :, :, 64], scalar1=1e-30, scalar2=None, op0=ALU.max), r=["ObrS", "Obr"], w=["l3"])
        P.add("dve", lambda e: e.reciprocal(out=l3[0:8], in_=l3[0:8]), r=["l3"], w=["l3"])
        P.add("dve", lambda e: e.tensor_tensor(out=coef[0:8], in0=l3[0:8], in1=g3, op=ALU.mult), r=["l3", "gate8"], w=["coef"])
        oh = otok[0:8, 4 * h:4 * h + 4, :]
        P.add("dve", lambda e: e.tensor_tensor(out=oh, in0=Ob[:, 0, :, 0:64], in1=coef[0:8, 0, :].unsqueeze(2).to_broadcast([8, 4, 64]), op=ALU.mult),
              r=["ObrS", "Obr", "coef"], w=["otok"])
        for br in (1, 2):
            P.add("dve", lambda e, br=br: e.tensor_tensor(out=otmp[0:8], in0=Ob[:, br, :, 0:64], in1=coef[0:8, br, :].unsqueeze(2).to_broadcast([8, 4, 64]), op=ALU.mult),
                  r=["ObrS", "Obr", "coef"], w=["otmp"])
            P.add("dve", lambda e: e.tensor_tensor(out=oh, in0=oh, in1=otmp[0:8], op=ALU.add), r=["otok", "otmp"], w=["otok"])
        if h == 1:
            P.add("pool", lambda e: e.tensor_copy(out=obf8[:, :], in_=otok[0:8].rearrange("p h d -> p (h d)")), r=["otok"], w=["obf8"])

    f1(xs_d[:, :], cs_s[:, :], 32, A1s, "A1s", selSb, "selSb", None, None, skv_o[:, :], 0,
       glu_dst=[(sconv_o[sq, 22:30, :], 8 * sq, 8 * sq + 8) for sq in range(4)])
    for sq in range(4):
        ld(bnc_c[0:22, :], sconv_d[sq, 8:30, :], "ecmp", "bnc_c")
        P.add("sp", lambda e, sq=sq: e.dma_start(out=sconv_o[sq, 0:22, :], in_=bnc_c[0:22, :]), r=["ecmp"], w=[], sem="st_bc")
        for kv in range(2):
            src_c = (cwk_d, cwv_d)[kv]
            ld(bnc_w[0:126, :, :], src_c[sq, 8:512, :].rearrange("(p a) c -> p a c", a=4), "sbias", "bnc_w")
            P.add("sp", lambda e, sq=sq, kv=kv: e.dma_start(out=swin_o[sq, kv, 0:504, :].rearrange("(p a) c -> p a c", a=4), in_=bnc_w[0:126, :, :]),
                  r=["sbias"], w=[], sem="st_bw")
            P.add("sp", lambda e, sq=sq, kv=kv: e.dma_start(out=swin_o[sq, kv, 504:512, :], in_=kvout[0][8 * sq:8 * sq + 8, 512 + 128 * kv:640 + 128 * kv]),
                  r=["kvout0"], w=[], sem="st_kvout0")

    if do_sample:
        qfront(32, selSb, "selSb", csb[(state["xi"] - 1) % 2], f"csb{(state['xi'] - 1) % 2}")
        sample_attention()

    for i in range(n_own):
        f1(xt[i, :, :], cs_t[i, :, :], 128, A1, "A1", selPb, "selPb", 32 + i, 3 + (i % 3), None, 1)
        last = (i == NT - 1)
        ib = f1(xo[i, :, :], cs_o[i, :, :], 128, A1, "A1", selPb, "selPb", i, i % 3, kv_o[i, :, :], i,
                glu_dst=[(conv_o[:, :], 98, 128)] if last else None)
        update_cmp()
        qfront(128, selPb, "selPb", csb[ib], f"csb{ib}")
        attention(i)
        P.add("sp", lambda e, i=i: e.dma_start(out=o_d[i, :, :], in_=obf[:, :]), r=["obf"], w=[f"o_d{i}"], sem="st_obf")

    esA.close()
    P.barrier(lambda e: e.memset(dummy[:], 2.0))

    esB = ExitStack()

    def sbB(name, shape, dt=F32):
        return esB.enter_context(nc.sbuf_tensor("sB_" + name, list(shape), dt))
    w_gluB = sbB("w_gluB", [128, 8, 1024], BF)
    for kc in range(8):
        ld(w_gluB[:, kc, :], w_in_v[:, kc, 0:1024], "w_gluB", "wb4", eng="pool")
    w_mg_bf = sbB("w_mg_bf", [128, 8, 2048], BF)
    w_pw2_bf = sbB("w_pw2_bf", [128, 4, D], BF)
    w_nso_bf = sbB("w_nso_bf", [128, 4, D], BF)
    w_out_bf = sbB("w_out_bf", [128, 8, D], BF)
    for kc in range(8):
        ld(w_mg_bf[:, kc, :], w_in_v[:, kc, OFF_MG:D_IN], "w_mg_bf", "wb0", eng="pool")
        ld(w_out_bf[:, kc, :], w_out.rearrange("(kc p) n -> p kc n", p=128)[:, kc, :], "w_out_bf", "wb3", eng="pool")
    for kc in range(4):
        ld(w_pw2_bf[:, kc, :], w_pw2.rearrange("(kc p) n -> p kc n", p=128)[:, kc, :], "w_pw2_bf", "wb1", eng="pool")
        ld(w_nso_bf[:, kc, :], w_nsa_o.rearrange("(kc p) n -> p kc n", p=128)[:, kc, :], "w_nso_bf", "wb2", eng="pool")
    diag = sbB("diag", [128, 4, CONV_K, 128], BF)
    ones_bf = sbB("ones_bf", [128, 128], BF)
    P.add("pool", lambda e: e.memset(ones_bf[:], 1.0), w=["ones_bf"])
    for c in range(4):
        for k in range(CONV_K):
            P.add("pool", lambda e, c=c, k=k: e.tensor_scalar(out=diag[:, c, k, :], in0=identb[:, :], scalar1=wdwT[:, c, k:k + 1], scalar2=1.0,
                                                            op0=ALU.mult, op1=ALU.mult), r=["identb", "wdwT"], w=["diag"])
    xhb = sbB("xhb", [32, D])
    ssh = sbB("ssh", [32, 1])
    rrh = sbB("rrh", [32, 1])
    hbfh = sbB("hbfh", [32, D], BF)
    hTh = sbB("hTh", [128, 8, 32], BF)
    sgh = sbB("sgh", [128, 128])
    uext = sbB("uext", [128, 4, 160], BF)
    sg = sbB("sg", [128, 512])
    ycv = sbB("ycv", [128, 4, 128])
    ybf = sbB("ybf", [128, 4, 128], BF)
    ysq = sbB("ysq", [128, 4, 128], BF)
    mu = sbB("mu", [128, 128])
    msq = sbB("msq", [128, 128])
    rstd = sbB("rstd", [128, 128])
    tn = sbB("tn", [128, 4, 128])
    zf = sbB("zf", [128, 4, 128])
    ysl = sbB("ysl", [128, 4, 128], BF)
    ob = sbB("ob", [128, 512], BF)
    oT = sbB("oT", [128, 4, 128], BF)
    gcn = sbB("gcn", [128, 8, 128])
    m1 = sbB("m1", [128, 8, 128])
    mT = sbB("mT", [128, 8, 128], BF)
    x1t = sbB("x1t", [128, D])
    sct = utok

    def featproj(bank0, nchunk, N, wt, wk, wcol0, zcol0, selb_, selk, rhsT, rkey, nk=8, bias=True):
        for b0 in range(0, nchunk, 4):
            bank = bank0 + b0 // 4

            def mm(e, b0=b0, bank=bank):
                for c in range(b0, min(b0 + 4, nchunk)):
                    out = ps[bank][:, (c % 4) * N:(c % 4 + 1) * N]
                    for kc in range(nk):
                        ins = e.matmul(out, lhsT=wt[:, kc, wcol0 + c * 128:wcol0 + (c + 1) * 128], rhs=rhsT[:, kc, 0:N], start=(kc == 0),
                                       stop=(kc == nk - 1 and not bias))
                    if bias:
                        ins = e.matmul(out, lhsT=zb1[:, zcol0 + c * 128:zcol0 + (c + 1) * 128], rhs=selb_[:, 0:N], start=False, stop=True)
                return ins
            P.add("pe", mm, r=[wk, rkey, "zb1", selk], w=[f"ps{bank}"])

    def sigmoid_from(dst_ap, src_ap, rk_, wk_):
        P.add("act", lambda e: e.activation(out=dst_ap, in_=src_ap, func=AF.Tanh, scale=0.5), r=[rk_], w=[wk_])
        P.add("dve", lambda e: e.tensor_scalar(out=dst_ap, in0=dst_ap, scalar1=0.5, scalar2=0.5, op0=ALU.mult, op1=ALU.add), r=[wk_], w=[wk_])

    def passB(idx, N, xsrc, At, Ak, Gt, Gk, selb_, selk, segs, halo_kind, o_src, x1_dst):
        ib = front_norm(xsrc, cs_o[0, 0:N, :], N, At, Ak)
        xb = xbuf[ib]
        xk = f"xbuf{ib}"
        nseg = len(segs)
        L = segs[0][1]
        uv = uext[:, :, 0:nseg * (30 + L)].rearrange("p c (s t) -> p c s t", s=nseg)
        if halo_kind == "x":
            ld(xhb[:], xh[idx, :, :], "xhb", "xhb")
            P.add("act", lambda e: e.activation(out=junk[0:32, :], in_=xhb[:], func=AF.Square, accum_out=ssh[:]), r=["xhb"], w=["junk", "ssh"])
            rsqrt_pool(rrh[:], ssh[:], 1.0 / D, 1, "ssh", "rrh")
            P.add("dve", lambda e: e.scalar_tensor_tensor(out=hbfh[:], in0=xhb[:], scalar=rrh[:, 0:1], in1=At[0:32, :], op0=ALU.mult, op1=ALU.mult),
                  r=["xhb", "rrh", Ak], w=["hbfh"])

            def trh(e):
                for kc in range(8):
                    ins = e.transpose(out=psb(4)[:, kc * 32:(kc + 1) * 32], in_=hbfh[:, kc * 128:(kc + 1) * 128], identity=identb[0:32, 0:32])
                return ins
            P.add("pe", trh, r=["hbfh", "identb"], w=["ps4"])
            P.add("act", lambda e: e.copy(out=hTh[:].rearrange("p a b -> p (a b)"), in_=psb(4)[:, 0:256]), r=["ps4"], w=["hTh"])
            featproj(3, 4, 32, w_gluB, "w_gluB", 0, 0, selb_, selk, hTh, "hTh")
            featproj(4, 4, 32, w_gluB, "w_gluB", 512, 512, selb_, selk, hTh, "hTh")
            sigmoid_from(sgh[:, :], ps[4][:, 0:128], "ps4", "sgh")
            P.add("dve", lambda e: e.tensor_tensor(out=sgh[:, :], in0=sgh[:, :], in1=ps[3][:, 0:128], op=ALU.mult), r=["sgh", "ps3"], w=["sgh"])
            P.add("dve", lambda e: e.tensor_scalar(out=uv[:, :, 0, 0:30], in0=sgh[:, :].rearrange("p (c t) -> p c t", c=4)[:, :, 2:32],
                                                  scalar1=hv_sb[:, idx:idx + 1], scalar2=None, op0=ALU.mult), r=["sgh", "hv"], w=["uext"])
        else:
            for sq in range(nseg):
                ld(sct[0:30, :], sconv_d[sq, :, :], "utok", "utok")

                def trs(e):
                    for c in range(4):
                        ins = e.transpose(out=ps[4][:, c * 32:c * 32 + 30], in_=sct[0:30, c * 128:(c + 1) * 128], identity=identf[0:30, 0:30])
                    return ins
                P.add("pe", trs, r=["utok", "identf"], w=["ps4"])
                P.add("act", lambda e, sq=sq: e.copy(out=uv[:, :, sq, 0:30], in_=ps[4][:, 0:128].rearrange("p (c t) -> p c t", c=4)[:, :, 0:30]),
                      r=["ps4"], w=["uext"])
        featproj(1, 4, N, w_gluB, "w_gluB", 0, 0, selb_, selk, hT, "hT")
        featproj(2, 4, N, w_gluB, "w_gluB", 512, 512, selb_, selk, hT, "hT")
        sigmoid_from(sg[:, 0:4 * N], ps[2][:, 0:4 * N], "ps2", "sg")
        for sq, (t0, Ls) in enumerate(segs):
            P.add("dve", lambda e, sq=sq, t0=t0, Ls=Ls: e.tensor_tensor(
                out=uv[:, :, sq, 30:30 + Ls], in0=sg[:, 0:4 * N].rearrange("p (c t) -> p c t", c=4)[:, :, t0:t0 + Ls],
                in1=ps[1][:, 0:4 * N].rearrange("p (c t) -> p c t", c=4)[:, :, t0:t0 + Ls], op=ALU.mult), r=["sg", "ps1"], w=["uext"])

        def conv(e):
            for c in range(4):
                for sq, (t0, Ls) in enumerate(segs):
                    for k in range(CONV_K):
                        ins = e.matmul(ps[3][:, c * N + t0:c * N + t0 + Ls], lhsT=diag[:, c, k, :], rhs=uv[:, c, sq, k:k + Ls],
                                       start=(k == 0), stop=(k == CONV_K - 1))
            return ins
        P.add("pe", conv, r=["diag", "uext"], w=["ps3"])
        for c in range(4):
            P.add("act", lambda e, c=c: e.activation(out=ycv[:, c, 0:N], in_=ps[3][:, c * N:(c + 1) * N], func=AF.Identity, bias=cvec[:, 0, c:c + 1]),
                  r=["ps3", "cvec"], w=["ycv"])
        P.add("pool", lambda e: e.tensor_copy(out=ybf[:, :, 0:N], in_=ycv[:, :, 0:N]), r=["ycv"], w=["ybf"])
        P.add("act", lambda e: e.activation(out=ysq[:, :, 0:N], in_=ycv[:, :, 0:N], func=AF.Square), r=["ycv"], w=["ysq"])

        def stats(e):
            for c in range(4):
                e.matmul(ps[4][:, 0:N], lhsT=ones_bf[:, :], rhs=ybf[:, c, 0:N], start=(c == 0), stop=(c == 3))
            for c in range(4):
                ins = e.matmul(ps[4][:, N:2 * N], lhsT=ones_bf[:, :], rhs=ysq[:, c, 0:N], start=(c == 0), stop=(c == 3))
            return ins
        P.add("pe", stats, r=["ones_bf", "ybf", "ysq"], w=["ps4"])
        P.add("act", lambda e: e.activation(out=mu[:, 0:N], in_=ps[4][:, 0:N], func=AF.Copy, scale=1.0 / 512), r=["ps4"], w=["mu"])
        P.add("act", lambda e: e.activation(out=msq[:, 0:N], in_=ps[4][:, N:2 * N], func=AF.Copy, scale=1.0 / 512), r=["ps4"], w=["msq"])
        P.add("dve", lambda e: e.tensor_tensor(out=rstd[:, 0:N], in0=mu[:, 0:N], in1=mu[:, 0:N], op=ALU.mult), r=["mu"], w=["rstd"])
        P.add("dve", lambda e: e.tensor_tensor(out=rstd[:, 0:N], in0=msq[:, 0:N], in1=rstd[:, 0:N], op=ALU.subtract), r=["msq", "rstd"], w=["rstd"])
        rsqrt_pool(rstd[:, 0:N], rstd[:, 0:N], 1.0, N, "rstd", "rstd")
        P.add("dve", lambda e: e.tensor_tensor(out=tn[:, :, 0:N], in0=ycv[:, :, 0:N], in1=mu[:, 0:N].unsqueeze(1).to_broadcast([128, 4, N]), op=ALU.subtract),
              r=["ycv", "mu"], w=["tn"])
        P.add("dve", lambda e: e.tensor_tensor(out=tn[:, :, 0:N], in0=tn[:, :, 0:N], in1=rstd[:, 0:N].unsqueeze(1).to_broadcast([128, 4, N]), op=ALU.mult),
              r=["tn", "rstd"], w=["tn"])
        for c in range(4):
            P.add("act", lambda e, c=c: e.activation(out=zf[:, c, 0:N], in_=tn[:, c, 0:N], func=AF.Identity, scale=cvec[:, 1, c:c + 1], bias=cvec[:, 2, c:c + 1]),
                  r=["tn", "cvec"], w=["zf"])
        sigmoid_from(tn[:, :, 0:N], zf[:, :, 0:N], "zf", "tn")
        P.add("dve", lambda e: e.tensor_tensor(out=ysl[:, :, 0:N], in0=tn[:, :, 0:N], in1=zf[:, :, 0:N], op=ALU.mult), r=["tn", "zf"], w=["ysl"])
        featproj(5, 8, N, w_pw2_bf, "w_pw2_bf", 0, 0, selb_, selk, ysl, "ysl", nk=4, bias=False)
        featproj(1, 8, N, w_mg_bf, "w_mg_bf", 0, OFF_MG, selb_, selk, hT, "hT")
        for hb in range(2):
            sigmoid_from(gcn[:, 4 * hb:4 * hb + 4, 0:N], ps[1 + hb][:, 0:4 * N].rearrange("p (c t) -> p c t", c=4), f"ps{1 + hb}", "gcn")
            P.add("dve", lambda e, hb=hb: e.tensor_tensor(out=m1[:, 4 * hb:4 * hb + 4, 0:N], in0=gcn[:, 4 * hb:4 * hb + 4, 0:N],
                                                       in1=ps[5 + hb][:, 0:4 * N].rearrange("p (c t) -> p c t", c=4), op=ALU.mult),
                  r=["gcn", f"ps{5 + hb}"], w=["m1"])
        P.add("sp", lambda e: e.dma_start(out=ob[0:N, :], in_=o_src), r=[f"o_d{idx}"], w=["ob"], sem="ob")

        def tro(e):
            for kc in range(4):
                ins = e.transpose(out=psb(3)[:, kc * 128:kc * 128 + N], in_=ob[0:N, kc * 128:(kc + 1) * 128], identity=identb[0:N, 0:N])
            return ins
        P.add("pe", tro, r=["ob", "identb"], w=["ps3"])
        P.add("act", lambda e: e.copy(out=oT[:, :, 0:N], in_=psb(3)[:, 0:512].rearrange("p (a b) -> p a b", a=4)[:, :, 0:N]), r=["ps3"], w=["oT"])
        featproj(5, 8, N, w_nso_bf, "w_nso_bf", 0, 0, selb_, selk, oT, "oT", nk=4, bias=False)
        featproj(1, 8, N, w_mg_bf, "w_mg_bf", 1024, OFF_MG + 1024, selb_, selk, hT, "hT")
        for hb in range(2):
            sigmoid_from(gcn[:, 4 * hb:4 * hb + 4, 0:N], ps[1 + hb][:, 0:4 * N].rearrange("p (c t) -> p c t", c=4), f"ps{1 + hb}", "gcn")
            P.add("dve", lambda e, hb=hb: e.tensor_tensor(out=gcn[:, 4 * hb:4 * hb + 4, 0:N], in0=gcn[:, 4 * hb:4 * hb + 4, 0:N],
                                                       in1=ps[5 + hb][:, 0:4 * N].rearrange("p (c t) -> p c t", c=4), op=ALU.mult),
                  r=["gcn", f"ps{5 + hb}"], w=["gcn"])
        P.add("dve", lambda e: e.tensor_tensor(out=mT[:, :, 0:N], in0=m1[:, :, 0:N], in1=gcn[:, :, 0:N], op=ALU.add), r=["m1", "gcn"], w=["mT"])
        for half in range(2):
            def mmo(e, half=half):
                for kc in range(8):
                    ins = e.matmul(ps[5 + half][0:N, :], lhsT=mT[:, kc, 0:N], rhs=w_out_bf[:, kc, half * 512:(half + 1) * 512], start=(kc == 0), stop=(kc == 7))
                return ins
            P.add("pe", mmo, r=["mT", "w_out_bf"], w=[f"ps{5 + half}"])
            P.add("dve", lambda e, half=half: e.tensor_tensor(out=x1t[0:N, half * 512:(half + 1) * 512], in0=ps[5 + half][0:N, :],
                                                           in1=Gt[0:N, half * 512:(half + 1) * 512], op=ALU.mult), r=[f"ps{5 + half}", Gk], w=["x1t"])
        P.add("dve", lambda e: e.tensor_tensor(out=x1t[0:N, :], in0=x1t[0:N, :], in1=xb[0:N, :], op=ALU.add), r=["x1t", xk], w=["x1t"])
        P.add("sp", lambda e: e.dma_start(out=x1_dst, in_=x1t[0:N, :]), r=["x1t"], w=[f"x1d{idx}"], sem="st_x1")

    for i in range(n_own):
        passB(i, 128, xo[i, :, :], A1, "A1", G1, "G1", selPb, "selPb", [(0, 128)], "x", o_d[i, :, :], x1_d[i, :, :])
    if do_sample:
        passB(NT, 32, xs_d[:, :], A1s, "A1s", G1s, "G1s", selSb, "selSb", [(8 * q_, 8) for q_ in range(4)], "s", o_d[NT, 0:32, :], x1_d[NT, 0:32, :])
    esB.close()
    P.barrier(lambda e: e.memset(dummy[:], 3.0))

    esC = ExitStack()

    def sbC(name, shape, dt=F32):
        return esC.enter_context(nc.sbuf_tensor("sC_" + name, list(shape), dt))
    w_g_bf = sbC("w_g_bf", [128, 8, D_FF], BF)
    w_u_bf = sbC("w_u_bf", [128, 8, D_FF], BF)
    w_d_bf = sbC("w_d_bf", [128, NFF, D], BF)
    for kc in range(8):
        for c0, c1 in ((0, 2048), (2048, D_FF)):
            ld(w_g_bf[:, kc, c0:c1], w_gate.rearrange("(kc p) n -> p kc n", p=128)[:, kc, c0:c1], "w_g_bf", "wc0", eng="pool")
            ld(w_u_bf[:, kc, c0:c1], w_up.rearrange("(kc p) n -> p kc n", p=128)[:, kc, c0:c1], "w_u_bf", "wc1", eng="pool")
    for fc in range(NFF):
        ld(w_d_bf[:, fc, :], w_down.rearrange("(fc p) n -> p fc n", p=128)[:, fc, :], "w_d_bf", "wc2", eng="pool")
    zbg = zb1
    zbu = sbC("zbu", [5, D_FF], BF)
    A2, G2, A2s, G2s = A1, G1, A1s, G1s
    sgc = utok
    actT = sbC("actT", [128, NFF, 128], BF)
    ybuf = sbC("ybuf", [128, D])
    for wt, wk, zb, zk in ((w_g_bf, "w_g_bf", zbg, "zb1"), (w_u_bf, "w_u_bf", zbu, "zbu")):
        for c0 in range(0, D_FF, 512):
            c1 = min(c0 + 512, D_FF)

            def mm(e, wt=wt, c0=c0, c1=c1):
                for kc in range(8):
                    ins = e.matmul(ps[3][0:5, 0:c1 - c0], lhsT=sh2T[:, kc, :], rhs=wt[:, kc, c0:c1], start=(kc == 0), stop=(kc == 7))
                return ins
            P.add("pe", mm, r=["sh2T", wk], w=["ps3"])
            P.add("act", lambda e, zb=zb, c0=c0, c1=c1: e.copy(out=zb[:, c0:c1], in_=ps[3][0:5, 0:c1 - c0]), r=["ps3"], w=[zk])
    for dst, key, src, skey, sl, slk, n in ((A2, "A1", a2row, "a2row", selP, "selP", 128), (G2, "G1", g2row, "g2row", selP, "selP", 128),
                                            (A2s, "A1s", a2row, "a2row", selS, "selS", 32), (G2s, "G1s", g2row, "g2row", selS, "selS", 32)):
        for half in range(2):
            P.add("pe", lambda e, src=src, half=half, sl=sl, n=n: e.matmul(ps[1][0:n, :], lhsT=sl[:, 0:n], rhs=src[:, half * 512:(half + 1) * 512], start=True, stop=True),
                  r=[slk, skey], w=["ps1"])
            P.add("act", lambda e, dst=dst, half=half, n=n: e.copy(out=dst[:, half * 512:(half + 1) * 512], in_=ps[1][0:n, :]), r=["ps1"], w=[key])

    def passC(idx, N, xsrc, At, Ak, Gt, Gk, selb_, selk, y_dst):
        ib = front_norm(xsrc, cs_o[0, 0:N, :], N, At, Ak, src_key=f"x1d{idx}")
        xb = xbuf[ib]
        xk = f"xbuf{ib}"
        for gi, f0 in enumerate(range(0, NFF, 4)):
            nf = min(4, NFF - f0)
            bg, bu = (1, 2) if gi % 2 == 0 else (3, 4)
            for bank, wt, wk, zb, zk in ((bg, w_g_bf, "w_g_bf", zbg, "zb1"), (bu, w_u_bf, "w_u_bf", zbu, "zbu")):
                def mm(e, bank=bank, wt=wt, zb=zb, f0=f0, nf=nf):
                    for j in range(nf):
                        out = ps[bank][:, j * N:(j + 1) * N]
                        col = (f0 + j) * 128
                        for kc in range(8):
                            e.matmul(out, lhsT=wt[:, kc, col:col + 128], rhs=hT[:, kc, 0:N], start=(kc == 0), stop=False)
                        ins = e.matmul(out, lhsT=zb[:, col:col + 128], rhs=selb_[:, 0:N], start=False, stop=True)
                    return ins
                P.add("pe", mm, r=[wk, zk, "hT", selk], w=[f"ps{bank}"])
            sigmoid_from(sgc[:, 0:nf * N], ps[bg][:, 0:nf * N], f"ps{bg}", "utok")
            P.add("dve", lambda e, bg=bg, nf=nf: e.tensor_tensor(out=sgc[:, 0:nf * N], in0=sgc[:, 0:nf * N], in1=ps[bg][:, 0:nf * N], op=ALU.mult),
                  r=["utok", f"ps{bg}"], w=["utok"])
            P.add("dve", lambda e, bu=bu, nf=nf, f0=f0: e.tensor_tensor(out=actT[:, f0:f0 + nf, 0:N], in0=sgc[:, 0:nf * N].rearrange("p (c t) -> p c t", c=nf),
                                                                      in1=ps[bu][:, 0:nf * N].rearrange("p (c t) -> p c t", c=nf), op=ALU.mult),
                  r=["utok", f"ps{bu}"], w=["actT"])
        for half in range(2):
            def mmd(e, half=half):
                for fc in range(NFF):
                    ins = e.matmul(ps[5 + half][0:N, :], lhsT=actT[:, fc, 0:N], rhs=w_d_bf[:, fc, half * 512:(half + 1) * 512], start=(fc == 0), stop=(fc == NFF - 1))
                return ins
            P.add("pe", mmd, r=["actT", "w_d_bf"], w=[f"ps{5 + half}"])
            P.add("dve", lambda e, half=half: e.tensor_tensor(out=ybuf[0:N, half * 512:(half + 1) * 512], in0=ps[5 + half][0:N, :],
                                                           in1=Gt[0:N, half * 512:(half + 1) * 512], op=ALU.mult), r=[f"ps{5 + half}", Gk], w=["ybuf"])
        P.add("dve", lambda e: e.tensor_tensor(out=ybuf[0:N, :], in0=ybuf[0:N, :], in1=xb[0:N, :], op=ALU.add), r=["ybuf", xk], w=["ybuf"])
        P.add("sp", lambda e: e.dma_start(out=y_dst, in_=ybuf[0:N, :]), r=["ybuf"], w=[], sem="st_y")

    for i in range(n_own):
        ld(dummy[:], dummy[:], "dummy", "dmy") if False else None
        passC(i, 128, x1_d[i, :, :], A2, "A1", G2, "G1", selPb, "selPb", y_o[i, :, :])
    if do_sample:
        passC(NT, 32, x1_d[NT, 0:32, :], A2s, "A1s", G2s, "G1s", selSb, "selSb", ys_o[:, :])
    esC.close()

    final = [k for k in P.dma_sems if k.startswith("st_")]
    P.emit(final_sems=final)
    es.close()
    return nc


def _rope_table(pos):
    half = 32
    inv = np.power(np.float32(10000.0), -np.arange(half, dtype=np.float32) / np.float32(half)).astype(np.float32)
    ang = pos.astype(np.float32)[:, None] * inv[None, :]
    return np.concatenate([np.cos(ang), np.sin(ang)], axis=-1).astype(np.float32)


def _host_prep(inp, n_own=NT):
    bf = ml_dtypes.bfloat16
    f32 = np.float32
    common = {}
    common["identf"] = np.eye(128, dtype=f32)
    e = np.zeros((128, 64, 128), f32)
    for kt in range(64):
        e[2 * kt, kt, 0:64] = 1.0
        e[2 * kt + 1, kt, 64:128] = 1.0
    common["eall"] = e.reshape(128, 64 * 128)
    s2 = np.zeros((128, 2), f32)
    s2[0:64, 0] = 1.0 / 64
    s2[64:128, 1] = 1.0 / 64
    common["sel2"] = s2
    sp = np.zeros((5, 128), f32)
    sp[0, :] = 1.0
    common["selP"] = sp
    ssel = np.zeros((5, 32), f32)
    for q in range(4):
        ssel[1 + q, 8 * q:8 * q + 8] = 1.0
    common["selS"] = ssel
    for k in ("cache_cmp_k", "cache_cmp_v", "cache_sel_k", "cache_sel_v"):
        common[k] = inp[k].reshape(5120 * 128, 128)
    mas = np.zeros((1, 256), f32)
    mas[0, 0] = 8.0
    mas[0, 255] = 8.0
    common["madd_s"] = mas
    rr_ = np.arange(128)
    common["mwin_s"] = np.where(rr_[:, None] > np.arange(8)[None, :], 0.0, -BIG).astype(f32)
    mn = np.full((32, 32), -BIG, f32)
    for q in range(4):
        for t in range(8):
            mn[8 * q:8 * q + t + 1, 8 * q + t] = 0.0
    common["mnew_s"] = mn
    common["cs_s"] = np.tile(_rope_table(16384 + np.arange(8)), (4, 1))
    for k in ("w_ada", "b_ada", "mix_norm_g", "ffn_norm_g", "w_in", "q_norm_g", "w_pw2", "w_nsa_o", "w_out", "w_gate", "w_up", "w_down"):
        common[k] = np.ascontiguousarray(inp[k][0]) if inp[k].ndim == 3 else np.ascontiguousarray(inp[k])
    common["k_norm_g"] = np.ascontiguousarray(inp["k_norm_g"][0].reshape(1, 192))
    common["wdwT"] = np.ascontiguousarray(inp["w_dw"][0].T.reshape(4, 128, CONV_K).transpose(1, 0, 2))
    cv = np.stack([inp["b_dw"][0], inp["conv_ln_g"][0], inp["conv_ln_b"][0]], 0)
    common["cvec"] = np.ascontiguousarray(cv.reshape(3, 4, 128).transpose(2, 0, 1))
    common["modk"] = np.ascontiguousarray(np.tile(inp["cmp_mod_k"][0], (2, 1)))
    common["modv"] = np.ascontiguousarray(np.tile(inp["cmp_mod_v"][0], (2, 1)))
    for nm, src in (("wkbd", "cmp_w_k"), ("wvbd", "cmp_w_v")):
        m = np.zeros((128, 128), f32)
        m[0:64, 0:64] = inp[src][0]
        m[64:128, 64:128] = inp[src][0]
        common[nm] = m
    r = np.arange(128)
    tri = np.where(r[:, None] <= r[None, :], 0.0, -BIG).astype(f32)
    winlo = np.where(r[:, None] > r[None, :], 0.0, -BIG).astype(f32)
    maps = []
    for c in range(8):
        b, par = c // 2, c % 2
        m = dict(common)
        xp = inp["x_prompt"][b].reshape(64, 128, D)
        own = np.arange(NT) * 2 + par
        oth = np.arange(NT) * 2 + (1 - par)
        m["xo"] = np.ascontiguousarray(xp[own])
        m["xt"] = np.ascontiguousarray(xp[oth])
        xflat = inp["x_prompt"][b]
        xhh = np.zeros((NT, 32, D), f32)
        hvv = np.zeros((128, NT), f32)
        for i in range(NT):
            s0 = own[i] * 128
            if s0 > 0:
                xhh[i] = xflat[s0 - 32:s0]
                hvv[:, i] = 1.0
        m["xh"] = xhh
        m["hv"] = hvv
        pos_o = (own[:, None] * 128 + r[None, :]).reshape(-1)
        pos_t = (oth[:, None] * 128 + r[None, :]).reshape(-1)
        m["cs_o"] = _rope_table(pos_o).reshape(NT, 128, 64)
        m["cs_t"] = _rope_table(pos_t).reshape(NT, 128, 64)
        gb = np.zeros(128, np.int64)
        for i in range(NT):
            gb[2 * i], gb[2 * i + 1] = 2 * own[i], 2 * own[i] + 1
            gb[64 + 2 * i], gb[64 + 2 * i + 1] = 2 * oth[i], 2 * oth[i] + 1
        mc = np.zeros((NT, 128, 128), f32)
        ma = np.zeros((NT, 128, 128), f32)
        for i in range(NT):
            qpos = own[i] * 128 + r
            cur = qpos // 64
            ok = (gb[None, :] + 1) * 64 <= qpos[:, None] + 1
            mc[i] = np.where(ok, 0.0, -BIG)
            vis = gb[None, :] <= cur[:, None]
            forced = (gb[None, :] == 0) | (gb[None, :] == cur[:, None]) | (gb[None, :] == cur[:, None] - 1)
            ma[i] = np.where(vis, np.where(forced, 8.0, 0.0), -1.0)
        m["m_cmp"] = mc
        m["m_add"] = ma
        m["m_cmpT"] = np.ascontiguousarray(mc.transpose(0, 2, 1))
        full = np.zeros((128, 128), f32)
        none = np.full((128, 128), -BIG, f32)
        m["masks"] = np.ascontiguousarray(np.stack([tri, winlo, none if par == 0 else full, full if par == 0 else none], 1))
        m["xs"] = np.ascontiguousarray(inp["x_sample"][4 * c:4 * c + 4].reshape(32, D))
        m["cwk"] = np.ascontiguousarray(inp["cache_win_k"][0, 4 * c:4 * c + 4].reshape(4, 512, 128))
        m["cwv"] = np.ascontiguousarray(inp["cache_win_v"][0, 4 * c:4 * c + 4].reshape(4, 512, 128))
        m["sconv"] = np.ascontiguousarray(inp["state_conv"][0, 4 * c:4 * c + 4])
        m["pt"] = np.ascontiguousarray(inp["page_table"][4 * c:4 * c + 4].reshape(1, 512).astype(np.int32))
        cs = inp["c_sample"][4 * c:4 * c + 4]
        c5 = np.concatenate([inp["c_prompt"][b:b + 1], cs], 0)
        m["c5T"] = np.ascontiguousarray(c5.T.reshape(8, 128, 5).transpose(1, 0, 2).reshape(128, 40))
        maps.append(m)
    return maps


_CACHE = {}


def kernel(**inp):
    inp = {k: np.asarray(v) for k, v in inp.items()}
    if "nc" not in _CACHE:
        _CACHE["nc"] = build()
    nc = _CACHE["nc"]
    maps = _host_prep(inp)
    res = run_bass_kernel_spmd(nc, maps, core_ids=list(range(8)))
    R = res.results
    B, T = 4, 8192
    yp = np.zeros((B, T, D), np.float32)
    kv = np.zeros((B, T, 768), np.float32)
    for c in range(8):
        b, par = c // 2, c % 2
        yp[b].reshape(64, 128, D)[par::2] = R[c]["y_o"]
        kv[b].reshape(64, 128, 768)[par::2] = R[c]["kv_o"]
    kv6 = kv.reshape(B, T, 6, 2, 64)
    outs = [yp, np.concatenate([R[c]["ys_o"] for c in range(8)], 0).reshape(32, 8, D)]
    for j in range(4):
        outs.append(np.ascontiguousarray(kv6[None, :, :, j]))
    for j in (4, 5):
        outs.append(np.ascontiguousarray(kv6[None, :, -512:, j]))
    pconv = np.stack([R[2 * b + 1]["conv_o"] for b in range(B)], 0)[None]
    outs.append(pconv)
    skv = np.concatenate([R[c]["skv_o"].reshape(4, 8, 6, 2, 64) for c in range(8)], 0)
    for j in range(4):
        outs.append(np.ascontiguousarray(skv[None, :, :, j]))
    swin = np.concatenate([R[c]["swin_o"] for c in range(8)], 0)
    for j in range(2):
        outs.append(np.ascontiguousarray(swin[None, :, j].reshape(1, 32, 512, 2, 64)))
    outs.append(np.concatenate([R[c]["sconv_o"] for c in range(8)], 0)[None])
    return tuple(outs)
```

```python
import numpy as np
import ml_dtypes
from contextlib import ExitStack
import concourse.bass as bass
import concourse.mybir as mybir
from concourse.bass_utils import run_bass_kernel_spmd

F32 = mybir.dt.float32
BF = mybir.dt.bfloat16
I32 = mybir.dt.int32
ALU = mybir.AluOpType
AF = mybir.ActivationFunctionType
AX = mybir.AxisListType

D = 1024
NT = 32
OFF_Q = 1024
OFF_KV = 1536
OFF_NG = 2304
OFF_MG = 2328
D_IN = 4376
D_FF = 2816
NFF = 22
EPS = 1e-6
BIG = 30000.0
SCALE = 0.125
CONV_K = 31


class Op:
    __slots__ = ("eng", "fn", "deps", "signal", "val", "sem", "is_dma", "idx")


class Prog:
    ENGS = ("pe", "act", "dve", "pool", "sp")

    def __init__(self, nc, es):
        self.nc = nc
        self.es = es
        self.ops = {e: [] for e in self.ENGS}
        self.lastw = {}
        self.readers = {}
        self.dma_sems = {}
        self.n = 0
        self.bar = None

    def add(self, eng, fn, r=(), w=(), sem=None):
        op = Op()
        op.eng, op.fn, op.signal, op.val, op.sem = eng, fn, False, None, None
        op.is_dma = sem is not None
        op.idx = self.n
        self.n += 1
        deps = []
        for k in list(r) + list(w):
            if k not in self.lastw and k not in self.readers and self.bar is not None:
                deps.append((self.bar, True))
        for k in r:
            d = self.lastw.get(k)
            if d is not None:
                deps.append((d, True))
        for k in w:
            d = self.lastw.get(k)
            if d is not None:
                deps.append((d, False))
            for d in self.readers.get(k, ()):
                deps.append((d, False))
        seen = set()
        op.deps = []
        for d, raw in sorted(deps, key=lambda t: not t[1]):
            if id(d) in seen or d is op:
                continue
            if d.eng == eng and not d.is_dma and not op.is_dma:
                if eng == "pe" or not raw:
                    continue
            seen.add(id(d))
            if d.is_dma:
                if sem is not None and d.sem == sem:
                    continue
                op.deps.append((d.sem, self.dma_sems[d.sem][1]))
            else:
                d.signal = True
                op.deps.append(d)
        if op.is_dma:
            if sem not in self.dma_sems:
                self.dma_sems[sem] = [self.es.enter_context(self.nc.semaphore("d_" + sem)), 0]
            self.dma_sems[sem][1] += 16
            op.sem = sem
            op.val = self.dma_sems[sem][1]
        for k in r:
            self.readers.setdefault(k, []).append(op)
        for k in w:
            self.lastw[k] = op
            self.readers[k] = []
        self.ops[eng].append(op)
        return op

    def barrier(self, fn):
        keys = set(self.lastw) | set(self.readers)
        op = self.add("dve", fn, r=[], w=sorted(keys))
        self.bar = op
        return op

    def emit(self, final_sems=()):
        nc = self.nc
        esem = {e: self.es.enter_context(nc.semaphore("e_" + e)) for e in self.ENGS}
        for e in self.ENGS:
            c = 0
            for op in self.ops[e]:
                if op.signal and not op.is_dma:
                    c += 1
                    op.val = c
        block = self.es.enter_context(nc.Block())
        prog = self

        def run(e, eng):
            waited = {}
            for op in prog.ops[e]:
                for d in op.deps:
                    if isinstance(d, tuple):
                        key, h, v = "d_" + d[0], prog.dma_sems[d[0]][0], d[1]
                    else:
                        key, h, v = "e_" + d.eng, esem[d.eng], d.val
                    if waited.get(key, 0) >= v:
                        continue
                    waited[key] = v
                    eng.wait_ge(h, v)
                ins = op.fn(eng)
                if op.is_dma:
                    ins.then_inc(prog.dma_sems[op.sem][0], 16)
                elif op.signal:
                    ins.then_inc(esem[e], 1)
            if e == "sp":
                for s in final_sems:
                    h, v = prog.dma_sems[s]
                    eng.wait_ge(h, v)

        @block.tensor
        def _(eng):
            run("pe", eng)

        @block.scalar
        def _(eng):
            run("act", eng)

        @block.vector
        def _(eng):
            run("dve", eng)

        @block.gpsimd
        def _(eng):
            run("pool", eng)

        @block.sync
        def _(eng):
            run("sp", eng)


def build(n_own=NT, do_sample=True):
    nc = bass.Bass("TRN2", target_bir_lowering=False)
    es = ExitStack()
    P = Prog(nc, es)

    def din(name, shape, dt=F32):
        return nc.dram_tensor(name, list(shape), dt, kind="ExternalInput").ap()

    def dout(name, shape, dt=F32):
        return nc.dram_tensor(name, list(shape), dt, kind="ExternalOutput").ap()

    def sb(name, shape, dt=F32):
        return es.enter_context(nc.sbuf_tensor("s_" + name, list(shape), dt))

    xo = din("xo", [NT, 128, D])
    xt = din("xt", [NT, 128, D])
    xh = din("xh", [NT, 32, D])
    cs_o = din("cs_o", [NT, 128, 64])
    cs_t = din("cs_t", [NT, 128, 64])
    hv = din("hv", [128, NT])
    m_cmp = din("m_cmp", [NT, 128, 128])
    m_add = din("m_add", [NT, 128, 128])
    m_cmpT = din("m_cmpT", [NT, 128, 128])
    masks_d = din("masks", [128, 4, 128])
    eall_d = din("eall", [128, 64 * 128])
    identf_d = din("identf", [128, 128])
    sel2_d = din("sel2", [128, 2])
    selP_d = din("selP", [5, 128])
    c5T_d = din("c5T", [128, 40])
    w_ada = din("w_ada", [D, 6 * D])
    b_ada = din("b_ada", [1, 6 * D])
    gmix = din("mix_norm_g", [1, D])
    gffn = din("ffn_norm_g", [1, D])
    w_in = din("w_in", [D, D_IN])
    gq_d = din("q_norm_g", [1, 64])
    gk_d = din("k_norm_g", [1, 192])
    wdwT_d = din("wdwT", [128, 4, CONV_K])
    cvec_d = din("cvec", [128, 3, 4])
    w_pw2 = din("w_pw2", [512, D])
    modk_d = din("modk", [128, 64])
    modv_d = din("modv", [128, 64])
    wkbd_d = din("wkbd", [128, 128])
    wvbd_d = din("wvbd", [128, 128])
    w_nsa_o = din("w_nsa_o", [512, D])
    w_out = din("w_out", [D, D])
    w_gate = din("w_gate", [D, D_FF])
    w_up = din("w_up", [D, D_FF])
    w_down = din("w_down", [D_FF, D])

    xs_d = din("xs", [32, D])
    cs_s = din("cs_s", [32, 64])
    selS_d = din("selS", [5, 32])
    cwk_d = din("cwk", [4, 512, 128])
    cwv_d = din("cwv", [4, 512, 128])
    sconv_d = din("sconv", [4, 30, 512])
    pt_d = din("pt", [1, 512], I32)
    ccmpk = din("cache_cmp_k", [5120 * 128, 128])
    ccmpv = din("cache_cmp_v", [5120 * 128, 128])
    cselk = din("cache_sel_k", [5120 * 128, 128])
    cselv = din("cache_sel_v", [5120 * 128, 128])
    madds_d = din("madd_s", [1, 256])
    mwin_d = din("mwin_s", [128, 8])
    mnew_d = din("mnew_s", [32, 32])
    ys_o = dout("ys_o", [32, D])
    skv_o = dout("skv_o", [32, 768])
    swin_o = dout("swin_o", [4, 2, 512, 128])
    sconv_o = dout("sconv_o", [4, 30, 512])
    y_o = dout("y_o", [NT, 128, D])
    kv_o = dout("kv_o", [NT, 128, 768])
    conv_o = dout("conv_o", [30, 512])
    x1_d = nc.dram_tensor("x1_scr", [NT + 1, 128, D], F32, kind="Internal").ap()
    o_d = nc.dram_tensor("o_scr", [NT + 1, 128, 512], BF, kind="Internal").ap()

    ps = [es.enter_context(nc.psum_tensor(f"ps{i}", [128, 512], F32)) for i in range(8)]

    def psb(i):
        return ps[i][:].bitcast(BF)

    identb = sb("identb", [128, 128], BF)
    identf = sb("identf", [128, 128])
    masks = sb("masks", [128, 4, 128], BF)
    sel2 = sb("sel2", [128, 2], BF)
    selP = sb("selP", [5, 128])
    selPb = sb("selPb", [5, 128], BF)
    selS = sb("selS", [5, 32])
    selSb = sb("selSb", [5, 32], BF)
    A1s = sb("A1s", [32, D])
    G1s = sb("G1s", [32, D])
    pm1 = sb("pm1", [128, 2, 64])
    gk_bc = sb("gk_bc", [128, 192])
    gq_bc = sb("gq_bc", [128, 64])
    wkbd = sb("wkbd", [128, 128], BF)
    wvbd = sb("wvbd", [128, 128], BF)
    hv_sb = sb("hv_sb", [128, NT])
    dummy = sb("dummy", [128, 1])
    nhalf = sb("nhalf", [128, 128])
    wdwT = sb("wdwT", [128, 4, CONV_K])
    cvec = sb("cvec", [128, 3, 4])
    A1 = sb("A1", [128, D])
    G1 = sb("G1", [128, D])
    zb1 = sb("zb1", [5, D_IN], BF)
    sh2T = sb("sh2T", [128, 8, 5], BF)
    a2row = sb("a2row", [5, D])
    g2row = sb("g2row", [5, D])

    def ld(dst, src, key, sem, eng="sp", extra_w=()):
        P.add(eng, lambda e, d=dst, s=src: e.dma_start(out=d, in_=s), r=[], w=[key] + list(extra_w), sem=sem)

    ld(identf[:], identf_d[:, :], "identf", "c0")
    ld(identb[:], identf_d[:, :], "identb", "c1", eng="pool")
    ld(masks[:], masks_d[:, :, :], "masks", "c3", eng="pool")
    ld(sel2[:], sel2_d[:, :], "sel2", "c4", eng="pool")
    ld(selP[:], selP_d[:, :], "selP", "c5")
    ld(selPb[:], selP_d[:, :], "selPb", "c6", eng="pool")
    ld(selS[:], selS_d[:, :], "selS", "c15")
    ld(selSb[:], selS_d[:, :], "selSb", "c16", eng="pool")
    ld(pm1[:, 0, :], modk_d[:, :], "pm1", "c7")
    ld(pm1[:, 1, :], modv_d[:, :], "pm1", "c7")
    ld(gk_bc[:], gk_d[0:1, :].to_broadcast([128, 192]), "gk_bc", "c8")
    ld(gq_bc[:], gq_d[0:1, :].to_broadcast([128, 64]), "gq_bc", "c9")
    ld(wkbd[:], wkbd_d[:, :], "wkbd", "c10", eng="pool")
    ld(wvbd[:], wvbd_d[:, :], "wvbd", "c11", eng="pool")
    ld(hv_sb[:], hv[:, :], "hv", "c12")
    ld(wdwT[:], wdwT_d[:, :, :], "wdwT", "c13")
    ld(cvec[:], cvec_d[:, :, :], "cvec", "c14")
    P.add("pool", lambda e: e.memset(nhalf[:], -0.5), w=["nhalf"])
    P.add("dve", lambda e: e.tensor_scalar(out=pm1[:], in0=pm1[:], scalar1=1.0, scalar2=None, op0=ALU.add),
          r=["pm1"], w=["pm1"])

    w_in_v = w_in.rearrange("(kc p) n -> p kc n", p=128)
    def rsqrt_pool(out_ap, in_ap, inv_n, ncol, rk, wk):
        P.add("pool", lambda e: e.tensor_scalar(out=out_ap, in0=in_ap, scalar1=float(inv_n), scalar2=float(EPS), op0=ALU.mult, op1=ALU.add),
              r=[rk], w=[wk])
        P.add("pool", lambda e: e.tensor_tensor(out=out_ap, in0=out_ap, in1=nhalf[0:out_ap.shape[0], 0:ncol], op=ALU.pow),
              r=[wk, "nhalf"], w=[wk])

    with ExitStack() as es0:
        def sb0(name, shape, dt=F32):
            return es0.enter_context(nc.sbuf_tensor("s0_" + name, list(shape), dt))
        c5T = sb0("c5T", [128, 40])
        scT = sb0("scT", [128, 40])
        tmp40 = sb0("tmp40", [128, 40])
        mod5 = sb0("mod5", [5, 6 * D])
        bada = [sb0(f"bada{i}", [5, 512]) for i in range(2)]
        wz = [sb0(f"wz{i}", [128, 8, 512], BF) for i in range(2)]
        g5 = sb0("g5", [5, 2, D])
        a5 = sb0("a5", [5, 2, D])
        wa = [sb0(f"wa{i}", [128, 8, 512]) for i in range(2)]
        sh1T = sb0("sh1T", [128, 8, 5], BF)

        ld(c5T[:], c5T_d[:, :], "c5T", "p0")
        ld(g5[:, 0, :], gmix[0:1, :].to_broadcast([5, D]), "g5", "p2")
        ld(g5[:, 1, :], gffn[0:1, :].to_broadcast([5, D]), "g5", "p2")
        P.add("act", lambda e: e.activation(out=tmp40[:], in_=c5T[:], func=AF.Tanh, scale=0.5), r=["c5T"], w=["tmp40"])
        P.add("dve", lambda e: e.tensor_scalar(out=tmp40[:], in0=tmp40[:], scalar1=0.5, scalar2=0.5, op0=ALU.mult, op1=ALU.add),
              r=["tmp40"], w=["tmp40"])
        P.add("dve", lambda e: e.tensor_tensor(out=scT[:], in0=c5T[:], in1=tmp40[:], op=ALU.mult), r=["tmp40", "c5T"], w=["scT"])
        wa_v = w_ada.rearrange("(kc p) n -> p kc n", p=128)
        for cg in range(12):
            wb = wa[cg % 2]
            wk = f"wa{cg % 2}"
            for kc in range(8):
                ld(wb[:, kc, :], wa_v[:, kc, cg * 512:(cg + 1) * 512], wk, wk)
            ld(bada[cg % 2][:], b_ada[0:1, cg * 512:(cg + 1) * 512].to_broadcast([5, 512]), f"bada{cg % 2}", f"bada{cg % 2}")

            def mm(e, wb=wb):
                for kc in range(8):
                    ins = e.matmul(ps[0][0:5, :], lhsT=scT[:, kc * 5:(kc + 1) * 5], rhs=wb[:, kc, :], start=(kc == 0), stop=(kc == 7))
                return ins
            P.add("pe", mm, r=["scT", wk], w=["ps0"])
            P.add("dve", lambda e, cg=cg: e.tensor_tensor(out=mod5[:, cg * 512:(cg + 1) * 512], in0=ps[0][0:5, :],
                                                         in1=bada[cg % 2][:], op=ALU.add),
                  r=["ps0", f"bada{cg % 2}"], w=["mod5"])
        P.add("dve", lambda e: e.scalar_tensor_tensor(out=a5[:, 0, :], in0=mod5[:, D:2 * D], scalar=1.0, in1=g5[:, 0, :], op0=ALU.add, op1=ALU.mult),
              r=["mod5", "g5"], w=["a5"])
        P.add("dve", lambda e: e.scalar_tensor_tensor(out=a5[:, 1, :], in0=mod5[:, 4 * D:5 * D], scalar=1.0, in1=g5[:, 1, :], op0=ALU.add, op1=ALU.mult),
              r=["mod5", "g5"], w=["a5"])
        P.add("pool", lambda e: e.tensor_copy(out=a2row[:], in_=a5[:, 1, :]), r=["a5"], w=["a2row"])
        P.add("pool", lambda e: e.tensor_copy(out=g2row[:], in_=mod5[:, 5 * D:6 * D]), r=["mod5"], w=["g2row"])
        for dst, key, src in ((A1, "A1", a5[:, 0, :]), (G1, "G1", mod5[:, 2 * D:3 * D])):
            for half in range(2):
                P.add("pe", lambda e, src=src, half=half: e.matmul(ps[1][:, :], lhsT=selP[:, :], rhs=src[:, half * 512:(half + 1) * 512], start=True, stop=True),
                      r=["selP", "a5", "mod5"], w=["ps1"])
                P.add("act", lambda e, dst=dst, half=half: e.copy(out=dst[:, half * 512:(half + 1) * 512], in_=ps[1][:, :]),
                      r=["ps1"], w=[key])
        for dst, key, src in ((A1s, "A1s", a5[:, 0, :]), (G1s, "G1s", mod5[:, 2 * D:3 * D])):
            for half in range(2):
                P.add("pe", lambda e, src=src, half=half: e.matmul(ps[1][0:32, :], lhsT=selS[:, :], rhs=src[:, half * 512:(half + 1) * 512], start=True, stop=True),
                      r=["selS", "a5", "mod5"], w=["ps1"])
                P.add("act", lambda e, dst=dst, half=half: e.copy(out=dst[:, half * 512:(half + 1) * 512], in_=ps[1][0:32, :]),
                      r=["ps1"], w=[key])
        for dst, key, off in ((sh1T, "sh1T", 0), (sh2T, "sh2T", 3 * D)):
            def tr(e, off=off):
                for kc in range(8):
                    ins = e.transpose(out=ps[2][:, kc * 5:(kc + 1) * 5], in_=mod5[0:5, off + kc * 128: off + (kc + 1) * 128], identity=identf[0:5, 0:5])
                return ins
            P.add("pe", tr, r=["mod5", "identf"], w=["ps2"])
            P.add("act", lambda e, dst=dst: e.copy(out=dst[:].rearrange("p a b -> p (a b)"), in_=ps[2][:, 0:40]), r=["ps2"], w=[key])
        for ci, c0 in enumerate(range(0, D_IN, 512)):
            c1 = min(c0 + 512, D_IN)
            wzb = wz[ci % 2]
            wzk = f"wz{ci % 2}"
            for kc in range(8):
                ld(wzb[:, kc, 0:c1 - c0], w_in_v[:, kc, c0:c1], wzk, wzk, eng="pool")

            def mm(e, c0=c0, c1=c1, wzb=wzb):
                for kc in range(8):
                    ins = e.matmul(ps[3][0:5, 0:c1 - c0], lhsT=sh1T[:, kc, :], rhs=wzb[:, kc, 0:c1 - c0], start=(kc == 0), stop=(kc == 7))
                return ins
            P.add("pe", mm, r=["sh1T", wzk], w=["ps3"])
            P.add("act", lambda e, c0=c0, c1=c1: e.copy(out=zb1[:, c0:c1], in_=ps[3][0:5, 0:c1 - c0]), r=["ps3"], w=["zb1"])
    P.barrier(lambda e: e.memset(dummy[:], 1.0))

    utok = sb("utok", [128, 512])
    xbuf = [sb(f"xbuf{i}", [128, D]) for i in range(2)]
    csb = [sb(f"csb{i}", [128, 64]) for i in range(2)]
    junk = sb("junk", [128, D], BF)
    ss = sb("ss", [128, 1])
    rr = sb("rr", [128, 1])
    hbf = sb("hbf", [128, D], BF)
    hT = sb("hT", [128, 8, 128], BF)
    rt = [sb(f"rt{i}", [128, 8, 32]) for i in range(4)]

    esA = ExitStack()

    def sbA(name, shape, dt=F32):
        return esA.enter_context(nc.sbuf_tensor("sA_" + name, list(shape), dt))
    w_glu_bf = sbA("w_glu_bf", [128, 8, 1024], BF)
    for kc in range(8):
        ld(w_glu_bf[:, kc, :], w_in_v[:, kc, 0:1024], "w_glu_bf", "w1", eng="pool")
    eall = sbA("eall", [128, 64 * 128], BF)
    ld(eall[:], eall_d[:, :], "eall", "c2", eng="pool")
    NA = OFF_MG - OFF_Q
    w_in_bf = sbA("w_in_bf", [128, 8, NA], BF)
    for kc in range(8):
        ld(w_in_bf[:, kc, :], w_in_v[:, kc, OFF_Q:OFF_MG], "w_in_bf", "w0", eng="pool")
    zkv = sbA("zkv", [128, 768])
    sqk = sbA("sqk", [128, 3, 128])
    ssk = sbA("ssk", [128, 6])
    rk = sbA("rk", [128, 6])
    kn = sbA("kn", [128, 3, 128])
    kvout = [sbA(f"kvout{i}", [128, 768]) for i in range(2)]
    kbf = sbA("kbf", [128, 2, 128], BF)
    cmod = sbA("cmod", [128, 2, 128], BF)
    KTsel = sbA("KTsel", [128, 64 * 128], BF)
    KTwin = sbA("KTwin", [128, 6 * 128], BF)
    Vsel = sbA("Vsel", [128, 64, 2, 65], BF)
    Vwin = sbA("Vwin", [128, 6, 2, 65], BF)
    summT = sbA("summT", [128, 2, 128], BF)
    kcT = sbA("kcT", [128, 128], BF)
    Vcmp = sbA("Vcmp", [128, 2, 65], BF)
    OTsb = sbA("OTsb", [128, 512])
    zeros_bf = sbA("zeros_bf", [128, 260], BF)
    P.add("pool", lambda e: e.memset(zeros_bf[:], 0.0), w=["zeros_bf"])
    zq = sbA("zq", [128, 512])
    sqq = sbA("sqq", [128, 512])
    ssq = sbA("ssq", [128, 8])
    rq = sbA("rq", [128, 8])
    gate24 = sbA("gate24", [128, 24])
    qpb = sbA("qpb", [128, 4, 2, 64], BF)
    qrb = sbA("qrb", [128, 4, 2, 64], BF)
    QT = sbA("QT", [128, 8, 128], BF)
    mcmp_t = sbA("mcmp_t", [128, 128])
    madd_t = sbA("madd_t", [128, 128])
    mcT_t = sbA("mcT_t", [128, 128], BF)
    sbias = sbA("sbias", [128, 4, 128])
    ecmp = sbA("ecmp", [128, 4, 128])
    bnc_w = sbias
    bnc_c = ecmp[0:32].rearrange("p a b -> p (a b)")
    l4 = sbA("l4", [128, 4])
    rl4 = sbA("rl4", [128, 4])
    score = sbA("score", [128, 128])
    sc2 = sbA("sc2", [128, 128])
    mx8 = sbA("mx8", [128, 16])
    selb = sbA("selb", [128, 128], BF)
    selbT = sbA("selbT", [128, 128], BF)
    PT = [sbA(f"PT{i}", [128, 512], BF) for i in range(2)]
    Obr = sbA("Obr", [128, 3, 4, 65])
    l3 = sbA("l3", [128, 3, 4])
    coef = sbA("coef", [128, 3, 4])
    otmp = sbA("otmp", [128, 4, 64])
    otok = sbA("otok", [128, 8, 64])
    obf = sbA("obf", [128, 512], BF)

    P.add("pool", lambda e: e.memset(Vsel[:].rearrange("p a b c -> p (a b c)"), 1.0), w=["Vsel"])
    P.add("pool", lambda e: e.memset(Vwin[:].rearrange("p a b c -> p (a b c)"), 1.0), w=["Vwin"])
    P.add("pool", lambda e: e.memset(Vcmp[:].rearrange("p a b -> p (a b)"), 1.0), w=["Vcmp"])
    P.add("pool", lambda e: e.memset(summT[:].rearrange("p a b -> p (a b)"), 0.0), w=["summT"])

    state = {"xi": 0}

    def rope_ops(src4, dst1, dst2, cs, nh, rkeys, wkey, N=128):
        x1, x2 = src4
        shp = list(x1.shape)
        cos = cs[0:N, 0:32]
        sin = cs[0:N, 32:64]
        for _ in range(len(shp) - 2):
            cos = cos.unsqueeze(1)
            sin = sin.unsqueeze(1)
        cos = cos.to_broadcast(shp)
        sin = sin.to_broadcast(shp)
        t = [rt[j][0:N, 0:nh, :] if len(shp) == 3 else rt[j][0:N, 0:nh, :].rearrange("p (a b) d -> p a b d", a=shp[1]) for j in range(4)]

        P.add("dve", lambda e: e.tensor_tensor(out=t[0], in0=x1, in1=cos, op=ALU.mult), r=rkeys, w=["rt0"])
        P.add("dve", lambda e: e.tensor_tensor(out=t[1], in0=x2, in1=sin, op=ALU.mult), r=rkeys, w=["rt1"])
        P.add("dve", lambda e: e.tensor_tensor(out=t[2], in0=x2, in1=cos, op=ALU.mult), r=rkeys, w=["rt2"])
        P.add("dve", lambda e: e.tensor_tensor(out=t[3], in0=x1, in1=sin, op=ALU.mult), r=rkeys, w=["rt3"])
        P.add("dve", lambda e: e.tensor_tensor(out=dst1, in0=t[0], in1=t[1], op=ALU.subtract), r=["rt0", "rt1"], w=[wkey])
        P.add("dve", lambda e: e.tensor_tensor(out=dst2, in0=t[2], in1=t[3], op=ALU.add), r=["rt2", "rt3"], w=[wkey])

    def front_norm(xsrc, cs_src, N, At, Ak, src_key=None):
        ib = state["xi"] % 2
        state["xi"] += 1
        xb = xbuf[ib]
        xk = f"xbuf{ib}"
        P.add("sp", lambda e: e.dma_start(out=xb[0:N, :], in_=xsrc), r=([src_key] if src_key else []), w=[xk], sem=xk)
        ld(csb[ib][0:N, :], cs_src, f"csb{ib}", f"csb{ib}")
        P.add("act", lambda e: e.activation(out=junk[0:N, :], in_=xb[0:N, :], func=AF.Square, accum_out=ss[0:N, :]), r=[xk], w=["junk", "ss"])
        rsqrt_pool(rr[0:N, :], ss[0:N, :], 1.0 / D, 1, "ss", "rr")
        P.add("dve", lambda e: e.scalar_tensor_tensor(out=hbf[0:N, :], in0=xb[0:N, :], scalar=rr[0:N, 0:1], in1=At[0:N, :], op0=ALU.mult, op1=ALU.mult),
              r=[xk, "rr", Ak], w=["hbf"])

        def tr(e):
            for kc in range(8):
                ins = e.transpose(out=psb(0)[:, kc * 128:kc * 128 + N], in_=hbf[0:N, kc * 128:(kc + 1) * 128], identity=identb[0:N, 0:N])
            return ins
        P.add("pe", tr, r=["hbf", "identb"], w=["ps0"])
        P.add("act", lambda e: e.copy(out=hT[:, :, 0:N], in_=psb(0)[:, :].rearrange("p (a b) -> p a b", a=8)[:, :, 0:N]), r=["ps0"], w=["hT"])
        return ib

    def proj_tok(bank, c0, n, N, selb, selk, wt=None, wk="w_in_bf", woff=OFF_Q):
        wt = w_in_bf if wt is None else wt

        def mm(e):
            for kc in range(8):
                e.matmul(ps[bank][0:N, 0:n], lhsT=hT[:, kc, 0:N], rhs=wt[:, kc, c0 - woff:c0 - woff + n], start=(kc == 0), stop=False)
            return e.matmul(ps[bank][0:N, 0:n], lhsT=selb[:, 0:N], rhs=zb1[:, c0:c0 + n], start=False, stop=True)
        P.add("pe", mm, r=["hT", wk, "zb1", selk], w=[f"ps{bank}"])

    def f1(xsrc, cs_src, N, At, Ak, selb, selk, slot, wslot, kv_dst, kvi, glu_dst=None):
        ib = front_norm(xsrc, cs_src, N, At, Ak)
        cs = csb[ib]
        csk = f"csb{ib}"
        proj_tok(1, OFF_KV, 512, N, selb, selk)
        proj_tok(2, OFF_KV + 512, 256, N, selb, selk)
        P.add("act", lambda e: e.copy(out=zkv[0:N, 0:512], in_=ps[1][0:N, :]), r=["ps1"], w=["zkv"])
        P.add("act", lambda e: e.copy(out=zkv[0:N, 512:768], in_=ps[2][0:N, 0:256]), r=["ps2"], w=["zkv"])
        zk = zkv[0:N, :].rearrange("p (k v c) -> p k v c", k=3, v=2, c=128)
        P.add("dve", lambda e: e.tensor_tensor(out=sqk[0:N], in0=zk[:, :, 0, :], in1=zk[:, :, 0, :], op=ALU.mult), r=["zkv"], w=["sqk"])
        P.add("dve", lambda e: e.tensor_reduce(out=ssk[0:N], in_=sqk[0:N].rearrange("p k (h d) -> p (k h) d", h=2), axis=AX.X, op=ALU.add),
              r=["sqk"], w=["ssk"])
        rsqrt_pool(rk[0:N], ssk[0:N], 1.0 / 64, 6, "ssk", "rk")
        gkv = gk_bc[0:N].rearrange("p (k d) -> p k d", k=3).unsqueeze(2).to_broadcast([N, 3, 2, 64])
        kn4 = kn[0:N].rearrange("p k (h d) -> p k h d", h=2)
        P.add("dve", lambda e: e.tensor_tensor(out=kn4, in0=zk[:, :, 0, :].rearrange("p k (h d) -> p k h d", h=2), in1=gkv, op=ALU.mult),
              r=["zkv", "gk_bc"], w=["kn"])
        kn6 = kn[0:N].rearrange("p k (h d) -> p (k h) d", h=2)
        P.add("dve", lambda e: e.tensor_tensor(out=kn6, in0=kn6, in1=rk[0:N].unsqueeze(2).to_broadcast([N, 6, 64]), op=ALU.mult),
              r=["kn", "rk"], w=["kn"])
        kvo = kvout[kvi % 2]
        kvk = f"kvout{kvi % 2}"
        kvo4 = kvo[0:N].rearrange("p (k v c) -> p k v c", k=3, v=2, c=128)
        P.add("pool", lambda e: e.tensor_copy(out=kvo4[:, 0, 0, :], in_=kn[0:N, 0, :]), r=["kn"], w=[kvk])
        P.add("pool", lambda e: e.tensor_copy(out=kvo4[:, :, 1, :], in_=zk[:, :, 1, :]), r=["zkv"], w=[kvk])
        src = kn[0:N, 1:3, :].rearrange("p k (h f d) -> p k h f d", h=2, f=2, d=32)
        dstv = kvo4[:, 1:3, 0, :].rearrange("p k (h f d) -> p k h f d", h=2, f=2, d=32)
        rope_ops((src[:, :, :, 0, :], src[:, :, :, 1, :]), dstv[:, :, :, 0, :], dstv[:, :, :, 1, :], cs, 4, ["kn", csk], kvk, N)
        if slot is not None:
            P.add("pool", lambda e: e.tensor_copy(out=kbf[:], in_=kvo4[:, 1:3, 0, :]), r=[kvk], w=["kbf"])
            P.add("pool", lambda e: e.tensor_copy(out=Vsel[:, slot, :, 0:64], in_=zk[:, 1, 1, :].rearrange("p (h d) -> p h d", h=2)),
                  r=["zkv"], w=["Vsel"])
            P.add("pool", lambda e: e.tensor_copy(out=Vwin[:, wslot, :, 0:64], in_=zk[:, 2, 1, :].rearrange("p (h d) -> p h d", h=2)),
                  r=["zkv"], w=["Vwin"])
            pm1k = pm1[:, 0, :].unsqueeze(1).to_broadcast([128, 2, 64])
            pm1v = pm1[:, 1, :].unsqueeze(1).to_broadcast([128, 2, 64])
            P.add("pool", lambda e: e.tensor_tensor(out=cmod[:, 0, :].rearrange("p (h d) -> p h d", h=2), in0=kn[:, 0, :].rearrange("p (h d) -> p h d", h=2),
                                                   in1=pm1k, op=ALU.mult), r=["kn", "pm1"], w=["cmod"])
            P.add("pool", lambda e: e.tensor_tensor(out=cmod[:, 1, :].rearrange("p (h d) -> p h d", h=2), in0=zk[:, 0, 1, :].rearrange("p (h d) -> p h d", h=2),
                                                   in1=pm1v, op=ALU.mult), r=["zkv", "pm1"], w=["cmod"])

            def tr(e):
                e.transpose(out=psb(3)[:, 0:128], in_=kbf[:, 0, :], identity=identb[:])
                return e.transpose(out=psb(3)[:, 128:256], in_=kbf[:, 1, :], identity=identb[:])
            P.add("pe", tr, r=["kbf", "identb"], w=["ps3"])
            P.add("act", lambda e: e.copy(out=KTsel[:, slot * 128:(slot + 1) * 128], in_=psb(3)[:, 0:128]), r=["ps3"], w=["KTsel"])
            P.add("act", lambda e: e.copy(out=KTwin[:, wslot * 128:(wslot + 1) * 128], in_=psb(3)[:, 128:256]), r=["ps3"], w=["KTwin"])

            def mmc(e):
                e.matmul(ps[4][:, 0:2], lhsT=cmod[:, 0, :], rhs=sel2[:, :], start=True, stop=True)
                return e.matmul(ps[4][:, 2:4], lhsT=cmod[:, 1, :], rhs=sel2[:, :], start=True, stop=True)
            P.add("pe", mmc, r=["cmod", "sel2"], w=["ps4"])
            P.add("act", lambda e: e.copy(out=summT[:, :, 2 * slot:2 * slot + 2], in_=ps[4][:, 0:4].rearrange("p (a b) -> p a b", a=2)),
                  r=["ps4"], w=["summT"])
        if kv_dst is not None:
            P.add("sp", lambda e: e.dma_start(out=kv_dst, in_=kvo[0:N, :]), r=[kvk], w=[], sem=f"st_{kvk}")
        if glu_dst is not None:
            proj_tok(5, 0, 512, N, selb, selk, wt=w_glu_bf, wk="w_glu_bf", woff=0)
            proj_tok(6, 512, 512, N, selb, selk, wt=w_glu_bf, wk="w_glu_bf", woff=0)
            P.add("act", lambda e: e.activation(out=utok[0:N, :], in_=ps[6][0:N, :], func=AF.Tanh, scale=0.5), r=["ps6"], w=["utok"])
            P.add("dve", lambda e: e.tensor_scalar(out=utok[0:N, :], in0=utok[0:N, :], scalar1=0.5, scalar2=0.5, op0=ALU.mult, op1=ALU.add),
                  r=["utok"], w=["utok"])
            P.add("dve", lambda e: e.tensor_tensor(out=utok[0:N, :], in0=utok[0:N, :], in1=ps[5][0:N, :], op=ALU.mult), r=["utok", "ps5"], w=["utok"])
            for (dap, r0, r1) in glu_dst:
                P.add("sp", lambda e, dap=dap, r0=r0, r1=r1: e.dma_start(out=dap, in_=utok[r0:r1, :]), r=["utok"], w=[], sem="st_utok")
        return ib

    def update_cmp():
        def mm(e):
            e.matmul(ps[4][:, 0:128], lhsT=wkbd[:], rhs=summT[:, 0, :], start=True, stop=True)
            return e.matmul(ps[4][:, 128:256], lhsT=summT[:, 1, :], rhs=wvbd[:], start=True, stop=True)
        P.add("pe", mm, r=["summT", "wkbd", "wvbd"], w=["ps4"])
        P.add("act", lambda e: e.copy(out=kcT[:], in_=ps[4][:, 0:128]), r=["ps4"], w=["kcT"])
        P.add("act", lambda e: e.copy(out=Vcmp[:, :, 0:64], in_=ps[4][:, 128:256].rearrange("p (h d) -> p h d", h=2)), r=["ps4"], w=["Vcmp"])

    def qfront(N, selb_, selk, cs, csk):
        proj_tok(1, OFF_Q, 512, N, selb_, selk)
        proj_tok(2, OFF_NG, 24, N, selb_, selk)
        P.add("act", lambda e: e.copy(out=zq[0:N, :], in_=ps[1][0:N, :]), r=["ps1"], w=["zq"])
        P.add("act", lambda e: e.activation(out=gate24[0:N, :], in_=ps[2][0:N, 0:24], func=AF.Tanh, scale=0.5), r=["ps2"], w=["gate24"])
        P.add("dve", lambda e: e.tensor_scalar(out=gate24[0:N, :], in0=gate24[0:N, :], scalar1=0.5, scalar2=0.5, op0=ALU.mult, op1=ALU.add),
              r=["gate24"], w=["gate24"])
        P.add("dve", lambda e: e.tensor_tensor(out=sqq[0:N, :], in0=zq[0:N, :], in1=zq[0:N, :], op=ALU.mult), r=["zq"], w=["sqq"])
        P.add("dve", lambda e: e.tensor_reduce(out=ssq[0:N, :], in_=sqq[0:N, :].rearrange("p (h d) -> p h d", h=8), axis=AX.X, op=ALU.add),
              r=["sqq"], w=["ssq"])
        rsqrt_pool(rq[0:N, :], ssq[0:N, :], 1.0 / 64, 8, "ssq", "rq")
        zq3 = zq[0:N, :].rearrange("p (h d) -> p h d", h=8)
        P.add("dve", lambda e: e.tensor_tensor(out=zq3, in0=zq3, in1=gq_bc[0:N, :].unsqueeze(1).to_broadcast([N, 8, 64]), op=ALU.mult),
              r=["zq", "gq_bc"], w=["zq"])
        P.add("dve", lambda e: e.tensor_tensor(out=zq3, in0=zq3, in1=rq[0:N, :].unsqueeze(2).to_broadcast([N, 8, 64]), op=ALU.mult),
              r=["zq", "rq"], w=["zq"])
        P.add("pool", lambda e: e.tensor_copy(out=qpb[0:N].rearrange("p g k d -> p k g d"), in_=zq[0:N, :].rearrange("p (k g d) -> p k g d", k=2, g=4)),
              r=["zq"], w=["qpb"])
        src = zq[0:N, :].rearrange("p (k g f d) -> p k g f d", k=2, g=4, f=2)
        dstv = qrb[0:N].rearrange("p g k (f d) -> p k g f d", f=2)
        rope_ops((src[:, :, :, 0, :], src[:, :, :, 1, :]), dstv[:, :, :, 0, :], dstv[:, :, :, 1, :], cs, 8, ["zq", csk], "qrb", N)

        def tr(e):
            for v, qx in enumerate((qpb, qrb)):
                for g in range(4):
                    ins = e.transpose(out=psb(3)[:, (v * 4 + g) * 128:(v * 4 + g) * 128 + N],
                                      in_=qx[0:N, g, :, :].rearrange("p k d -> p (k d)"), identity=identb[0:N, 0:N])
            return ins
        P.add("pe", tr, r=["qpb", "qrb", "identb"], w=["ps3"])
        P.add("act", lambda e: e.copy(out=QT[:, :, 0:N], in_=psb(3)[:, :].rearrange("p (a b) -> p a b", a=8)[:, :, 0:N]), r=["ps3"], w=["QT"])

    st_att = {"s": 0, "first7": True}

    def s_tile(N, h, var, KT_ap, kkeys, bias_list):
        hp = slice(64 * h, 64 * h + 64)
        b = 5 + (st_att["s"] % 2)
        pt = PT[st_att["s"] % 2]
        ptk = f"PT{st_att['s'] % 2}"
        st_att["s"] += 1

        def mm(e):
            last = len(bias_list) == 0
            ins = e.matmul(ps[b][:, 0:4 * N], lhsT=KT_ap, rhs=QT[hp, var * 4:var * 4 + 4, 0:N], start=True, stop=last)
            for bi, (l_ap, r_ap) in enumerate(bias_list):
                ins = e.matmul(ps[b][:, 0:4 * N], lhsT=l_ap, rhs=r_ap.unsqueeze(1).to_broadcast([128, 4, N]), start=False,
                               stop=(bi == len(bias_list) - 1))
            return ins
        P.add("pe", mm, r=["QT"] + list(kkeys), w=[f"ps{b}"])
        P.add("act", lambda e: e.activation(out=pt[:, 0:4 * N], in_=ps[b][:, 0:4 * N], func=AF.Exp, scale=SCALE), r=[f"ps{b}"], w=[ptk])
        return pt, ptk

    def pv(N, pt, ptk, V_ap, vkeys, first):
        P.add("pe", lambda e: e.matmul(ps[7][0:65, 0:4 * N], lhsT=V_ap, rhs=pt[:, 0:4 * N], start=first, stop=False, skip_group_check=True),
              r=[ptk] + list(vkeys), w=["ps7"])

    def attention(i, N=128):
        ld(mcmp_t[:], m_cmp[i, :, :], "mcmp_t", "mcmp_t")
        ld(madd_t[:], m_add[i, :, :], "madd_t", "madd_t")
        ld(mcT_t[:], m_cmpT[i, :, :], "mcT_t", "mcT_t", eng="pool")
        for h in range(2):
            hp = slice(64 * h, 64 * h + 64)
            def mmA(e, hp=hp):
                for g in range(4):
                    ins = e.matmul(ps[4][0:N, g * 128:(g + 1) * 128], lhsT=QT[hp, g, 0:N], rhs=kcT[hp, :], start=True, stop=True)
                return ins
            P.add("pe", mmA, r=["QT", "kcT"], w=["ps4"])
            P.add("dve", lambda e: e.tensor_tensor(out=sbias[0:N], in0=ps[4][0:N, :].rearrange("p (g b) -> p g b", g=4),
                                                  in1=mcmp_t[0:N, :].unsqueeze(1).to_broadcast([N, 4, 128]), op=ALU.add),
                  r=["ps4", "mcmp_t"], w=["sbias"])

            def expA(e):
                for g in range(4):
                    ins = e.activation(out=ecmp[0:N, g, :], in_=sbias[0:N, g, :], func=AF.Exp, scale=SCALE, accum_out=l4[0:N, g:g + 1])
                return ins
            P.add("act", expA, r=["sbias"], w=["ecmp", "l4"])

            P.add("dve", lambda e: e.tensor_scalar(out=rl4[0:N, :], in0=l4[0:N, :], scalar1=1e-30, scalar2=None, op0=ALU.max), r=["l4"], w=["rl4"])
            P.add("dve", lambda e: e.reciprocal(out=rl4[0:N, :], in_=rl4[0:N, :]), r=["rl4"], w=["rl4"])
            P.add("dve", lambda e: e.scalar_tensor_tensor(out=score[0:N, :], in0=ecmp[0:N, 0, :], scalar=rl4[0:N, 0:1], in1=madd_t[0:N, :], op0=ALU.mult, op1=ALU.add),
                  r=["ecmp", "rl4", "madd_t"], w=["score"])
            for g in range(1, 4):
                P.add("dve", lambda e, g=g: e.scalar_tensor_tensor(out=score[0:N, :], in0=ecmp[0:N, g, :], scalar=rl4[0:N, g:g + 1], in1=score[0:N, :], op0=ALU.mult, op1=ALU.add),
                      r=["ecmp", "rl4", "score"], w=["score"])
            P.add("dve", lambda e: e.max(out=mx8[0:N, 0:8], in_=score[0:N, :]), r=["score"], w=["mx8a"])
            P.add("dve", lambda e: e.match_replace(out=sc2[0:N, :], in_to_replace=mx8[0:N, 0:8], in_values=score[0:N, :], imm_value=-1e9),
                  r=["score", "mx8a"], w=["sc2"])
            P.add("dve", lambda e: e.max(out=mx8[0:N, 8:16], in_=sc2[0:N, :]), r=["sc2"], w=["mx8b"])
            P.add("dve", lambda e: e.tensor_scalar(out=selb[0:N, :], in0=score[0:N, :], scalar1=mx8[0:N, 15:16], scalar2=-BIG, op0=ALU.is_lt, op1=ALU.mult),
                  r=["score", "mx8b"], w=["selb"])
            P.add("pe", lambda e: e.transpose(out=psb(3)[:, 0:N], in_=selb[0:N, :], identity=identb[0:N, 0:N]), r=["selb", "identb"], w=["ps3"])
            P.add("act", lambda e: e.copy(out=selbT[:, 0:N], in_=psb(3)[:, 0:N]), r=["ps3"], w=["selbT"])

            seq = []
            seq.append((0, 0, kcT[hp, :], ["kcT"], [(identb[:, :], mcT_t[:, 0:N])], Vcmp[:, h, :], ["Vcmp"]))
            wt_ = []
            for d in (2, 1, 0):
                if i - d >= 0:
                    wt_.append(((i - d) % 3, {2: 1, 1: None, 0: 0}[d]))
            for d in (2, 1, 0):
                if i - d >= 0:
                    wt_.append((3 + (i - d) % 3, {2: 3, 1: None, 0: 2}[d]))
            for (w, mk) in wt_:
                bl = [] if mk is None else [(identb[:, :], masks[:, mk, 0:N])]
                seq.append((2, 1, KTwin[hp, w * 128:(w + 1) * 128], ["KTwin", "masks"], bl, Vwin[:, w, h, :], ["Vwin"]))
            st_ = [(j, 0 if j == i else None) for j in range(i + 1)] + [(32 + j, 2 if j == i else None) for j in range(i + 1)]
            for (sl, mk) in st_:
                bl = [(eall[:, sl * 128:(sl + 1) * 128], selbT[:, 0:N])]
                if mk is not None:
                    bl.append((identb[:, :], masks[:, mk, 0:N]))
                seq.append((1, 1, KTsel[hp, sl * 128:(sl + 1) * 128], ["KTsel", "eall", "selbT", "masks"], bl, Vsel[:, sl, h, :], ["Vsel"]))

            def flush(prev, nxt_br):
                br, pt_, ptk_, V_ap, vkeys, first = prev
                pv(N, pt_, ptk_, V_ap, vkeys, first)
                if nxt_br != br:
                    P.add("act", lambda e: e.copy(out=OTsb[0:65, 0:4 * N], in_=ps[7][0:65, 0:4 * N]), r=["ps7"], w=["OTsb"])

                    def trO(e):
                        for g in range(4):
                            ins = e.transpose(out=ps[4][0:N, g * 65:(g + 1) * 65], in_=OTsb[0:65, g * N:(g + 1) * N], identity=identf[0:65, 0:65])
                        return ins
                    P.add("pe", trO, r=["OTsb", "identf"], w=["ps4"])
                    P.add("act", lambda e, br=br: e.copy(out=Obr[0:N, br].rearrange("p g c -> p (g c)"), in_=ps[4][0:N, 0:260]), r=["ps4"], w=["Obr"])
            prev = None
            last_br = None
            for (br, var, KT_ap, kkeys, bl, V_ap, vkeys) in seq:
                pt_, ptk_ = s_tile(N, h, var, KT_ap, kkeys, bl)
                if prev is not None:
                    flush(prev, br)
                prev = (br, pt_, ptk_, V_ap, vkeys, br != last_br)
                last_br = br
            flush(prev, None)
            combine(N, h)
        P.add("pool", lambda e: e.tensor_copy(out=obf[0:N, :], in_=otok[0:N].rearrange("p h d -> p (h d)")), r=["otok"], w=["obf"])

    def combine(N, h):
        g3 = gate24[0:N, :].rearrange("p (k g b) -> p k b g", k=2, g=4, b=3)[:, h, :, :]

        P.add("dve", lambda e: e.tensor_scalar(out=l3[0:N], in0=Obr[0:N, :, :, 64], scalar1=1e-30, scalar2=None, op0=ALU.max), r=["Obr"], w=["l3"])
        P.add("dve", lambda e: e.reciprocal(out=l3[0:N], in_=l3[0:N]), r=["l3"], w=["l3"])
        P.add("dve", lambda e: e.tensor_tensor(out=coef[0:N], in0=l3[0:N], in1=g3, op=ALU.mult), r=["l3", "gate24"], w=["coef"])
        oh = otok[0:N, 4 * h:4 * h + 4, :]
        P.add("dve", lambda e: e.tensor_tensor(out=oh, in0=Obr[0:N, 0, :, 0:64], in1=coef[0:N, 0, :].unsqueeze(2).to_broadcast([N, 4, 64]), op=ALU.mult),
              r=["Obr", "coef"], w=["otok"])
        for br in (1, 2):
            P.add("dve", lambda e, br=br: e.tensor_tensor(out=otmp[0:N], in0=Obr[0:N, br, :, 0:64], in1=coef[0:N, br, :].unsqueeze(2).to_broadcast([N, 4, 64]), op=ALU.mult),
                  r=["Obr", "coef"], w=["otmp"])
            P.add("dve", lambda e: e.tensor_tensor(out=oh, in0=oh, in1=otmp[0:N], op=ALU.add), r=["otok", "otmp"], w=["otok"])

    ptb = sbA("ptb", [128, 512], I32)
    ptf = ptb[:].bitcast(F32)
    idx = ptb
    riota = sbA("riota", [128, 1], I32)
    riof = sbA("riof", [128, 1])
    pgk = [sbA(f"pgk{i}", [128, 128]) for i in range(2)]
    pgv = [sbA(f"pgv{i}", [128, 128]) for i in range(2)]
    pgkb = [sbA(f"pgkb{i}", [128, 128], BF) for i in range(2)]
    pgvb = [sbA(f"pgvb{i}", [128, 2, 65], BF) for i in range(2)]
    KTp = [sbA(f"KTp{i}", [128, 128], BF) for i in range(2)]
    summTs = sbA("summTs", [128, 2, 256], BF)
    kcTs = sbA("kcTs", [128, 256], BF)
    Vcs = sbA("Vcs", [128, 2, 2, 65], BF)
    madds = sbA("madds", [8, 256])
    mwin = sbA("mwin", [128, 8], BF)
    mnew = sbA("mnew", [32, 32], BF)
    ecs = xbuf[1][0:8, :].rearrange("p (a b) -> p a b", a=4)
    scs = sbA("scs", [8, 256])
    sc2s = sbA("sc2s", [8, 256])
    selbs = sbA("selbs", [8, 256], BF)
    knew = sbA("knew", [32, 2, 128], BF)
    KTnew = sbA("KTnew", [128, 2, 32], BF)
    Vnew = sbA("Vnew", [32, 2, 2, 65], BF)
    for i_ in range(2):
        P.add("pool", lambda e, i_=i_: e.memset(pgvb[i_][:].rearrange("p a b -> p (a b)"), 1.0), w=[f"pgvb{i_}"])
    P.add("pool", lambda e: e.memset(Vcs[:].rearrange("p a b c -> p (a b c)"), 1.0), w=["Vcs"])
    P.add("pool", lambda e: e.memset(Vnew[:].rearrange("p a b c -> p (a b c)"), 1.0), w=["Vnew"])
    ld(ptb[:], pt_d[0:1, :].to_broadcast([128, 512]), "ptb", "ptb")
    ld(madds[:], madds_d[0:1, :].to_broadcast([8, 256]), "madds", "madds")
    ld(mwin[:], mwin_d[:, :], "mwin", "mwin", eng="pool")
    ld(mnew[:], mnew_d[:, :], "mnew", "mnew", eng="pool")
    P.add("pool", lambda e: e.iota(out=riota[:], pattern=[[0, 1]], base=0, channel_multiplier=1), w=["riota"])
    P.add("pool", lambda e: e.tensor_copy(out=riof[:], in_=riota[:]), r=["riota"], w=["riof"])
    P.add("pool", lambda e: e.tensor_copy(out=ptf, in_=ptb[:]), r=["ptb"], w=["ptb"])
    P.add("pool", lambda e: e.tensor_scalar(out=ptf, in0=ptf, scalar1=128.0, scalar2=riof[:, 0:1], op0=ALU.mult, op1=ALU.add), r=["ptb", "riof"], w=["ptb"])
    P.add("pool", lambda e: e.tensor_copy(out=ptb[:], in_=ptf), r=["ptb"], w=["idx"])

    pg_state = {"n": 0}

    def gather(dst, dkey, cache, col):
        P.add("pool", lambda e: e.indirect_dma_start(out=dst[:, :], out_offset=None, in_=cache[:, :],
                                                    in_offset=bass.IndirectOffsetOnAxis(ap=idx[:, col:col + 1], axis=0)),
              r=["idx"], w=[dkey], sem=dkey)

    wkb_t = sbA("wkb_t", [128, 4, 128], BF)
    wvb_t = sbA("wvb_t", [128, 4, 2, 65], BF)
    wKT = sbA("wKT", [128, 4, 128], BF)
    ObrS1 = sbA("ObrS1", [8, 3, 4, 65])
    gate8 = sbA("gate8", [8, 4, 24])
    obf8 = sbA("obf8", [8, 512], BF)
    selbT2 = sbA("selbT2", [128, 2, 2, 8], BF)
    P.add("pool", lambda e: e.memset(wvb_t[:].rearrange("p a b c -> p (a b c)"), 1.0), w=["wvb_t"])

    def sample_attention():
        kvo4 = kvout[0][0:32].rearrange("p (k v c) -> p k v c", k=3, v=2, c=128)
        P.add("pool", lambda e: e.tensor_copy(out=knew[:], in_=kvo4[:, 1:3, 0, :]), r=["kvout0"], w=["knew"])
        P.add("pool", lambda e: e.tensor_copy(out=Vnew[:, :, :, 0:64], in_=kvo4[:, 1:3, 1, :].rearrange("p k (h d) -> p k h d", h=2)), r=["kvout0"], w=["Vnew"])

        def trn(e):
            e.transpose(out=psb(3)[:, 0:32], in_=knew[:, 0, :], identity=identb[0:32, 0:32])
            return e.transpose(out=psb(3)[:, 32:64], in_=knew[:, 1, :], identity=identb[0:32, 0:32])
        P.add("pe", trn, r=["knew", "identb"], w=["ps3"])
        P.add("act", lambda e: e.copy(out=KTnew[:].rearrange("p a b -> p (a b)"), in_=psb(3)[:, 0:64]), r=["ps3"], w=["KTnew"])
        for sq in range(4):
            P.add("sp", lambda e, sq=sq: e.dma_start(out=gate8[:, sq, :], in_=gate24[8 * sq:8 * sq + 8, :]), r=["gate24"], w=["gate8"], sem="gate8")
        for sq in range(4):
            cs0 = 8 * sq

            def s_tile_s(h, var, KT_ap, nkeys, kkeys, bias_list, cs0=cs0):
                hp = slice(64 * h, 64 * h + 64)
                b = 5 + (st_att["s"] % 2)
                pt = PT[st_att["s"] % 2]
                ptk = f"PT{st_att['s'] % 2}"
                st_att["s"] += 1

                def mm(e):
                    last = len(bias_list) == 0
                    ins = e.matmul(ps[b][0:nkeys, 0:32], lhsT=KT_ap, rhs=QT[hp, var * 4:var * 4 + 4, cs0:cs0 + 8], start=True, stop=last)
                    for bi, (l_ap, r_ap) in enumerate(bias_list):
                        ins = e.matmul(ps[b][0:nkeys, 0:32], lhsT=l_ap, rhs=r_ap.unsqueeze(1).to_broadcast([r_ap.shape[0], 4, 8]), start=False,
                                       stop=(bi == len(bias_list) - 1))
                    return ins
                P.add("pe", mm, r=["QT"] + list(kkeys), w=[f"ps{b}"])
                P.add("act", lambda e: e.activation(out=pt[0:nkeys, 0:32], in_=ps[b][0:nkeys, 0:32], func=AF.Exp, scale=SCALE), r=[f"ps{b}"], w=[ptk])
                return pt, ptk

            def pv_s(pt, ptk, nkeys, V_ap, vkeys, first, bank=7):
                def mm(e):
                    if first:
                        e.matmul(ps[bank][0:8, 0:260], lhsT=zeros_bf[:, 0:8], rhs=zeros_bf[:, 0:260], start=True, stop=False, skip_group_check=True)
                    for g in range(4):
                        ins = e.matmul(ps[bank][0:8, g * 65:(g + 1) * 65], lhsT=pt[0:nkeys, g * 8:(g + 1) * 8], rhs=V_ap,
                                       start=False, stop=False, skip_group_check=True)
                    return ins
                P.add("pe", mm, r=[ptk, "zeros_bf"] + list(vkeys), w=[f"ps{bank}"])

            pm1k = pm1[:, 0, :].unsqueeze(1).to_broadcast([128, 2, 64])
            pm1v = pm1[:, 1, :].unsqueeze(1).to_broadcast([128, 2, 64])
            for p in range(128):
                col = sq * 128 + p
                ib_ = pg_state["n"] % 2
                pg_state["n"] += 1
                gather(pgk[ib_], f"pgk{ib_}", ccmpk, col)
                gather(pgv[ib_], f"pgv{ib_}", ccmpv, col)
                P.add("dve", lambda e, ib_=ib_: e.tensor_tensor(out=cmod[:, 0, :].rearrange("p (h d) -> p h d", h=2), in0=pgk[ib_][:, :].rearrange("p (h d) -> p h d", h=2),
                                                             in1=pm1k, op=ALU.mult), r=[f"pgk{ib_}", "pm1"], w=["cmod"])
                P.add("dve", lambda e, ib_=ib_: e.tensor_tensor(out=cmod[:, 1, :].rearrange("p (h d) -> p h d", h=2), in0=pgv[ib_][:, :].rearrange("p (h d) -> p h d", h=2),
                                                             in1=pm1v, op=ALU.mult), r=[f"pgv{ib_}", "pm1"], w=["cmod"])

                def mmc(e):
                    e.matmul(ps[4][:, 0:2], lhsT=cmod[:, 0, :], rhs=sel2[:, :], start=True, stop=True)
                    return e.matmul(ps[4][:, 2:4], lhsT=cmod[:, 1, :], rhs=sel2[:, :], start=True, stop=True)
                P.add("pe", mmc, r=["cmod", "sel2"], w=["ps4"])
                P.add("act", lambda e, p=p: e.copy(out=summTs[:, :, 2 * p:2 * p + 2], in_=ps[4][:, 0:4].rearrange("p (a b) -> p a b", a=2)),
                      r=["ps4"], w=["summTs"])

            def mmk(e):
                e.matmul(ps[4][:, 0:256], lhsT=wkbd[:], rhs=summTs[:, 0, :], start=True, stop=True)
                e.matmul(ps[3][:, 0:128], lhsT=summTs[:, 1, 0:128], rhs=wvbd[:], start=True, stop=True)
                return e.matmul(ps[3][:, 128:256], lhsT=summTs[:, 1, 128:256], rhs=wvbd[:], start=True, stop=True)
            P.add("pe", mmk, r=["summTs", "wkbd", "wvbd"], w=["ps4", "ps3"])
            P.add("act", lambda e: e.copy(out=kcTs[:], in_=ps[4][:, 0:256]), r=["ps4"], w=["kcTs"])
            P.add("act", lambda e: e.copy(out=Vcs[:, :, :, 0:64], in_=ps[3][:, 0:256].rearrange("p (c h d) -> p c h d", c=2, h=2)), r=["ps3"], w=["Vcs"])
            ld(wkb_t[:], cwk_d[sq, :, :].rearrange("(w p) c -> p w c", p=128), "wkb_t", "wkb_t", eng="pool")
            for w_ in range(4):
                ld(wvb_t[:, w_, :, 0:64], cwv_d[sq, w_ * 128:(w_ + 1) * 128, :].rearrange("p (h d) -> p h d", h=2), "wvb_t", "wvb_t", eng="pool")

            def trw(e):
                for w in range(4):
                    ins = e.transpose(out=psb(3)[:, 128 * w:128 * w + 128], in_=wkb_t[:, w, :], identity=identb[:])
                return ins
            P.add("pe", trw, r=["wkb_t", "identb"], w=["ps3"])
            P.add("act", lambda e: e.copy(out=wKT[:].rearrange("p a b -> p (a b)"), in_=psb(3)[:, 0:512]), r=["ps3"], w=["wKT"])

            for h in range(2):
                hp = slice(64 * h, 64 * h + 64)
                def mmA(e, hp=hp, cs0=cs0):
                    for g in range(4):
                        ins = e.matmul(ps[4 - g // 2][0:8, (g % 2) * 256:(g % 2) * 256 + 256], lhsT=QT[hp, g, cs0:cs0 + 8], rhs=kcTs[hp, :], start=True, stop=True)
                    return ins
                P.add("pe", mmA, r=["QT", "kcTs"], w=["ps4", "ps3"])

                def expA(e):
                    for g in range(4):
                        ins = e.activation(out=ecs[:, g, :], in_=ps[4 - g // 2][0:8, (g % 2) * 256:(g % 2) * 256 + 256], func=AF.Exp, scale=SCALE,
                                           accum_out=l4[0:8, g:g + 1])
                    return ins
                P.add("act", expA, r=["ps4", "ps3"], w=["xbuf1", "l4"])
                P.add("dve", lambda e: e.tensor_scalar(out=rl4[0:8, :], in0=l4[0:8, :], scalar1=1e-30, scalar2=None, op0=ALU.max), r=["l4"], w=["rl4"])
                P.add("dve", lambda e: e.reciprocal(out=rl4[0:8, :], in_=rl4[0:8, :]), r=["rl4"], w=["rl4"])
                P.add("dve", lambda e: e.scalar_tensor_tensor(out=scs[:, :], in0=ecs[:, 0, :], scalar=rl4[0:8, 0:1], in1=madds[:, :], op0=ALU.mult, op1=ALU.add),
                      r=["xbuf1", "rl4", "madds"], w=["scs"])
                for g in range(1, 4):
                    P.add("dve", lambda e, g=g: e.scalar_tensor_tensor(out=scs[:, :], in0=ecs[:, g, :], scalar=rl4[0:8, g:g + 1], in1=scs[:, :], op0=ALU.mult, op1=ALU.add),
                          r=["xbuf1", "rl4", "scs"], w=["scs"])
                P.add("dve", lambda e: e.max(out=mx8[0:8, 0:8], in_=scs[:, :]), r=["scs"], w=["mx8a"])
                P.add("dve", lambda e: e.match_replace(out=sc2s[:, :], in_to_replace=mx8[0:8, 0:8], in_values=scs[:, :], imm_value=-1e9), r=["scs", "mx8a"], w=["sc2s"])
                P.add("dve", lambda e: e.max(out=mx8[0:8, 8:16], in_=sc2s[:, :]), r=["sc2s"], w=["mx8b"])
                P.add("dve", lambda e: e.tensor_scalar(out=selbs[:, :], in0=scs[:, :], scalar1=mx8[0:8, 14:15], scalar2=-BIG, op0=ALU.is_lt, op1=ALU.mult),
                      r=["scs", "mx8b"], w=["selbs"])

                def trsb(e):
                    e.transpose(out=psb(3)[:, 0:8], in_=selbs[:, 0:128], identity=identb[0:8, 0:8])
                    return e.transpose(out=psb(3)[:, 8:16], in_=selbs[:, 128:256], identity=identb[0:8, 0:8])
                P.add("pe", trsb, r=["selbs", "identb"], w=["ps3"])
                P.add("act", lambda e, h=h: e.copy(out=selbT2[:, h].rearrange("p a b -> p (a b)"), in_=psb(3)[:, 0:16]), r=["ps3"], w=["selbT2"])
                for c in range(2):
                    pt, ptk = s_tile_s(h, 0, kcTs[hp, c * 128:(c + 1) * 128], 128, ["kcTs"], [])
                    pv_s(pt, ptk, 128, Vcs[:, c, h, :], ["Vcs"], c == 0)
                P.add("act", lambda e, h=h: e.copy(out=(Obr[0:8] if h == 0 else ObrS1[:])[:, 0].rearrange("p g c -> p (g c)"), in_=ps[7][0:8, 0:260]), r=["ps7"], w=["ObrS", "Obr"])
                for w in range(4):
                    bl = [(identb[:, :], mwin[:, :])] if w == 0 else []
                    pt, ptk = s_tile_s(h, 1, wKT[hp, w, :], 128, ["wKT", "mwin"], bl)
                    pv_s(pt, ptk, 128, wvb_t[:, w, h, :], ["wvb_t"], w == 0)
                pt, ptk = s_tile_s(h, 1, KTnew[hp, 1, :], 32, ["KTnew", "mnew"], [(identb[0:32, 0:32], mnew[:, cs0:cs0 + 8])])
                pv_s(pt, ptk, 32, Vnew[:, 1, h, :], ["Vnew"], False)
                P.add("act", lambda e, h=h: e.copy(out=(Obr[0:8] if h == 0 else ObrS1[:])[:, 2].rearrange("p g c -> p (g c)"), in_=ps[7][0:8, 0:260]), r=["ps7"], w=["ObrS", "Obr"])
            for p in range(128):
                col = sq * 128 + p
                ib_ = pg_state["n"] % 2
                pg_state["n"] += 1
                gather(pgk[ib_], f"pgk{ib_}", cselk, col)
                gather(pgv[ib_], f"pgv{ib_}", cselv, col)
                P.add("dve", lambda e, ib_=ib_: e.tensor_copy(out=pgkb[ib_][:], in_=pgk[ib_][:]), r=[f"pgk{ib_}"], w=[f"pgkb{ib_}"])
                P.add("pe", lambda e, ib_=ib_: e.transpose(out=psb(3)[:, 256 * ib_:256 * ib_ + 128], in_=pgkb[ib_][:], identity=identb[:]),
                      r=[f"pgkb{ib_}", "identb"], w=["ps3"])
                P.add("act", lambda e, ib_=ib_: e.copy(out=KTp[ib_][:], in_=psb(3)[:, 256 * ib_:256 * ib_ + 128]), r=["ps3"], w=[f"KTp{ib_}"])
                P.add("dve", lambda e, ib_=ib_: e.tensor_copy(out=pgvb[ib_][:, :, 0:64], in_=pgv[ib_][:, :].rearrange("p (h d) -> p h d", h=2)),
                      r=[f"pgv{ib_}"], w=[f"pgvb{ib_}"])
                for h in range(2):
                    hp = slice(64 * h, 64 * h + 64)
                    bl = [(eall[:, (p % 64) * 128:(p % 64 + 1) * 128], selbT2[:, h, p // 64, :])]
                    pt, ptk = s_tile_s(h, 1, KTp[ib_][hp, :], 128, [f"KTp{ib_}", "eall", "selbT2"], bl)
                    pv_s(pt, ptk, 128, pgvb[ib_][:, h, :], [f"pgvb{ib_}"], p == 0, bank=(7 if h == 0 else 2))
            for h in range(2):
                hp = slice(64 * h, 64 * h + 64)
                pt, ptk = s_tile_s(h, 1, KTnew[hp, 0, :], 32, ["KTnew", "mnew"], [(identb[0:32, 0:32], mnew[:, cs0:cs0 + 8])])
                pv_s(pt, ptk, 32, Vnew[:, 0, h, :], ["Vnew"], False, bank=(7 if h == 0 else 2))
                bk = 7 if h == 0 else 2
                P.add("act", lambda e, h=h, bk=bk: e.copy(out=(Obr[0:8] if h == 0 else ObrS1[:])[:, 1].rearrange("p g c -> p (g c)"), in_=ps[bk][0:8, 0:260]), r=[f"ps{bk}"], w=["ObrS", "Obr"])
            for h in range(2):
                combine_s(sq, h)
            P.add("sp", lambda e, sq=sq: e.dma_start(out=o_d[NT, 8 * sq:8 * sq + 8, :], in_=obf8[:, :]), r=["obf8"], w=[f"o_d{NT}"], sem="st_obf8")

    def combine_s(sq, h):
        g3 = gate8[:, sq, :].rearrange("p (k g b) -> p k b g", k=2, g=4, b=3)[:, h, :, :]
        Ob = Obr[0:8] if h == 0 else ObrS1[:]
        P.add("dve", lambda e: e.tensor_scalar(out=l3[0:8], in0=Ob[:, :, :, 64], scalar1=1e-30, scalar2=None, op0=ALU.max), r=["ObrS", "Obr"], w=["l3"])
        P.add("dve", lambda e: e.reciprocal(out=l3[0:8], in_=l3[0:8]), r=["l3"], w=["l3"])
        P.add("dve", lambda e: e.tensor_tensor(out=coef[0:8], in0=l3[0:8], in1=g3, op=ALU.mult), r=["l3", "gate8"], w=["coef"])
        oh = otok[0:8, 4 * h:4 * h + 4, :]
        P.add("dve", lambda e: e.tensor_tensor(out=oh, in0=Ob[:, 0, :, 0:64], in1=coef[0:8, 0, :].unsqueeze(2).to_broadcast([8, 4, 64]), op=ALU.mult),
              r=["ObrS", "Obr", "coef"], w=["otok"])
        for br in (1, 2):
            P.add("dve", lambda e, br=br: e.tensor_tensor(out=otmp[0:8], in0=Ob[:, br, :, 0:64], in1=coef[0:8, br, :].unsqueeze(2).to_broadcast([8, 4, 64]), op=ALU.mult),
                  r=["ObrS", "Obr", "coef"], w=["otmp"])
            P.add("dve", lambda e: e.tensor_tensor(out=oh, in0=oh, in1=otmp[0:8], op=ALU.add), r=["otok", "otmp"], w=["otok"])
        if h == 1:
            P.add("pool", lambda e: e.tensor_copy(out=obf8[:, :], in_=otok[0:8].rearrange("p h d -> p (h d)")), r=["otok"], w=["obf8"])

    f1(xs_d[:, :], cs_s[:, :], 32, A1s, "A1s", selSb, "selSb", None, None, skv_o[:, :], 0,
       glu_dst=[(sconv_o[sq, 22:30, :], 8 * sq, 8 * sq + 8) for sq in range(4)])
    for sq in range(4):
        ld(bnc_c[0:22, :], sconv_d[sq, 8:30, :], "ecmp", "bnc_c")
        P.add("sp", lambda e, sq=sq: e.dma_start(out=sconv_o[sq, 0:22, :], in_=bnc_c[0:22, :]), r=["ecmp"], w=[], sem="st_bc")
        for kv in range(2):
            src_c = (cwk_d, cwv_d)[kv]
            ld(bnc_w[0:126, :, :], src_c[sq, 8:512, :].rearrange("(p a) c -> p a c", a=4), "sbias", "bnc_w")
            P.add("sp", lambda e, sq=sq, kv=kv: e.dma_start(out=swin_o[sq, kv, 0:504, :].rearrange("(p a) c -> p a c", a=4), in_=bnc_w[0:126, :, :]),
                  r=["sbias"], w=[], sem="st_bw")
            P.add("sp", lambda e, sq=sq, kv=kv: e.dma_start(out=swin_o[sq, kv, 504:512, :], in_=kvout[0][8 * sq:8 * sq + 8, 512 + 128 * kv:640 + 128 * kv]),
                  r=["kvout0"], w=[], sem="st_kvout0")

    if do_sample:
        qfront(32, selSb, "selSb", csb[(state["xi"] - 1) % 2], f"csb{(state['xi'] - 1) % 2}")
        sample_attention()

    for i in range(n_own):
        f1(xt[i, :, :], cs_t[i, :, :], 128, A1, "A1", selPb, "selPb", 32 + i, 3 + (i % 3), None, 1)
        last = (i == NT - 1)
        ib = f1(xo[i, :, :], cs_o[i, :, :], 128, A1, "A1", selPb, "selPb", i, i % 3, kv_o[i, :, :], i,
                glu_dst=[(conv_o[:, :], 98, 128)] if last else None)
        update_cmp()
        qfront(128, selPb, "selPb", csb[ib], f"csb{ib}")
        attention(i)
        P.add("sp", lambda e, i=i: e.dma_start(out=o_d[i, :, :], in_=obf[:, :]), r=["obf"], w=[f"o_d{i}"], sem="st_obf")

    esA.close()
    P.barrier(lambda e: e.memset(dummy[:], 2.0))

    esB = ExitStack()

    def sbB(name, shape, dt=F32):
        return esB.enter_context(nc.sbuf_tensor("sB_" + name, list(shape), dt))
    w_gluB = sbB("w_gluB", [128, 8, 1024], BF)
    for kc in range(8):
        ld(w_gluB[:, kc, :], w_in_v[:, kc, 0:1024], "w_gluB", "wb4", eng="pool")
    w_mg_bf = sbB("w_mg_bf", [128, 8, 2048], BF)
    w_pw2_bf = sbB("w_pw2_bf", [128, 4, D], BF)
    w_nso_bf = sbB("w_nso_bf", [128, 4, D], BF)
    w_out_bf = sbB("w_out_bf", [128, 8, D], BF)
    for kc in range(8):
        ld(w_mg_bf[:, kc, :], w_in_v[:, kc, OFF_MG:D_IN], "w_mg_bf", "wb0", eng="pool")
        ld(w_out_bf[:, kc, :], w_out.rearrange("(kc p) n -> p kc n", p=128)[:, kc, :], "w_out_bf", "wb3", eng="pool")
    for kc in range(4):
        ld(w_pw2_bf[:, kc, :], w_pw2.rearrange("(kc p) n -> p kc n", p=128)[:, kc, :], "w_pw2_bf", "wb1", eng="pool")
        ld(w_nso_bf[:, kc, :], w_nsa_o.rearrange("(kc p) n -> p kc n", p=128)[:, kc, :], "w_nso_bf", "wb2", eng="pool")
    diag = sbB("diag", [128, 4, CONV_K, 128], BF)
    ones_bf = sbB("ones_bf", [128, 128], BF)
    P.add("pool", lambda e: e.memset(ones_bf[:], 1.0), w=["ones_bf"])
    for c in range(4):
        for k in range(CONV_K):
            P.add("pool", lambda e, c=c, k=k: e.tensor_scalar(out=diag[:, c, k, :], in0=identb[:, :], scalar1=wdwT[:, c, k:k + 1], scalar2=1.0,
                                                            op0=ALU.mult, op1=ALU.mult), r=["identb", "wdwT"], w=["diag"])
    xhb = sbB("xhb", [32, D])
    ssh = sbB("ssh", [32, 1])
    rrh = sbB("rrh", [32, 1])
    hbfh = sbB("hbfh", [32, D], BF)
    hTh = sbB("hTh", [128, 8, 32], BF)
    sgh = sbB("sgh", [128, 128])
    uext = sbB("uext", [128, 4, 160], BF)
    sg = sbB("sg", [128, 512])
    ycv = sbB("ycv", [128, 4, 128])
    ybf = sbB("ybf", [128, 4, 128], BF)
    ysq = sbB("ysq", [128, 4, 128], BF)
    mu = sbB("mu", [128, 128])
    msq = sbB("msq", [128, 128])
    rstd = sbB("rstd", [128, 128])
    tn = sbB("tn", [128, 4, 128])
    zf = sbB("zf", [128, 4, 128])
    ysl = sbB("ysl", [128, 4, 128], BF)
    ob = sbB("ob", [128, 512], BF)
    oT = sbB("oT", [128, 4, 128], BF)
    gcn = sbB("gcn", [128, 8, 128])
    m1 = sbB("m1", [128, 8, 128])
    mT = sbB("mT", [128, 8, 128], BF)
    x1t = sbB("x1t", [128, D])
    sct = utok

    def featproj(bank0, nchunk, N, wt, wk, wcol0, zcol0, selb_, selk, rhsT, rkey, nk=8, bias=True):
        for b0 in range(0, nchunk, 4):
            bank = bank0 + b0 // 4

            def mm(e, b0=b0, bank=bank):
                for c in range(b0, min(b0 + 4, nchunk)):
                    out = ps[bank][:, (c % 4) * N:(c % 4 + 1) * N]
                    for kc in range(nk):
                        ins = e.matmul(out, lhsT=wt[:, kc, wcol0 + c * 128:wcol0 + (c + 1) * 128], rhs=rhsT[:, kc, 0:N], start=(kc == 0),
                                       stop=(kc == nk - 1 and not bias))
                    if bias:
                        ins = e.matmul(out, lhsT=zb1[:, zcol0 + c * 128:zcol0 + (c + 1) * 128], rhs=selb_[:, 0:N], start=False, stop=True)
                return ins
            P.add("pe", mm, r=[wk, rkey, "zb1", selk], w=[f"ps{bank}"])

    def sigmoid_from(dst_ap, src_ap, rk_, wk_):
        P.add("act", lambda e: e.activation(out=dst_ap, in_=src_ap, func=AF.Tanh, scale=0.5), r=[rk_], w=[wk_])
        P.add("dve", lambda e: e.tensor_scalar(out=dst_ap, in0=dst_ap, scalar1=0.5, scalar2=0.5, op0=ALU.mult, op1=ALU.add), r=[wk_], w=[wk_])

    def passB(idx, N, xsrc, At, Ak, Gt, Gk, selb_, selk, segs, halo_kind, o_src, x1_dst):
        ib = front_norm(xsrc, cs_o[0, 0:N, :], N, At, Ak)
        xb = xbuf[ib]
        xk = f"xbuf{ib}"
        nseg = len(segs)
        L = segs[0][1]
        uv = uext[:, :, 0:nseg * (30 + L)].rearrange("p c (s t) -> p c s t", s=nseg)
        if halo_kind == "x":
            ld(xhb[:], xh[idx, :, :], "xhb", "xhb")
            P.add("act", lambda e: e.activation(out=junk[0:32, :], in_=xhb[:], func=AF.Square, accum_out=ssh[:]), r=["xhb"], w=["junk", "ssh"])
            rsqrt_pool(rrh[:], ssh[:], 1.0 / D, 1, "ssh", "rrh")
            P.add("dve", lambda e: e.scalar_tensor_tensor(out=hbfh[:], in0=xhb[:], scalar=rrh[:, 0:1], in1=At[0:32, :], op0=ALU.mult, op1=ALU.mult),
                  r=["xhb", "rrh", Ak], w=["hbfh"])

            def trh(e):
                for kc in range(8):
                    ins = e.transpose(out=psb(4)[:, kc * 32:(kc + 1) * 32], in_=hbfh[:, kc * 128:(kc + 1) * 128], identity=identb[0:32, 0:32])
                return ins
            P.add("pe", trh, r=["hbfh", "identb"], w=["ps4"])
            P.add("act", lambda e: e.copy(out=hTh[:].rearrange("p a b -> p (a b)"), in_=psb(4)[:, 0:256]), r=["ps4"], w=["hTh"])
            featproj(3, 4, 32, w_gluB, "w_gluB", 0, 0, selb_, selk, hTh, "hTh")
            featproj(4, 4, 32, w_gluB, "w_gluB", 512, 512, selb_, selk, hTh, "hTh")
            sigmoid_from(sgh[:, :], ps[4][:, 0:128], "ps4", "sgh")
            P.add("dve", lambda e: e.tensor_tensor(out=sgh[:, :], in0=sgh[:, :], in1=ps[3][:, 0:128], op=ALU.mult), r=["sgh", "ps3"], w=["sgh"])
            P.add("dve", lambda e: e.tensor_scalar(out=uv[:, :, 0, 0:30], in0=sgh[:, :].rearrange("p (c t) -> p c t", c=4)[:, :, 2:32],
                                                  scalar1=hv_sb[:, idx:idx + 1], scalar2=None, op0=ALU.mult), r=["sgh", "hv"], w=["uext"])
        else:
            for sq in range(nseg):
                ld(sct[0:30, :], sconv_d[sq, :, :], "utok", "utok")

                def trs(e):
                    for c in range(4):
                        ins = e.transpose(out=ps[4][:, c * 32:c * 32 + 30], in_=sct[0:30, c * 128:(c + 1) * 128], identity=identf[0:30, 0:30])
                    return ins
                P.add("pe", trs, r=["utok", "identf"], w=["ps4"])
                P.add("act", lambda e, sq=sq: e.copy(out=uv[:, :, sq, 0:30], in_=ps[4][:, 0:128].rearrange("p (c t) -> p c t", c=4)[:, :, 0:30]),
                      r=["ps4"], w=["uext"])
        featproj(1, 4, N, w_gluB, "w_gluB", 0, 0, selb_, selk, hT, "hT")
        featproj(2, 4, N, w_gluB, "w_gluB", 512, 512, selb_, selk, hT, "hT")
        sigmoid_from(sg[:, 0:4 * N], ps[2][:, 0:4 * N], "ps2", "sg")
        for sq, (t0, Ls) in enumerate(segs):
            P.add("dve", lambda e, sq=sq, t0=t0, Ls=Ls: e.tensor_tensor(
                out=uv[:, :, sq, 30:30 + Ls], in0=sg[:, 0:4 * N].rearrange("p (c t) -> p c t", c=4)[:, :, t0:t0 + Ls],
                in1=ps[1][:, 0:4 * N].rearrange("p (c t) -> p c t", c=4)[:, :, t0:t0 + Ls], op=ALU.mult), r=["sg", "ps1"], w=["uext"])

        def conv(e):
            for c in range(4):
                for sq, (t0, Ls) in enumerate(segs):
                    for k in range(CONV_K):
                        ins = e.matmul(ps[3][:, c * N + t0:c * N + t0 + Ls], lhsT=diag[:, c, k, :], rhs=uv[:, c, sq, k:k + Ls],
                                       start=(k == 0), stop=(k == CONV_K - 1))
            return ins
        P.add("pe", conv, r=["diag", "uext"], w=["ps3"])
        for c in range(4):
            P.add("act", lambda e, c=c: e.activation(out=ycv[:, c, 0:N], in_=ps[3][:, c * N:(c + 1) * N], func=AF.Identity, bias=cvec[:, 0, c:c + 1]),
                  r=["ps3", "cvec"], w=["ycv"])
        P.add("pool", lambda e: e.tensor_copy(out=ybf[:, :, 0:N], in_=ycv[:, :, 0:N]), r=["ycv"], w=["ybf"])
        P.add("act", lambda e: e.activation(out=ysq[:, :, 0:N], in_=ycv[:, :, 0:N], func=AF.Square), r=["ycv"], w=["ysq"])

        def stats(e):
            for c in range(4):
                e.matmul(ps[4][:, 0:N], lhsT=ones_bf[:, :], rhs=ybf[:, c, 0:N], start=(c == 0), stop=(c == 3))
            for c in range(4):
                ins = e.matmul(ps[4][:, N:2 * N], lhsT=ones_bf[:, :], rhs=ysq[:, c, 0:N], start=(c == 0), stop=(c == 3))
            return ins
        P.add("pe", stats, r=["ones_bf", "ybf", "ysq"], w=["ps4"])
        P.add("act", lambda e: e.activation(out=mu[:, 0:N], in_=ps[4][:, 0:N], func=AF.Copy, scale=1.0 / 512), r=["ps4"], w=["mu"])
        P.add("act", lambda e: e.activation(out=msq[:, 0:N], in_=ps[4][:, N:2 * N], func=AF.Copy, scale=1.0 / 512), r=["ps4"], w=["msq"])
        P.add("dve", lambda e: e.tensor_tensor(out=rstd[:, 0:N], in0=mu[:, 0:N], in1=mu[:, 0:N], op=ALU.mult), r=["mu"], w=["rstd"])
        P.add("dve", lambda e: e.tensor_tensor(out=rstd[:, 0:N], in0=msq[:, 0:N], in1=rstd[:, 0:N], op=ALU.subtract), r=["msq", "rstd"], w=["rstd"])
        rsqrt_pool(rstd[:, 0:N], rstd[:, 0:N], 1.0, N, "rstd", "rstd")
        P.add("dve", lambda e: e.tensor_tensor(out=tn[:, :, 0:N], in0=ycv[:, :, 0:N], in1=mu[:, 0:N].unsqueeze(1).to_broadcast([128, 4, N]), op=ALU.subtract),
              r=["ycv", "mu"], w=["tn"])
        P.add("dve", lambda e: e.tensor_tensor(out=tn[:, :, 0:N], in0=tn[:, :, 0:N], in1=rstd[:, 0:N].unsqueeze(1).to_broadcast([128, 4, N]), op=ALU.mult),
              r=["tn", "rstd"], w=["tn"])
        for c in range(4):
            P.add("act", lambda e, c=c: e.activation(out=zf[:, c, 0:N], in_=tn[:, c, 0:N], func=AF.Identity, scale=cvec[:, 1, c:c + 1], bias=cvec[:, 2, c:c + 1]),
                  r=["tn", "cvec"], w=["zf"])
        sigmoid_from(tn[:, :, 0:N], zf[:, :, 0:N], "zf", "tn")
        P.add("dve", lambda e: e.tensor_tensor(out=ysl[:, :, 0:N], in0=tn[:, :, 0:N], in1=zf[:, :, 0:N], op=ALU.mult), r=["tn", "zf"], w=["ysl"])
        featproj(5, 8, N, w_pw2_bf, "w_pw2_bf", 0, 0, selb_, selk, ysl, "ysl", nk=4, bias=False)
        featproj(1, 8, N, w_mg_bf, "w_mg_bf", 0, OFF_MG, selb_, selk, hT, "hT")
        for hb in range(2):
            sigmoid_from(gcn[:, 4 * hb:4 * hb + 4, 0:N], ps[1 + hb][:, 0:4 * N].rearrange("p (c t) -> p c t", c=4), f"ps{1 + hb}", "gcn")
            P.add("dve", lambda e, hb=hb: e.tensor_tensor(out=m1[:, 4 * hb:4 * hb + 4, 0:N], in0=gcn[:, 4 * hb:4 * hb + 4, 0:N],
                                                       in1=ps[5 + hb][:, 0:4 * N].rearrange("p (c t) -> p c t", c=4), op=ALU.mult),
                  r=["gcn", f"ps{5 + hb}"], w=["m1"])
        P.add("sp", lambda e: e.dma_start(out=ob[0:N, :], in_=o_src), r=[f"o_d{idx}"], w=["ob"], sem="ob")

        def tro(e):
            for kc in range(4):
                ins = e.transpose(out=psb(3)[:, kc * 128:kc * 128 + N], in_=ob[0:N, kc * 128:(kc + 1) * 128], identity=identb[0:N, 0:N])
            return ins
        P.add("pe", tro, r=["ob", "identb"], w=["ps3"])
        P.add("act", lambda e: e.copy(out=oT[:, :, 0:N], in_=psb(3)[:, 0:512].rearrange("p (a b) -> p a b", a=4)[:, :, 0:N]), r=["ps3"], w=["oT"])
        featproj(5, 8, N, w_nso_bf, "w_nso_bf", 0, 0, selb_, selk, oT, "oT", nk=4, bias=False)
        featproj(1, 8, N, w_mg_bf, "w_mg_bf", 1024, OFF_MG + 1024, selb_, selk, hT, "hT")
        for hb in range(2):
            sigmoid_from(gcn[:, 4 * hb:4 * hb + 4, 0:N], ps[1 + hb][:, 0:4 * N].rearrange("p (c t) -> p c t", c=4), f"ps{1 + hb}", "gcn")
            P.add("dve", lambda e, hb=hb: e.tensor_tensor(out=gcn[:, 4 * hb:4 * hb + 4, 0:N], in0=gcn[:, 4 * hb:4 * hb + 4, 0:N],
                                                       in1=ps[5 + hb][:, 0:4 * N].rearrange("p (c t) -> p c t", c=4), op=ALU.mult),
                  r=["gcn", f"ps{5 + hb}"], w=["gcn"])
        P.add("dve", lambda e: e.tensor_tensor(out=mT[:, :, 0:N], in0=m1[:, :, 0:N], in1=gcn[:, :, 0:N], op=ALU.add), r=["m1", "gcn"], w=["mT"])
        for half in range(2):
            def mmo(e, half=half):
                for kc in range(8):
                    ins = e.matmul(ps[5 + half][0:N, :], lhsT=mT[:, kc, 0:N], rhs=w_out_bf[:, kc, half * 512:(half + 1) * 512], start=(kc == 0), stop=(kc == 7))
                return ins
            P.add("pe", mmo, r=["mT", "w_out_bf"], w=[f"ps{5 + half}"])
            P.add("dve", lambda e, half=half: e.tensor_tensor(out=x1t[0:N, half * 512:(half + 1) * 512], in0=ps[5 + half][0:N, :],
                                                           in1=Gt[0:N, half * 512:(half + 1) * 512], op=ALU.mult), r=[f"ps{5 + half}", Gk], w=["x1t"])
        P.add("dve", lambda e: e.tensor_tensor(out=x1t[0:N, :], in0=x1t[0:N, :], in1=xb[0:N, :], op=ALU.add), r=["x1t", xk], w=["x1t"])
        P.add("sp", lambda e: e.dma_start(out=x1_dst, in_=x1t[0:N, :]), r=["x1t"], w=[f"x1d{idx}"], sem="st_x1")

    for i in range(n_own):
        passB(i, 128, xo[i, :, :], A1, "A1", G1, "G1", selPb, "selPb", [(0, 128)], "x", o_d[i, :, :], x1_d[i, :, :])
    if do_sample:
        passB(NT, 32, xs_d[:, :], A1s, "A1s", G1s, "G1s", selSb, "selSb", [(8 * q_, 8) for q_ in range(4)], "s", o_d[NT, 0:32, :], x1_d[NT, 0:32, :])
    esB.close()
    P.barrier(lambda e: e.memset(dummy[:], 3.0))

    esC = ExitStack()

    def sbC(name, shape, dt=F32):
        return esC.enter_context(nc.sbuf_tensor("sC_" + name, list(shape), dt))
    w_g_bf = sbC("w_g_bf", [128, 8, D_FF], BF)
    w_u_bf = sbC("w_u_bf", [128, 8, D_FF], BF)
    w_d_bf = sbC("w_d_bf", [128, NFF, D], BF)
    for kc in range(8):
        for c0, c1 in ((0, 2048), (2048, D_FF)):
            ld(w_g_bf[:, kc, c0:c1], w_gate.rearrange("(kc p) n -> p kc n", p=128)[:, kc, c0:c1], "w_g_bf", "wc0", eng="pool")
            ld(w_u_bf[:, kc, c0:c1], w_up.rearrange("(kc p) n -> p kc n", p=128)[:, kc, c0:c1], "w_u_bf", "wc1", eng="pool")
    for fc in range(NFF):
        ld(w_d_bf[:, fc, :], w_down.rearrange("(fc p) n -> p fc n", p=128)[:, fc, :], "w_d_bf", "wc2", eng="pool")
    zbg = zb1
    zbu = sbC("zbu", [5, D_FF], BF)
    A2, G2, A2s, G2s = A1, G1, A1s, G1s
    sgc = utok
    actT = sbC("actT", [128, NFF, 128], BF)
    ybuf = sbC("ybuf", [128, D])
    for wt, wk, zb, zk in ((w_g_bf, "w_g_bf", zbg, "zb1"), (w_u_bf, "w_u_bf", zbu, "zbu")):
        for c0 in range(0, D_FF, 512):
            c1 = min(c0 + 512, D_FF)

            def mm(e, wt=wt, c0=c0, c1=c1):
                for kc in range(8):
                    ins = e.matmul(ps[3][0:5, 0:c1 - c0], lhsT=sh2T[:, kc, :], rhs=wt[:, kc, c0:c1], start=(kc == 0), stop=(kc == 7))
                return ins
            P.add("pe", mm, r=["sh2T", wk], w=["ps3"])
            P.add("act", lambda e, zb=zb, c0=c0, c1=c1: e.copy(out=zb[:, c0:c1], in_=ps[3][0:5, 0:c1 - c0]), r=["ps3"], w=[zk])
    for dst, key, src, skey, sl, slk, n in ((A2, "A1", a2row, "a2row", selP, "selP", 128), (G2, "G1", g2row, "g2row", selP, "selP", 128),
                                            (A2s, "A1s", a2row, "a2row", selS, "selS", 32), (G2s, "G1s", g2row, "g2row", selS, "selS", 32)):
        for half in range(2):
            P.add("pe", lambda e, src=src, half=half, sl=sl, n=n: e.matmul(ps[1][0:n, :], lhsT=sl[:, 0:n], rhs=src[:, half * 512:(half + 1) * 512], start=True, stop=True),
                  r=[slk, skey], w=["ps1"])
            P.add("act", lambda e, dst=dst, half=half, n=n: e.copy(out=dst[:, half * 512:(half + 1) * 512], in_=ps[1][0:n, :]), r=["ps1"], w=[key])

    def passC(idx, N, xsrc, At, Ak, Gt, Gk, selb_, selk, y_dst):
        ib = front_norm(xsrc, cs_o[0, 0:N, :], N, At, Ak, src_key=f"x1d{idx}")
        xb = xbuf[ib]
        xk = f"xbuf{ib}"
        for gi, f0 in enumerate(range(0, NFF, 4)):
            nf = min(4, NFF - f0)
            bg, bu = (1, 2) if gi % 2 == 0 else (3, 4)
            for bank, wt, wk, zb, zk in ((bg, w_g_bf, "w_g_bf", zbg, "zb1"), (bu, w_u_bf, "w_u_bf", zbu, "zbu")):
                def mm(e, bank=bank, wt=wt, zb=zb, f0=f0, nf=nf):
                    for j in range(nf):
                        out = ps[bank][:, j * N:(j + 1) * N]
                        col = (f0 + j) * 128
                        for kc in range(8):
                            e.matmul(out, lhsT=wt[:, kc, col:col + 128], rhs=hT[:, kc, 0:N], start=(kc == 0), stop=False)
                        ins = e.matmul(out, lhsT=zb[:, col:col + 128], rhs=selb_[:, 0:N], start=False, stop=True)
                    return ins
                P.add("pe", mm, r=[wk, zk, "hT", selk], w=[f"ps{bank}"])
            sigmoid_from(sgc[:, 0:nf * N], ps[bg][:, 0:nf * N], f"ps{bg}", "utok")
            P.add("dve", lambda e, bg=bg, nf=nf: e.tensor_tensor(out=sgc[:, 0:nf * N], in0=sgc[:, 0:nf * N], in1=ps[bg][:, 0:nf * N], op=ALU.mult),
                  r=["utok", f"ps{bg}"], w=["utok"])
            P.add("dve", lambda e, bu=bu, nf=nf, f0=f0: e.tensor_tensor(out=actT[:, f0:f0 + nf, 0:N], in0=sgc[:, 0:nf * N].rearrange("p (c t) -> p c t", c=nf),
                                                                      in1=ps[bu][:, 0:nf * N].rearrange("p (c t) -> p c t", c=nf), op=ALU.mult),
                  r=["utok", f"ps{bu}"], w=["actT"])
        for half in range(2):
            def mmd(e, half=half):
                for fc in range(NFF):
                    ins = e.matmul(ps[5 + half][0:N, :], lhsT=actT[:, fc, 0:N], rhs=w_d_bf[:, fc, half * 512:(half + 1) * 512], start=(fc == 0), stop=(fc == NFF - 1))
                return ins
            P.add("pe", mmd, r=["actT", "w_d_bf"], w=[f"ps{5 + half}"])
            P.add("dve", lambda e, half=half: e.tensor_tensor(out=ybuf[0:N, half * 512:(half + 1) * 512], in0=ps[5 + half][0:N, :],
                                                           in1=Gt[0:N, half * 512:(half + 1) * 512], op=ALU.mult), r=[f"ps{5 + half}", Gk], w=["ybuf"])
        P.add("dve", lambda e: e.tensor_tensor(out=ybuf[0:N, :], in0=ybuf[0:N, :], in1=xb[0:N, :], op=ALU.add), r=["ybuf", xk], w=["ybuf"])
        P.add("sp", lambda e: e.dma_start(out=y_dst, in_=ybuf[0:N, :]), r=["ybuf"], w=[], sem="st_y")

    for i in range(n_own):
        ld(dummy[:], dummy[:], "dummy", "dmy") if False else None
        passC(i, 128, x1_d[i, :, :], A2, "A1", G2, "G1", selPb, "selPb", y_o[i, :, :])
    if do_sample:
        passC(NT, 32, x1_d[NT, 0:32, :], A2s, "A1s", G2s, "G1s", selSb, "selSb", ys_o[:, :])
    esC.close()

    final = [k for k in P.dma_sems if k.startswith("st_")]
    P.emit(final_sems=final)
    es.close()
    return nc


def _rope_table(pos):
    half = 32
    inv = np.power(np.float32(10000.0), -np.arange(half, dtype=np.float32) / np.float32(half)).astype(np.float32)
    ang = pos.astype(np.float32)[:, None] * inv[None, :]
    return np.concatenate([np.cos(ang), np.sin(ang)], axis=-1).astype(np.float32)


def _host_prep(inp, n_own=NT):
    bf = ml_dtypes.bfloat16
    f32 = np.float32
    common = {}
    common["identf"] = np.eye(128, dtype=f32)
    e = np.zeros((128, 64, 128), f32)
    for kt in range(64):
        e[2 * kt, kt, 0:64] = 1.0
        e[2 * kt + 1, kt, 64:128] = 1.0
    common["eall"] = e.reshape(128, 64 * 128)
    s2 = np.zeros((128, 2), f32)
    s2[0:64, 0] = 1.0 / 64
    s2[64:128, 1] = 1.0 / 64
    common["sel2"] = s2
    sp = np.zeros((5, 128), f32)
    sp[0, :] = 1.0
    common["selP"] = sp
    ssel = np.zeros((5, 32), f32)
    for q in range(4):
        ssel[1 + q, 8 * q:8 * q + 8] = 1.0
    common["selS"] = ssel
    for k in ("cache_cmp_k", "cache_cmp_v", "cache_sel_k", "cache_sel_v"):
        common[k] = inp[k].reshape(5120 * 128, 128)
    mas = np.zeros((1, 256), f32)
    mas[0, 0] = 8.0
    mas[0, 255] = 8.0
    common["madd_s"] = mas
    rr_ = np.arange(128)
    common["mwin_s"] = np.where(rr_[:, None] > np.arange(8)[None, :], 0.0, -BIG).astype(f32)
    mn = np.full((32, 32), -BIG, f32)
    for q in range(4):
        for t in range(8):
            mn[8 * q:8 * q + t + 1, 8 * q + t] = 0.0
    common["mnew_s"] = mn
    common["cs_s"] = np.tile(_rope_table(16384 + np.arange(8)), (4, 1))
    for k in ("w_ada", "b_ada", "mix_norm_g", "ffn_norm_g", "w_in", "q_norm_g", "w_pw2", "w_nsa_o", "w_out", "w_gate", "w_up", "w_down"):
        common[k] = np.ascontiguousarray(inp[k][0]) if inp[k].ndim == 3 else np.ascontiguousarray(inp[k])
    common["k_norm_g"] = np.ascontiguousarray(inp["k_norm_g"][0].reshape(1, 192))
    common["wdwT"] = np.ascontiguousarray(inp["w_dw"][0].T.reshape(4, 128, CONV_K).transpose(1, 0, 2))
    cv = np.stack([inp["b_dw"][0], inp["conv_ln_g"][0], inp["conv_ln_b"][0]], 0)
    common["cvec"] = np.ascontiguousarray(cv.reshape(3, 4, 128).transpose(2, 0, 1))
    common["modk"] = np.ascontiguousarray(np.tile(inp["cmp_mod_k"][0], (2, 1)))
    common["modv"] = np.ascontiguousarray(np.tile(inp["cmp_mod_v"][0], (2, 1)))
    for nm, src in (("wkbd", "cmp_w_k"), ("wvbd", "cmp_w_v")):
        m = np.zeros((128, 128), f32)
        m[0:64, 0:64] = inp[src][0]
        m[64:128, 64:128] = inp[src][0]
        common[nm] = m
    r = np.arange(128)
    tri = np.where(r[:, None] <= r[None, :], 0.0, -BIG).astype(f32)
    winlo = np.where(r[:, None] > r[None, :], 0.0, -BIG).astype(f32)
    maps = []
    for c in range(8):
        b, par = c // 2, c % 2
        m = dict(common)
        xp = inp["x_prompt"][b].reshape(64, 128, D)
        own = np.arange(NT) * 2 + par
        oth = np.arange(NT) * 2 + (1 - par)
        m["xo"] = np.ascontiguousarray(xp[own])
        m["xt"] = np.ascontiguousarray(xp[oth])
        xflat = inp["x_prompt"][b]
        xhh = np.zeros((NT, 32, D), f32)
        hvv = np.zeros((128, NT), f32)
        for i in range(NT):
            s0 = own[i] * 128
            if s0 > 0:
                xhh[i] = xflat[s0 - 32:s0]
                hvv[:, i] = 1.0
        m["xh"] = xhh
        m["hv"] = hvv
        pos_o = (own[:, None] * 128 + r[None, :]).reshape(-1)
        pos_t = (oth[:, None] * 128 + r[None, :]).reshape(-1)
        m["cs_o"] = _rope_table(pos_o).reshape(NT, 128, 64)
        m["cs_t"] = _rope_table(pos_t).reshape(NT, 128, 64)
        gb = np.zeros(128, np.int64)
        for i in range(NT):
            gb[2 * i], gb[2 * i + 1] = 2 * own[i], 2 * own[i] + 1
            gb[64 + 2 * i], gb[64 + 2 * i + 1] = 2 * oth[i], 2 * oth[i] + 1
        mc = np.zeros((NT, 128, 128), f32)
        ma = np.zeros((NT, 128, 128), f32)
        for i in range(NT):
            qpos = own[i] * 128 + r
            cur = qpos // 64
            ok = (gb[None, :] + 1) * 64 <= qpos[:, None] + 1
            mc[i] = np.where(ok, 0.0, -BIG)
            vis = gb[None, :] <= cur[:, None]
            forced = (gb[None, :] == 0) | (gb[None, :] == cur[:, None]) | (gb[None, :] == cur[:, None] - 1)
            ma[i] = np.where(vis, np.where(forced, 8.0, 0.0), -1.0)
        m["m_cmp"] = mc
        m["m_add"] = ma
        m["m_cmpT"] = np.ascontiguousarray(mc.transpose(0, 2, 1))
        full = np.zeros((128, 128), f32)
        none = np.full((128, 128), -BIG, f32)
        m["masks"] = np.ascontiguousarray(np.stack([tri, winlo, none if par == 0 else full, full if par == 0 else none], 1))
        m["xs"] = np.ascontiguousarray(inp["x_sample"][4 * c:4 * c + 4].reshape(32, D))
        m["cwk"] = np.ascontiguousarray(inp["cache_win_k"][0, 4 * c:4 * c + 4].reshape(4, 512, 128))
        m["cwv"] = np.ascontiguousarray(inp["cache_win_v"][0, 4 * c:4 * c + 4].reshape(4, 512, 128))
        m["sconv"] = np.ascontiguousarray(inp["state_conv"][0, 4 * c:4 * c + 4])
        m["pt"] = np.ascontiguousarray(inp["page_table"][4 * c:4 * c + 4].reshape(1, 512).astype(np.int32))
        cs = inp["c_sample"][4 * c:4 * c + 4]
        c5 = np.concatenate([inp["c_prompt"][b:b + 1], cs], 0)
        m["c5T"] = np.ascontiguousarray(c5.T.reshape(8, 128, 5).transpose(1, 0, 2).reshape(128, 40))
        maps.append(m)
    return maps


_CACHE = {}


def kernel(**inp):
    inp = {k: np.asarray(v) for k, v in inp.items()}
    if "nc" not in _CACHE:
        _CACHE["nc"] = build()
    nc = _CACHE["nc"]
    maps = _host_prep(inp)
    res = run_bass_kernel_spmd(nc, maps, core_ids=list(range(8)))
    R = res.results
    B, T = 4, 8192
    yp = np.zeros((B, T, D), np.float32)
    kv = np.zeros((B, T, 768), np.float32)
    for c in range(8):
        b, par = c // 2, c % 2
        yp[b].reshape(64, 128, D)[par::2] = R[c]["y_o"]
        kv[b].reshape(64, 128, 768)[par::2] = R[c]["kv_o"]
    kv6 = kv.reshape(B, T, 6, 2, 64)
    outs = [yp, np.concatenate([R[c]["ys_o"] for c in range(8)], 0).reshape(32, 8, D)]
    for j in range(4):
        outs.append(np.ascontiguousarray(kv6[None, :, :, j]))
    for j in (4, 5):
        outs.append(np.ascontiguousarray(kv6[None, :, -512:, j]))
    pconv = np.stack([R[2 * b + 1]["conv_o"] for b in range(B)], 0)[None]
    outs.append(pconv)
    skv = np.concatenate([R[c]["skv_o"].reshape(4, 8, 6, 2, 64) for c in range(8)], 0)
    for j in range(4):
        outs.append(np.ascontiguousarray(skv[None, :, :, j]))
    swin = np.concatenate([R[c]["swin_o"] for c in range(8)], 0)
    for j in range(2):
        outs.append(np.ascontiguousarray(swin[None, :, j].reshape(1, 32, 512, 2, 64)))
    outs.append(np.concatenate([R[c]["sconv_o"] for c in range(8)], 0)[None])
    return tuple(outs)
```

```python
import numpy as np
import ml_dtypes
from contextlib import ExitStack
import concourse.bass as bass
import concourse.mybir as mybir
from concourse.bass_utils import run_bass_kernel_spmd

F32 = mybir.dt.float32
BF = mybir.dt.bfloat16
I32 = mybir.dt.int32
ALU = mybir.AluOpType
AF = mybir.ActivationFunctionType
AX = mybir.AxisListType

D = 1024
NT = 32
OFF_Q = 1024
OFF_KV = 1536
OFF_NG = 2304
OFF_MG = 2328
D_IN = 4376
D_FF = 2816
NFF = 22
EPS = 1e-6
BIG = 30000.0
SCALE = 0.125
CONV_K = 31


class Op:
    __slots__ = ("eng", "fn", "deps", "signal", "val", "sem", "is_dma", "idx")


class Prog:
    ENGS = ("pe", "act", "dve", "pool", "sp")

    def __init__(self, nc, es):
        self.nc = nc
        self.es = es
        self.ops = {e: [] for e in self.ENGS}
        self.lastw = {}
        self.readers = {}
        self.dma_sems = {}
        self.n = 0
        self.bar = None

    def add(self, eng, fn, r=(), w=(), sem=None):
        op = Op()
        op.eng, op.fn, op.signal, op.val, op.sem = eng, fn, False, None, None
        op.is_dma = sem is not None
        op.idx = self.n
        self.n += 1
        deps = []
        for k in list(r) + list(w):
            if k not in self.lastw and k not in self.readers and self.bar is not None:
                deps.append((self.bar, True))
        for k in r:
            d = self.lastw.get(k)
            if d is not None:
                deps.append((d, True))
        for k in w:
            d = self.lastw.get(k)
            if d is not None:
                deps.append((d, False))
            for d in self.readers.get(k, ()):
                deps.append((d, False))
        seen = set()
        op.deps = []
        for d, raw in sorted(deps, key=lambda t: not t[1]):
            if id(d) in seen or d is op:
                continue
            if d.eng == eng and not d.is_dma and not op.is_dma:
                if eng == "pe" or not raw:
                    continue
            seen.add(id(d))
            if d.is_dma:
                if sem is not None and d.sem == sem:
                    continue
                op.deps.append((d.sem, self.dma_sems[d.sem][1]))
            else:
                d.signal = True
                op.deps.append(d)
        if op.is_dma:
            if sem not in self.dma_sems:
                self.dma_sems[sem] = [self.es.enter_context(self.nc.semaphore("d_" + sem)), 0]
            self.dma_sems[sem][1] += 16
            op.sem = sem
            op.val = self.dma_sems[sem][1]
        for k in r:
            self.readers.setdefault(k, []).append(op)
        for k in w:
            self.lastw[k] = op
            self.readers[k] = []
        self.ops[eng].append(op)
        return op

    def barrier(self, fn):
        keys = set(self.lastw) | set(self.readers)
        op = self.add("dve", fn, r=[], w=sorted(keys))
        self.bar = op
        return op

    def emit(self, final_sems=()):
        nc = self.nc
        esem = {e: self.es.enter_context(nc.semaphore("e_" + e)) for e in self.ENGS}
        for e in self.ENGS:
            c = 0
            for op in self.ops[e]:
                if op.signal and not op.is_dma:
                    c += 1
                    op.val = c
        block = self.es.enter_context(nc.Block())
        prog = self

        def run(e, eng):
            waited = {}
            for op in prog.ops[e]:
                for d in op.deps:
                    if isinstance(d, tuple):
                        key, h, v = "d_" + d[0], prog.dma_sems[d[0]][0], d[1]
                    else:
                        key, h, v = "e_" + d.eng, esem[d.eng], d.val
                    if waited.get(key, 0) >= v:
                        continue
                    waited[key] = v
                    eng.wait_ge(h, v)
                ins = op.fn(eng)
                if op.is_dma:
                    ins.then_inc(prog.dma_sems[op.sem][0], 16)
                elif op.signal:
                    ins.then_inc(esem[e], 1)
            if e == "sp":
                for s in final_sems:
                    h, v = prog.dma_sems[s]
                    eng.wait_ge(h, v)

        @block.tensor
        def _(eng):
            run("pe", eng)

        @block.scalar
        def _(eng):
            run("act", eng)

        @block.vector
        def _(eng):
            run("dve", eng)

        @block.gpsimd
        def _(eng):
            run("pool", eng)

        @block.sync
        def _(eng):
            run("sp", eng)


def build(n_own=NT, do_sample=True):
    nc = bass.Bass("TRN2", target_bir_lowering=False)
    es = ExitStack()
    P = Prog(nc, es)

    def din(name, shape, dt=F32):
        return nc.dram_tensor(name, list(shape), dt, kind="ExternalInput").ap()

    def dout(name, shape, dt=F32):
        return nc.dram_tensor(name, list(shape), dt, kind="ExternalOutput").ap()

    def sb(name, shape, dt=F32):
        return es.enter_context(nc.sbuf_tensor("s_" + name, list(shape), dt))

    xo = din("xo", [NT, 128, D])
    xt = din("xt", [NT, 128, D])
    xh = din("xh", [NT, 32, D])
    cs_o = din("cs_o", [NT, 128, 64])
    cs_t = din("cs_t", [NT, 128, 64])
    hv = din("hv", [128, NT])
    m_cmp = din("m_cmp", [NT, 128, 128])
    m_add = din("m_add", [NT, 128, 128])
    m_cmpT = din("m_cmpT", [NT, 128, 128])
    masks_d = din("masks", [128, 4, 128])
    eall_d = din("eall", [128, 64 * 128])
    identf_d = din("identf", [128, 128])
    sel2_d = din("sel2", [128, 2])
    selP_d = din("selP", [5, 128])
    c5T_d = din("c5T", [128, 40])
    w_ada = din("w_ada", [D, 6 * D])
    b_ada = din("b_ada", [1, 6 * D])
    gmix = din("mix_norm_g", [1, D])
    gffn = din("ffn_norm_g", [1, D])
    w_in = din("w_in", [D, D_IN])
    gq_d = din("q_norm_g", [1, 64])
    gk_d = din("k_norm_g", [1, 192])
    wdwT_d = din("wdwT", [128, 4, CONV_K])
    cvec_d = din("cvec", [128, 3, 4])
    w_pw2 = din("w_pw2", [512, D])
    modk_d = din("modk", [128, 64])
    modv_d = din("modv", [128, 64])
    wkbd_d = din("wkbd", [128, 128])
    wvbd_d = din("wvbd", [128, 128])
    w_nsa_o = din("w_nsa_o", [512, D])
    w_out = din("w_out", [D, D])
    w_gate = din("w_gate", [D, D_FF])
    w_up = din("w_up", [D, D_FF])
    w_down = din("w_down", [D_FF, D])

    xs_d = din("xs", [32, D])
    cs_s = din("cs_s", [32, 64])
    selS_d = din("selS", [5, 32])
    cwk_d = din("cwk", [4, 512, 128])
    cwv_d = din("cwv", [4, 512, 128])
    sconv_d = din("sconv", [4, 30, 512])
    pt_d = din("pt", [1, 512], I32)
    ccmpk = din("cache_cmp_k", [5120 * 128, 128])
    ccmpv = din("cache_cmp_v", [5120 * 128, 128])
    cselk = din("cache_sel_k", [5120 * 128, 128])
    cselv = din("cache_sel_v", [5120 * 128, 128])
    madds_d = din("madd_s", [1, 256])
    mwin_d = din("mwin_s", [128, 8])
    mnew_d = din("mnew_s", [32, 32])
    ys_o = dout("ys_o", [32, D])
    skv_o = dout("skv_o", [32, 768])
    swin_o = dout("swin_o", [4, 2, 512, 128])
    sconv_o = dout("sconv_o", [4, 30, 512])
    y_o = dout("y_o", [NT, 128, D])
    kv_o = dout("kv_o", [NT, 128, 768])
    conv_o = dout("conv_o", [30, 512])
    x1_d = nc.dram_tensor("x1_scr", [NT + 1, 128, D], F32, kind="Internal").ap()
    o_d = nc.dram_tensor("o_scr", [NT + 1, 128, 512], BF, kind="Internal").ap()

    ps = [es.enter_context(nc.psum_tensor(f"ps{i}", [128, 512], F32)) for i in range(8)]

    def psb(i):
        return ps[i][:].bitcast(BF)

    identb = sb("identb", [128, 128], BF)
    identf = sb("identf", [128, 128])
    masks = sb("masks", [128, 4, 128], BF)
    sel2 = sb("sel2", [128, 2], BF)
    selP = sb("selP", [5, 128])
    selPb = sb("selPb", [5, 128], BF)
    selS = sb("selS", [5, 32])
    selSb = sb("selSb", [5, 32], BF)
    A1s = sb("A1s", [32, D])
    G1s = sb("G1s", [32, D])
    pm1 = sb("pm1", [128, 2, 64])
    gk_bc = sb("gk_bc", [128, 192])
    gq_bc = sb("gq_bc", [128, 64])
    wkbd = sb("wkbd", [128, 128], BF)
    wvbd = sb("wvbd", [128, 128], BF)
    hv_sb = sb("hv_sb", [128, NT])
    dummy = sb("dummy", [128, 1])
    nhalf = sb("nhalf", [128, 128])
    wdwT = sb("wdwT", [128, 4, CONV_K])
    cvec = sb("cvec", [128, 3, 4])
    A1 = sb("A1", [128, D])
    G1 = sb("G1", [128, D])
    zb1 = sb("zb1", [5, D_IN], BF)
    sh2T = sb("sh2T", [128, 8, 5], BF)
    a2row = sb("a2row", [5, D])
    g2row = sb("g2row", [5, D])

    def ld(dst, src, key, sem, eng="sp", extra_w=()):
        P.add(eng, lambda e, d=dst, s=src: e.dma_start(out=d, in_=s), r=[], w=[key] + list(extra_w), sem=sem)

    ld(identf[:], identf_d[:, :], "identf", "c0")
    ld(identb[:], identf_d[:, :], "identb", "c1", eng="pool")
    ld(masks[:], masks_d[:, :, :], "masks", "c3", eng="pool")
    ld(sel2[:], sel2_d[:, :], "sel2", "c4", eng="pool")
    ld(selP[:], selP_d[:, :], "selP", "c5")
    ld(selPb[:], selP_d[:, :], "selPb", "c6", eng="pool")
    ld(selS[:], selS_d[:, :], "selS", "c15")
    ld(selSb[:], selS_d[:, :], "selSb", "c16", eng="pool")
    ld(pm1[:, 0, :], modk_d[:, :], "pm1", "c7")
    ld(pm1[:, 1, :], modv_d[:, :], "pm1", "c7")
    ld(gk_bc[:], gk_d[0:1, :].to_broadcast([128, 192]), "gk_bc", "c8")
    ld(gq_bc[:], gq_d[0:1, :].to_broadcast([128, 64]), "gq_bc", "c9")
    ld(wkbd[:], wkbd_d[:, :], "wkbd", "c10", eng="pool")
    ld(wvbd[:], wvbd_d[:, :], "wvbd", "c11", eng="pool")
    ld(hv_sb[:], hv[:, :], "hv", "c12")
    ld(wdwT[:], wdwT_d[:, :, :], "wdwT", "c13")
    ld(cvec[:], cvec_d[:, :, :], "cvec", "c14")
    P.add("pool", lambda e: e.memset(nhalf[:], -0.5), w=["nhalf"])
    P.add("dve", lambda e: e.tensor_scalar(out=pm1[:], in0=pm1[:], scalar1=1.0, scalar2=None, op0=ALU.add),
          r=["pm1"], w=["pm1"])

    w_in_v = w_in.rearrange("(kc p) n -> p kc n", p=128)
    def rsqrt_pool(out_ap, in_ap, inv_n, ncol, rk, wk):
        P.add("pool", lambda e: e.tensor_scalar(out=out_ap, in0=in_ap, scalar1=float(inv_n), scalar2=float(EPS), op0=ALU.mult, op1=ALU.add),
              r=[rk], w=[wk])
        P.add("pool", lambda e: e.tensor_tensor(out=out_ap, in0=out_ap, in1=nhalf[0:out_ap.shape[0], 0:ncol], op=ALU.pow),
              r=[wk, "nhalf"], w=[wk])

    with ExitStack() as es0:
        def sb0(name, shape, dt=F32):
            return es0.enter_context(nc.sbuf_tensor("s0_" + name, list(shape), dt))
        c5T = sb0("c5T", [128, 40])
        scT = sb0("scT", [128, 40])
        tmp40 = sb0("tmp40", [128, 40])
        mod5 = sb0("mod5", [5, 6 * D])
        bada = [sb0(f"bada{i}", [5, 512]) for i in range(2)]
        wz = [sb0(f"wz{i}", [128, 8, 512], BF) for i in range(2)]
        g5 = sb0("g5", [5, 2, D])
        a5 = sb0("a5", [5, 2, D])
        wa = [sb0(f"wa{i}", [128, 8, 512]) for i in range(2)]
        sh1T = sb0("sh1T", [128, 8, 5], BF)

        ld(c5T[:], c5T_d[:, :], "c5T", "p0")
        ld(g5[:, 0, :], gmix[0:1, :].to_broadcast([5, D]), "g5", "p2")
        ld(g5[:, 1, :], gffn[0:1, :].to_broadcast([5, D]), "g5", "p2")
        P.add("act", lambda e: e.activation(out=tmp40[:], in_=c5T[:], func=AF.Tanh, scale=0.5), r=["c5T"], w=["tmp40"])
        P.add("dve", lambda e: e.tensor_scalar(out=tmp40[:], in0=tmp40[:], scalar1=0.5, scalar2=0.5, op0=ALU.mult, op1=ALU.add),
              r=["tmp40"], w=["tmp40"])
        P.add("dve", lambda e: e.tensor_tensor(out=scT[:], in0=c5T[:], in1=tmp40[:], op=ALU.mult), r=["tmp40", "c5T"], w=["scT"])
        wa_v = w_ada.rearrange("(kc p) n -> p kc n", p=128)
        for cg in range(12):
            wb = wa[cg % 2]
            wk = f"wa{cg % 2}"
            for kc in range(8):
                ld(wb[:, kc, :], wa_v[:, kc, cg * 512:(cg + 1) * 512], wk, wk)
            ld(bada[cg % 2][:], b_ada[0:1, cg * 512:(cg + 1) * 512].to_broadcast([5, 512]), f"bada{cg % 2}", f"bada{cg % 2}")

            def mm(e, wb=wb):
                for kc in range(8):
                    ins = e.matmul(ps[0][0:5, :], lhsT=scT[:, kc * 5:(kc + 1) * 5], rhs=wb[:, kc, :], start=(kc == 0), stop=(kc == 7))
                return ins
            P.add("pe", mm, r=["scT", wk], w=["ps0"])
            P.add("dve", lambda e, cg=cg: e.tensor_tensor(out=mod5[:, cg * 512:(cg + 1) * 512], in0=ps[0][0:5, :],
                                                         in1=bada[cg % 2][:], op=ALU.add),
                  r=["ps0", f"bada{cg % 2}"], w=["mod5"])
        P.add("dve", lambda e: e.scalar_tensor_tensor(out=a5[:, 0, :], in0=mod5[:, D:2 * D], scalar=1.0, in1=g5[:, 0, :], op0=ALU.add, op1=ALU.mult),
              r=["mod5", "g5"], w=["a5"])
        P.add("dve", lambda e: e.scalar_tensor_tensor(out=a5[:, 1, :], in0=mod5[:, 4 * D:5 * D], scalar=1.0, in1=g5[:, 1, :], op0=ALU.add, op1=ALU.mult),
              r=["mod5", "g5"], w=["a5"])
        P.add("pool", lambda e: e.tensor_copy(out=a2row[:], in_=a5[:, 1, :]), r=["a5"], w=["a2row"])
        P.add("pool", lambda e: e.tensor_copy(out=g2row[:], in_=mod5[:, 5 * D:6 * D]), r=["mod5"], w=["g2row"])
        for dst, key, src in ((A1, "A1", a5[:, 0, :]), (G1, "G1", mod5[:, 2 * D:3 * D])):
            for half in range(2):
                P.add("pe", lambda e, src=src, half=half: e.matmul(ps[1][:, :], lhsT=selP[:, :], rhs=src[:, half * 512:(half + 1) * 512], start=True, stop=True),
                      r=["selP", "a5", "mod5"], w=["ps1"])
                P.add("act", lambda e, dst=dst, half=half: e.copy(out=dst[:, half * 512:(half + 1) * 512], in_=ps[1][:, :]),
                      r=["ps1"], w=[key])
        for dst, key, src in ((A1s, "A1s", a5[:, 0, :]), (G1s, "G1s", mod5[:, 2 * D:3 * D])):
            for half in range(2):
                P.add("pe", lambda e, src=src, half=half: e.matmul(ps[1][0:32, :], lhsT=selS[:, :], rhs=src[:, half * 512:(half + 1) * 512], start=True, stop=True),
                      r=["selS", "a5", "mod5"], w=["ps1"])
                P.add("act", lambda e, dst=dst, half=half: e.copy(out=dst[:, half * 512:(half + 1) * 512], in_=ps[1][0:32, :]),
                      r=["ps1"], w=[key])
        for dst, key, off in ((sh1T, "sh1T", 0), (sh2T, "sh2T", 3 * D)):
            def tr(e, off=off):
                for kc in range(8):
                    ins = e.transpose(out=ps[2][:, kc * 5:(kc + 1) * 5], in_=mod5[0:5, off + kc * 128: off + (kc + 1) * 128], identity=identf[0:5, 0:5])
                return ins
            P.add("pe", tr, r=["mod5", "identf"], w=["ps2"])
            P.add("act", lambda e, dst=dst: e.copy(out=dst[:].rearrange("p a b -> p (a b)"), in_=ps[2][:, 0:40]), r=["ps2"], w=[key])
        for ci, c0 in enumerate(range(0, D_IN, 512)):
            c1 = min(c0 + 512, D_IN)
            wzb = wz[ci % 2]
            wzk = f"wz{ci % 2}"
            for kc in range(8):
                ld(wzb[:, kc, 0:c1 - c0], w_in_v[:, kc, c0:c1], wzk, wzk, eng="pool")

            def mm(e, c0=c0, c1=c1, wzb=wzb):
                for kc in range(8):
                    ins = e.matmul(ps[3][0:5, 0:c1 - c0], lhsT=sh1T[:, kc, :], rhs=wzb[:, kc, 0:c1 - c0], start=(kc == 0), stop=(kc == 7))
                return ins
            P.add("pe", mm, r=["sh1T", wzk], w=["ps3"])
            P.add("act", lambda e, c0=c0, c1=c1: e.copy(out=zb1[:, c0:c1], in_=ps[3][0:5, 0:c1 - c0]), r=["ps3"], w=["zb1"])
    P.barrier(lambda e: e.memset(dummy[:], 1.0))

    utok = sb("utok", [128, 512])
    xbuf = [sb(f"xbuf{i}", [128, D]) for i in range(2)]
    csb = [sb(f"csb{i}", [128, 64]) for i in range(2)]
    junk = sb("junk", [128, D], BF)
    ss = sb("ss", [128, 1])
    rr = sb("rr", [128, 1])
    hbf = sb("hbf", [128, D], BF)
    hT = sb("hT", [128, 8, 128], BF)
    rt = [sb(f"rt{i}", [128, 8, 32]) for i in range(4)]

    esA = ExitStack()

    def sbA(name, shape, dt=F32):
        return esA.enter_context(nc.sbuf_tensor("sA_" + name, list(shape), dt))
    w_glu_bf = sbA("w_glu_bf", [128, 8, 1024], BF)
    for kc in range(8):
        ld(w_glu_bf[:, kc, :], w_in_v[:, kc, 0:1024], "w_glu_bf", "w1", eng="pool")
    eall = sbA("eall", [128, 64 * 128], BF)
    ld(eall[:], eall_d[:, :], "eall", "c2", eng="pool")
    NA = OFF_MG - OFF_Q
    w_in_bf = sbA("w_in_bf", [128, 8, NA], BF)
    for kc in range(8):
        ld(w_in_bf[:, kc, :], w_in_v[:, kc, OFF_Q:OFF_MG], "w_in_bf", "w0", eng="pool")
    zkv = sbA("zkv", [128, 768])
    sqk = sbA("sqk", [128, 3, 128])
    ssk = sbA("ssk", [128, 6])
    rk = sbA("rk", [128, 6])
    kn = sbA("kn", [128, 3, 128])
    kvout = [sbA(f"kvout{i}", [128, 768]) for i in range(2)]
    kbf = sbA("kbf", [128, 2, 128], BF)
    cmod = sbA("cmod", [128, 2, 128], BF)
    KTsel = sbA("KTsel", [128, 64 * 128], BF)
    KTwin = sbA("KTwin", [128, 6 * 128], BF)
    Vsel = sbA("Vsel", [128, 64, 2, 65], BF)
    Vwin = sbA("Vwin", [128, 6, 2, 65], BF)
    summT = sbA("summT", [128, 2, 128], BF)
    kcT = sbA("kcT", [128, 128], BF)
    Vcmp = sbA("Vcmp", [128, 2, 65], BF)
    zeros_bf = sbA("zeros_bf", [128, 260], BF)
    P.add("pool", lambda e: e.memset(zeros_bf[:], 0.0), w=["zeros_bf"])
    zq = sbA("zq", [128, 512])
    sqq = sbA("sqq", [128, 512])
    ssq = sbA("ssq", [128, 8])
    rq = sbA("rq", [128, 8])
    gate24 = sbA("gate24", [128, 24])
    qpb = sbA("qpb", [128, 4, 2, 64], BF)
    qrb = sbA("qrb", [128, 4, 2, 64], BF)
    QT = sbA("QT", [128, 8, 128], BF)
    mcmp_t = sbA("mcmp_t", [128, 128])
    madd_t = sbA("madd_t", [128, 128])
    mcT_t = sbA("mcT_t", [128, 128], BF)
    sbias = sbA("sbias", [128, 4, 128])
    ecmp = sbA("ecmp", [128, 4, 128])
    bnc_w = sbias
    bnc_c = ecmp[0:32].rearrange("p a b -> p (a b)")
    l4 = sbA("l4", [128, 4])
    rl4 = sbA("rl4", [128, 4])
    score = sbA("score", [128, 128])
    sc2 = sbA("sc2", [128, 128])
    mx8 = sbA("mx8", [128, 16])
    selb = sbA("selb", [128, 128], BF)
    selbT = sbA("selbT", [128, 128], BF)
    PT = [sbA(f"PT{i}", [128, 512], BF) for i in range(2)]
    Obr = sbA("Obr", [128, 3, 4, 65])
    l3 = sbA("l3", [128, 3, 4])
    coef = sbA("coef", [128, 3, 4])
    otmp = sbA("otmp", [128, 4, 64])
    otok = sbA("otok", [128, 8, 64])
    obf = sbA("obf", [128, 512], BF)

    P.add("pool", lambda e: e.memset(Vsel[:].rearrange("p a b c -> p (a b c)"), 1.0), w=["Vsel"])
    P.add("pool", lambda e: e.memset(Vwin[:].rearrange("p a b c -> p (a b c)"), 1.0), w=["Vwin"])
    P.add("pool", lambda e: e.memset(Vcmp[:].rearrange("p a b -> p (a b)"), 1.0), w=["Vcmp"])
    P.add("pool", lambda e: e.memset(summT[:].rearrange("p a b -> p (a b)"), 0.0), w=["summT"])

    state = {"xi": 0}

    def rope_ops(src4, dst1, dst2, cs, nh, rkeys, wkey, N=128):
        x1, x2 = src4
        shp = list(x1.shape)
        cos = cs[0:N, 0:32]
        sin = cs[0:N, 32:64]
        for _ in range(len(shp) - 2):
            cos = cos.unsqueeze(1)
            sin = sin.unsqueeze(1)
        cos = cos.to_broadcast(shp)
        sin = sin.to_broadcast(shp)
        t = [rt[j][0:N, 0:nh, :] if len(shp) == 3 else rt[j][0:N, 0:nh, :].rearrange("p (a b) d -> p a b d", a=shp[1]) for j in range(4)]

        P.add("dve", lambda e: e.tensor_tensor(out=t[0], in0=x1, in1=cos, op=ALU.mult), r=rkeys, w=["rt0"])
        P.add("dve", lambda e: e.tensor_tensor(out=t[1], in0=x2, in1=sin, op=ALU.mult), r=rkeys, w=["rt1"])
        P.add("dve", lambda e: e.tensor_tensor(out=t[2], in0=x2, in1=cos, op=ALU.mult), r=rkeys, w=["rt2"])
        P.add("dve", lambda e: e.tensor_tensor(out=t[3], in0=x1, in1=sin, op=ALU.mult), r=rkeys, w=["rt3"])
        P.add("dve", lambda e: e.tensor_tensor(out=dst1, in0=t[0], in1=t[1], op=ALU.subtract), r=["rt0", "rt1"], w=[wkey])
        P.add("dve", lambda e: e.tensor_tensor(out=dst2, in0=t[2], in1=t[3], op=ALU.add), r=["rt2", "rt3"], w=[wkey])

    def front_norm(xsrc, cs_src, N, At, Ak, src_key=None):
        ib = state["xi"] % 2
        state["xi"] += 1
        xb = xbuf[ib]
        xk = f"xbuf{ib}"
        P.add("sp", lambda e: e.dma_start(out=xb[0:N, :], in_=xsrc), r=([src_key] if src_key else []), w=[xk], sem=xk)
        ld(csb[ib][0:N, :], cs_src, f"csb{ib}", f"csb{ib}")
        P.add("act", lambda e: e.activation(out=junk[0:N, :], in_=xb[0:N, :], func=AF.Square, accum_out=ss[0:N, :]), r=[xk], w=["junk", "ss"])
        rsqrt_pool(rr[0:N, :], ss[0:N, :], 1.0 / D, 1, "ss", "rr")
        P.add("dve", lambda e: e.scalar_tensor_tensor(out=hbf[0:N, :], in0=xb[0:N, :], scalar=rr[0:N, 0:1], in1=At[0:N, :], op0=ALU.mult, op1=ALU.mult),
              r=[xk, "rr", Ak], w=["hbf"])

        def tr(e):
            for kc in range(8):
                ins = e.transpose(out=psb(0)[:, kc * 128:kc * 128 + N], in_=hbf[0:N, kc * 128:(kc + 1) * 128], identity=identb[0:N, 0:N])
            return ins
        P.add("pe", tr, r=["hbf", "identb"], w=["ps0"])
        P.add("act", lambda e: e.copy(out=hT[:, :, 0:N], in_=psb(0)[:, :].rearrange("p (a b) -> p a b", a=8)[:, :, 0:N]), r=["ps0"], w=["hT"])
        return ib

    def proj_tok(bank, c0, n, N, selb, selk, wt=None, wk="w_in_bf", woff=OFF_Q):
        wt = w_in_bf if wt is None else wt

        def mm(e):
            for kc in range(8):
                e.matmul(ps[bank][0:N, 0:n], lhsT=hT[:, kc, 0:N], rhs=wt[:, kc, c0 - woff:c0 - woff + n], start=(kc == 0), stop=False)
            return e.matmul(ps[bank][0:N, 0:n], lhsT=selb[:, 0:N], rhs=zb1[:, c0:c0 + n], start=False, stop=True)
        P.add("pe", mm, r=["hT", wk, "zb1", selk], w=[f"ps{bank}"])

    def f1(xsrc, cs_src, N, At, Ak, selb, selk, slot, wslot, kv_dst, kvi, glu_dst=None):
        ib = front_norm(xsrc, cs_src, N, At, Ak)
        cs = csb[ib]
        csk = f"csb{ib}"
        proj_tok(1, OFF_KV, 512, N, selb, selk)
        proj_tok(2, OFF_KV + 512, 256, N, selb, selk)
        P.add("act", lambda e: e.copy(out=zkv[0:N, 0:512], in_=ps[1][0:N, :]), r=["ps1"], w=["zkv"])
        P.add("act", lambda e: e.copy(out=zkv[0:N, 512:768], in_=ps[2][0:N, 0:256]), r=["ps2"], w=["zkv"])
        zk = zkv[0:N, :].rearrange("p (k v c) -> p k v c", k=3, v=2, c=128)
        P.add("dve", lambda e: e.tensor_tensor(out=sqk[0:N], in0=zk[:, :, 0, :], in1=zk[:, :, 0, :], op=ALU.mult), r=["zkv"], w=["sqk"])
        P.add("dve", lambda e: e.tensor_reduce(out=ssk[0:N], in_=sqk[0:N].rearrange("p k (h d) -> p (k h) d", h=2), axis=AX.X, op=ALU.add),
              r=["sqk"], w=["ssk"])
        rsqrt_pool(rk[0:N], ssk[0:N], 1.0 / 64, 6, "ssk", "rk")
        gkv = gk_bc[0:N].rearrange("p (k d) -> p k d", k=3).unsqueeze(2).to_broadcast([N, 3, 2, 64])
        kn4 = kn[0:N].rearrange("p k (h d) -> p k h d", h=2)
        P.add("dve", lambda e: e.tensor_tensor(out=kn4, in0=zk[:, :, 0, :].rearrange("p k (h d) -> p k h d", h=2), in1=gkv, op=ALU.mult),
              r=["zkv", "gk_bc"], w=["kn"])
        kn6 = kn[0:N].rearrange("p k (h d) -> p (k h) d", h=2)
        P.add("dve", lambda e: e.tensor_tensor(out=kn6, in0=kn6, in1=rk[0:N].unsqueeze(2).to_broadcast([N, 6, 64]), op=ALU.mult),
              r=["kn", "rk"], w=["kn"])
        kvo = kvout[kvi % 2]
        kvk = f"kvout{kvi % 2}"
        kvo4 = kvo[0:N].rearrange("p (k v c) -> p k v c", k=3, v=2, c=128)
        P.add("pool", lambda e: e.tensor_copy(out=kvo4[:, 0, 0, :], in_=kn[0:N, 0, :]), r=["kn"], w=[kvk])
        P.add("pool", lambda e: e.tensor_copy(out=kvo4[:, :, 1, :], in_=zk[:, :, 1, :]), r=["zkv"], w=[kvk])
        src = kn[0:N, 1:3, :].rearrange("p k (h f d) -> p k h f d", h=2, f=2, d=32)
        dstv = kvo4[:, 1:3, 0, :].rearrange("p k (h f d) -> p k h f d", h=2, f=2, d=32)
        rope_ops((src[:, :, :, 0, :], src[:, :, :, 1, :]), dstv[:, :, :, 0, :], dstv[:, :, :, 1, :], cs, 4, ["kn", csk], kvk, N)
        if slot is not None:
            P.add("pool", lambda e: e.tensor_copy(out=kbf[:], in_=kvo4[:, 1:3, 0, :]), r=[kvk], w=["kbf"])
            P.add("pool", lambda e: e.tensor_copy(out=Vsel[:, slot, :, 0:64], in_=zk[:, 1, 1, :].rearrange("p (h d) -> p h d", h=2)),
                  r=["zkv"], w=["Vsel"])
            P.add("pool", lambda e: e.tensor_copy(out=Vwin[:, wslot, :, 0:64], in_=zk[:, 2, 1, :].rearrange("p (h d) -> p h d", h=2)),
                  r=["zkv"], w=["Vwin"])
            pm1k = pm1[:, 0, :].unsqueeze(1).to_broadcast([128, 2, 64])
            pm1v = pm1[:, 1, :].unsqueeze(1).to_broadcast([128, 2, 64])
            P.add("pool", lambda e: e.tensor_tensor(out=cmod[:, 0, :].rearrange("p (h d) -> p h d", h=2), in0=kn[:, 0, :].rearrange("p (h d) -> p h d", h=2),
                                                   in1=pm1k, op=ALU.mult), r=["kn", "pm1"], w=["cmod"])
            P.add("pool", lambda e: e.tensor_tensor(out=cmod[:, 1, :].rearrange("p (h d) -> p h d", h=2), in0=zk[:, 0, 1, :].rearrange("p (h d) -> p h d", h=2),
                                                   in1=pm1v, op=ALU.mult), r=["zkv", "pm1"], w=["cmod"])

            def tr(e):
                e.transpose(out=psb(3)[:, 0:128], in_=kbf[:, 0, :], identity=identb[:])
                return e.transpose(out=psb(3)[:, 128:256], in_=kbf[:, 1, :], identity=identb[:])
            P.add("pe", tr, r=["kbf", "identb"], w=["ps3"])
            P.add("act", lambda e: e.copy(out=KTsel[:, slot * 128:(slot + 1) * 128], in_=psb(3)[:, 0:128]), r=["ps3"], w=["KTsel"])
            P.add("act", lambda e: e.copy(out=KTwin[:, wslot * 128:(wslot + 1) * 128], in_=psb(3)[:, 128:256]), r=["ps3"], w=["KTwin"])

            def mmc(e):
                e.matmul(ps[4][:, 0:2], lhsT=cmod[:, 0, :], rhs=sel2[:, :], start=True, stop=True)
                return e.matmul(ps[4][:, 2:4], lhsT=cmod[:, 1, :], rhs=sel2[:, :], start=True, stop=True)
            P.add("pe", mmc, r=["cmod", "sel2"], w=["ps4"])
            P.add("act", lambda e: e.copy(out=summT[:, :, 2 * slot:2 * slot + 2], in_=ps[4][:, 0:4].rearrange("p (a b) -> p a b", a=2)),
                  r=["ps4"], w=["summT"])
        if kv_dst is not None:
            P.add("sp", lambda e: e.dma_start(out=kv_dst, in_=kvo[0:N, :]), r=[kvk], w=[], sem=f"st_{kvk}")
        if glu_dst is not None:
            proj_tok(5, 0, 512, N, selb, selk, wt=w_glu_bf, wk="w_glu_bf", woff=0)
            proj_tok(6, 512, 512, N, selb, selk, wt=w_glu_bf, wk="w_glu_bf", woff=0)
            P.add("act", lambda e: e.activation(out=utok[0:N, :], in_=ps[6][0:N, :], func=AF.Tanh, scale=0.5), r=["ps6"], w=["utok"])
            P.add("dve", lambda e: e.tensor_scalar(out=utok[0:N, :], in0=utok[0:N, :], scalar1=0.5, scalar2=0.5, op0=ALU.mult, op1=ALU.add),
                  r=["utok"], w=["utok"])
            P.add("dve", lambda e: e.tensor_tensor(out=utok[0:N, :], in0=utok[0:N, :], in1=ps[5][0:N, :], op=ALU.mult), r=["utok", "ps5"], w=["utok"])
            for (dap, r0, r1) in glu_dst:
                P.add("sp", lambda e, dap=dap, r0=r0, r1=r1: e.dma_start(out=dap, in_=utok[r0:r1, :]), r=["utok"], w=[], sem="st_utok")
        return ib

    def update_cmp():
        def mm(e):
            e.matmul(ps[4][:, 0:128], lhsT=wkbd[:], rhs=summT[:, 0, :], start=True, stop=True)
            return e.matmul(ps[4][:, 128:256], lhsT=summT[:, 1, :], rhs=wvbd[:], start=True, stop=True)
        P.add("pe", mm, r=["summT", "wkbd", "wvbd"], w=["ps4"])
        P.add("act", lambda e: e.copy(out=kcT[:], in_=ps[4][:, 0:128]), r=["ps4"], w=["kcT"])
        P.add("act", lambda e: e.copy(out=Vcmp[:, :, 0:64], in_=ps[4][:, 128:256].rearrange("p (h d) -> p h d", h=2)), r=["ps4"], w=["Vcmp"])

    def qfront(N, selb_, selk, cs, csk):
        proj_tok(1, OFF_Q, 512, N, selb_, selk)
        proj_tok(2, OFF_NG, 24, N, selb_, selk)
        P.add("act", lambda e: e.copy(out=zq[0:N, :], in_=ps[1][0:N, :]), r=["ps1"], w=["zq"])
        P.add("act", lambda e: e.activation(out=gate24[0:N, :], in_=ps[2][0:N, 0:24], func=AF.Tanh, scale=0.5), r=["ps2"], w=["gate24"])
        P.add("dve", lambda e: e.tensor_scalar(out=gate24[0:N, :], in0=gate24[0:N, :], scalar1=0.5, scalar2=0.5, op0=ALU.mult, op1=ALU.add),
              r=["gate24"], w=["gate24"])
        P.add("dve", lambda e: e.tensor_tensor(out=sqq[0:N, :], in0=zq[0:N, :], in1=zq[0:N, :], op=ALU.mult), r=["zq"], w=["sqq"])
        P.add("dve", lambda e: e.tensor_reduce(out=ssq[0:N, :], in_=sqq[0:N, :].rearrange("p (h d) -> p h d", h=8), axis=AX.X, op=ALU.add),
              r=["sqq"], w=["ssq"])
        rsqrt_pool(rq[0:N, :], ssq[0:N, :], 1.0 / 64, 8, "ssq", "rq")
        zq3 = zq[0:N, :].rearrange("p (h d) -> p h d", h=8)
        P.add("dve", lambda e: e.tensor_tensor(out=zq3, in0=zq3, in1=gq_bc[0:N, :].unsqueeze(1).to_broadcast([N, 8, 64]), op=ALU.mult),
              r=["zq", "gq_bc"], w=["zq"])
        P.add("dve", lambda e: e.tensor_tensor(out=zq3, in0=zq3, in1=rq[0:N, :].unsqueeze(2).to_broadcast([N, 8, 64]), op=ALU.mult),
              r=["zq", "rq"], w=["zq"])
        P.add("pool", lambda e: e.tensor_copy(out=qpb[0:N].rearrange("p g k d -> p k g d"), in_=zq[0:N, :].rearrange("p (k g d) -> p k g d", k=2, g=4)),
              r=["zq"], w=["qpb"])
        src = zq[0:N, :].rearrange("p (k g f d) -> p k g f d", k=2, g=4, f=2)
        dstv = qrb[0:N].rearrange("p g k (f d) -> p k g f d", f=2)
        rope_ops((src[:, :, :, 0, :], src[:, :, :, 1, :]), dstv[:, :, :, 0, :], dstv[:, :, :, 1, :], cs, 8, ["zq", csk], "qrb", N)

        def tr(e):
            for v, qx in enumerate((qpb, qrb)):
                for g in range(4):
                    ins = e.transpose(out=psb(3)[:, (v * 4 + g) * 128:(v * 4 + g) * 128 + N],
                                      in_=qx[0:N, g, :, :].rearrange("p k d -> p (k d)"), identity=identb[0:N, 0:N])
            return ins
        P.add("pe", tr, r=["qpb", "qrb", "identb"], w=["ps3"])
        P.add("act", lambda e: e.copy(out=QT[:, :, 0:N], in_=psb(3)[:, :].rearrange("p (a b) -> p a b", a=8)[:, :, 0:N]), r=["ps3"], w=["QT"])

    st_att = {"s": 0, "first7": True}

    def s_tile(N, h, var, KT_ap, kkeys, bias_list):
        hp = slice(64 * h, 64 * h + 64)
        b = 5 + (st_att["s"] % 2)
        pt = PT[st_att["s"] % 2]
        ptk = f"PT{st_att['s'] % 2}"
        st_att["s"] += 1

        def mm(e):
            last = len(bias_list) == 0
            ins = e.matmul(ps[b][:, 0:4 * N], lhsT=KT_ap, rhs=QT[hp, var * 4:var * 4 + 4, 0:N], start=True, stop=last)
            for bi, (l_ap, r_ap) in enumerate(bias_list):
                ins = e.matmul(ps[b][:, 0:4 * N], lhsT=l_ap, rhs=r_ap.unsqueeze(1).to_broadcast([128, 4, N]), start=False,
                               stop=(bi == len(bias_list) - 1))
            return ins
        P.add("pe", mm, r=["QT"] + list(kkeys), w=[f"ps{b}"])
        P.add("act", lambda e: e.activation(out=pt[:, 0:4 * N], in_=ps[b][:, 0:4 * N], func=AF.Exp, scale=SCALE), r=[f"ps{b}"], w=[ptk])
        return pt, ptk

    def pv(N, pt, ptk, V_ap, vkeys, first):
        def mm(e):
            if first:
                e.matmul(ps[7][0:N, 0:260], lhsT=zeros_bf[:, 0:N], rhs=zeros_bf[:, 0:260], start=True, stop=False, skip_group_check=True)
            for g in range(4):
                ins = e.matmul(ps[7][0:N, g * 65:(g + 1) * 65], lhsT=pt[:, g * N:(g + 1) * N], rhs=V_ap,
                               start=False, stop=False, skip_group_check=True)
            return ins
        P.add("pe", mm, r=[ptk, "zeros_bf"] + list(vkeys), w=["ps7"])

    def attention(i, N=128):
        ld(mcmp_t[:], m_cmp[i, :, :], "mcmp_t", "mcmp_t")
        ld(madd_t[:], m_add[i, :, :], "madd_t", "madd_t")
        ld(mcT_t[:], m_cmpT[i, :, :], "mcT_t", "mcT_t", eng="pool")
        for h in range(2):
            hp = slice(64 * h, 64 * h + 64)
            def mmA(e, hp=hp):
                for g in range(4):
                    ins = e.matmul(ps[4][0:N, g * 128:(g + 1) * 128], lhsT=QT[hp, g, 0:N], rhs=kcT[hp, :], start=True, stop=True)
                return ins
            P.add("pe", mmA, r=["QT", "kcT"], w=["ps4"])
            P.add("dve", lambda e: e.tensor_tensor(out=sbias[0:N], in0=ps[4][0:N, :].rearrange("p (g b) -> p g b", g=4),
                                                  in1=mcmp_t[0:N, :].unsqueeze(1).to_broadcast([N, 4, 128]), op=ALU.add),
                  r=["ps4", "mcmp_t"], w=["sbias"])

            def expA(e):
                for g in range(4):
                    ins = e.activation(out=ecmp[0:N, g, :], in_=sbias[0:N, g, :], func=AF.Exp, scale=SCALE, accum_out=l4[0:N, g:g + 1])
                return ins
            P.add("act", expA, r=["sbias"], w=["ecmp", "l4"])

            P.add("dve", lambda e: e.tensor_scalar(out=rl4[0:N, :], in0=l4[0:N, :], scalar1=1e-30, scalar2=None, op0=ALU.max), r=["l4"], w=["rl4"])
            P.add("dve", lambda e: e.reciprocal(out=rl4[0:N, :], in_=rl4[0:N, :]), r=["rl4"], w=["rl4"])
            P.add("dve", lambda e: e.scalar_tensor_tensor(out=score[0:N, :], in0=ecmp[0:N, 0, :], scalar=rl4[0:N, 0:1], in1=madd_t[0:N, :], op0=ALU.mult, op1=ALU.add),
                  r=["ecmp", "rl4", "madd_t"], w=["score"])
            for g in range(1, 4):
                P.add("dve", lambda e, g=g: e.scalar_tensor_tensor(out=score[0:N, :], in0=ecmp[0:N, g, :], scalar=rl4[0:N, g:g + 1], in1=score[0:N, :], op0=ALU.mult, op1=ALU.add),
                      r=["ecmp", "rl4", "score"], w=["score"])
            P.add("dve", lambda e: e.max(out=mx8[0:N, 0:8], in_=score[0:N, :]), r=["score"], w=["mx8a"])
            P.add("dve", lambda e: e.match_replace(out=sc2[0:N, :], in_to_replace=mx8[0:N, 0:8], in_values=score[0:N, :], imm_value=-1e9),
                  r=["score", "mx8a"], w=["sc2"])
            P.add("dve", lambda e: e.max(out=mx8[0:N, 8:16], in_=sc2[0:N, :]), r=["sc2"], w=["mx8b"])
            P.add("dve", lambda e: e.tensor_scalar(out=selb[0:N, :], in0=score[0:N, :], scalar1=mx8[0:N, 15:16], scalar2=-BIG, op0=ALU.is_lt, op1=ALU.mult),
                  r=["score", "mx8b"], w=["selb"])
            P.add("pe", lambda e: e.transpose(out=psb(3)[:, 0:N], in_=selb[0:N, :], identity=identb[0:N, 0:N]), r=["selb", "identb"], w=["ps3"])
            P.add("act", lambda e: e.copy(out=selbT[:, 0:N], in_=psb(3)[:, 0:N]), r=["ps3"], w=["selbT"])

            seq = []
            seq.append((0, 0, kcT[hp, :], ["kcT"], [(identb[:, :], mcT_t[:, 0:N])], Vcmp[:, h, :], ["Vcmp"]))
            wt_ = []
            for d in (2, 1, 0):
                if i - d >= 0:
                    wt_.append(((i - d) % 3, {2: 1, 1: None, 0: 0}[d]))
            for d in (2, 1, 0):
                if i - d >= 0:
                    wt_.append((3 + (i - d) % 3, {2: 3, 1: None, 0: 2}[d]))
            for (w, mk) in wt_:
                bl = [] if mk is None else [(identb[:, :], masks[:, mk, 0:N])]
                seq.append((2, 1, KTwin[hp, w * 128:(w + 1) * 128], ["KTwin", "masks"], bl, Vwin[:, w, h, :], ["Vwin"]))
            st_ = [(j, 0 if j == i else None) for j in range(i + 1)] + [(32 + j, 2 if j == i else None) for j in range(i + 1)]
            for (sl, mk) in st_:
                bl = [(eall[:, sl * 128:(sl + 1) * 128], selbT[:, 0:N])]
                if mk is not None:
                    bl.append((identb[:, :], masks[:, mk, 0:N]))
                seq.append((1, 1, KTsel[hp, sl * 128:(sl + 1) * 128], ["KTsel", "eall", "selbT", "masks"], bl, Vsel[:, sl, h, :], ["Vsel"]))

            def flush(prev, nxt_br):
                br, pt_, ptk_, V_ap, vkeys, first = prev
                pv(N, pt_, ptk_, V_ap, vkeys, first)
                if nxt_br != br:
                    P.add("act", lambda e, br=br: e.copy(out=Obr[0:N, br].rearrange("p g c -> p (g c)"), in_=ps[7][0:N, 0:260]), r=["ps7"], w=["Obr"])
            prev = None
            last_br = None
            for (br, var, KT_ap, kkeys, bl, V_ap, vkeys) in seq:
                pt_, ptk_ = s_tile(N, h, var, KT_ap, kkeys, bl)
                if prev is not None:
                    flush(prev, br)
                prev = (br, pt_, ptk_, V_ap, vkeys, br != last_br)
                last_br = br
            flush(prev, None)
            combine(N, h)
        P.add("pool", lambda e: e.tensor_copy(out=obf[0:N, :], in_=otok[0:N].rearrange("p h d -> p (h d)")), r=["otok"], w=["obf"])

    def combine(N, h):
        g3 = gate24[0:N, :].rearrange("p (k g b) -> p k b g", k=2, g=4, b=3)[:, h, :, :]

        P.add("dve", lambda e: e.tensor_scalar(out=l3[0:N], in0=Obr[0:N, :, :, 64], scalar1=1e-30, scalar2=None, op0=ALU.max), r=["Obr"], w=["l3"])
        P.add("dve", lambda e: e.reciprocal(out=l3[0:N], in_=l3[0:N]), r=["l3"], w=["l3"])
        P.add("dve", lambda e: e.tensor_tensor(out=coef[0:N], in0=l3[0:N], in1=g3, op=ALU.mult), r=["l3", "gate24"], w=["coef"])
        oh = otok[0:N, 4 * h:4 * h + 4, :]
        P.add("dve", lambda e: e.tensor_tensor(out=oh, in0=Obr[0:N, 0, :, 0:64], in1=coef[0:N, 0, :].unsqueeze(2).to_broadcast([N, 4, 64]), op=ALU.mult),
              r=["Obr", "coef"], w=["otok"])
        for br in (1, 2):
            P.add("dve", lambda e, br=br: e.tensor_tensor(out=otmp[0:N], in0=Obr[0:N, br, :, 0:64], in1=coef[0:N, br, :].unsqueeze(2).to_broadcast([N, 4, 64]), op=ALU.mult),
                  r=["Obr", "coef"], w=["otmp"])
            P.add("dve", lambda e: e.tensor_tensor(out=oh, in0=oh, in1=otmp[0:N], op=ALU.add), r=["otok", "otmp"], w=["otok"])

    ptb = sbA("ptb", [128, 512], I32)
    ptf = ptb[:].bitcast(F32)
    idx = ptb
    riota = sbA("riota", [128, 1], I32)
    riof = sbA("riof", [128, 1])
    pgk = [sbA(f"pgk{i}", [128, 128]) for i in range(2)]
    pgv = [sbA(f"pgv{i}", [128, 128]) for i in range(2)]
    pgkb = [sbA(f"pgkb{i}", [128, 128], BF) for i in range(2)]
    pgvb = [sbA(f"pgvb{i}", [128, 2, 65], BF) for i in range(2)]
    KTp = [sbA(f"KTp{i}", [128, 128], BF) for i in range(2)]
    summTs = sbA("summTs", [128, 2, 256], BF)
    kcTs = sbA("kcTs", [128, 256], BF)
    Vcs = sbA("Vcs", [128, 2, 2, 65], BF)
    madds = sbA("madds", [8, 256])
    mwin = sbA("mwin", [128, 8], BF)
    mnew = sbA("mnew", [32, 32], BF)
    ecs = xbuf[1][0:8, :].rearrange("p (a b) -> p a b", a=4)
    scs = sbA("scs", [8, 256])
    sc2s = sbA("sc2s", [8, 256])
    selbs = sbA("selbs", [8, 256], BF)
    knew = sbA("knew", [32, 2, 128], BF)
    KTnew = sbA("KTnew", [128, 2, 32], BF)
    Vnew = sbA("Vnew", [32, 2, 2, 65], BF)
    for i_ in range(2):
        P.add("pool", lambda e, i_=i_: e.memset(pgvb[i_][:].rearrange("p a b -> p (a b)"), 1.0), w=[f"pgvb{i_}"])
    P.add("pool", lambda e: e.memset(Vcs[:].rearrange("p a b c -> p (a b c)"), 1.0), w=["Vcs"])
    P.add("pool", lambda e: e.memset(Vnew[:].rearrange("p a b c -> p (a b c)"), 1.0), w=["Vnew"])
    ld(ptb[:], pt_d[0:1, :].to_broadcast([128, 512]), "ptb", "ptb")
    ld(madds[:], madds_d[0:1, :].to_broadcast([8, 256]), "madds", "madds")
    ld(mwin[:], mwin_d[:, :], "mwin", "mwin", eng="pool")
    ld(mnew[:], mnew_d[:, :], "mnew", "mnew", eng="pool")
    P.add("pool", lambda e: e.iota(out=riota[:], pattern=[[0, 1]], base=0, channel_multiplier=1), w=["riota"])
    P.add("pool", lambda e: e.tensor_copy(out=riof[:], in_=riota[:]), r=["riota"], w=["riof"])
    P.add("pool", lambda e: e.tensor_copy(out=ptf, in_=ptb[:]), r=["ptb"], w=["ptb"])
    P.add("pool", lambda e: e.tensor_scalar(out=ptf, in0=ptf, scalar1=128.0, scalar2=riof[:, 0:1], op0=ALU.mult, op1=ALU.add), r=["ptb", "riof"], w=["ptb"])
    P.add("pool", lambda e: e.tensor_copy(out=ptb[:], in_=ptf), r=["ptb"], w=["idx"])

    pg_state = {"n": 0}

    def gather(dst, dkey, cache, col):
        P.add("pool", lambda e: e.indirect_dma_start(out=dst[:, :], out_offset=None, in_=cache[:, :],
                                                    in_offset=bass.IndirectOffsetOnAxis(ap=idx[:, col:col + 1], axis=0)),
              r=["idx"], w=[dkey], sem=dkey)

    wkb_t = sbA("wkb_t", [128, 4, 128], BF)
    wvb_t = sbA("wvb_t", [128, 4, 2, 65], BF)
    wKT = sbA("wKT", [128, 4, 128], BF)
    ObrS1 = sbA("ObrS1", [8, 3, 4, 65])
    gate8 = sbA("gate8", [8, 4, 24])
    obf8 = sbA("obf8", [8, 512], BF)
    selbT2 = sbA("selbT2", [128, 2, 2, 8], BF)
    P.add("pool", lambda e: e.memset(wvb_t[:].rearrange("p a b c -> p (a b c)"), 1.0), w=["wvb_t"])

    def sample_attention():
        kvo4 = kvout[0][0:32].rearrange("p (k v c) -> p k v c", k=3, v=2, c=128)
        P.add("pool", lambda e: e.tensor_copy(out=knew[:], in_=kvo4[:, 1:3, 0, :]), r=["kvout0"], w=["knew"])
        P.add("pool", lambda e: e.tensor_copy(out=Vnew[:, :, :, 0:64], in_=kvo4[:, 1:3, 1, :].rearrange("p k (h d) -> p k h d", h=2)), r=["kvout0"], w=["Vnew"])

        def trn(e):
            e.transpose(out=psb(3)[:, 0:32], in_=knew[:, 0, :], identity=identb[0:32, 0:32])
            return e.transpose(out=psb(3)[:, 32:64], in_=knew[:, 1, :], identity=identb[0:32, 0:32])
        P.add("pe", trn, r=["knew", "identb"], w=["ps3"])
        P.add("act", lambda e: e.copy(out=KTnew[:].rearrange("p a b -> p (a b)"), in_=psb(3)[:, 0:64]), r=["ps3"], w=["KTnew"])
        for sq in range(4):
            P.add("sp", lambda e, sq=sq: e.dma_start(out=gate8[:, sq, :], in_=gate24[8 * sq:8 * sq + 8, :]), r=["gate24"], w=["gate8"], sem="gate8")
        for sq in range(4):
            cs0 = 8 * sq

            def s_tile_s(h, var, KT_ap, nkeys, kkeys, bias_list, cs0=cs0):
                hp = slice(64 * h, 64 * h + 64)
                b = 5 + (st_att["s"] % 2)
                pt = PT[st_att["s"] % 2]
                ptk = f"PT{st_att['s'] % 2}"
                st_att["s"] += 1

                def mm(e):
                    last = len(bias_list) == 0
                    ins = e.matmul(ps[b][0:nkeys, 0:32], lhsT=KT_ap, rhs=QT[hp, var * 4:var * 4 + 4, cs0:cs0 + 8], start=True, stop=last)
                    for bi, (l_ap, r_ap) in enumerate(bias_list):
                        ins = e.matmul(ps[b][0:nkeys, 0:32], lhsT=l_ap, rhs=r_ap.unsqueeze(1).to_broadcast([r_ap.shape[0], 4, 8]), start=False,
                                       stop=(bi == len(bias_list) - 1))
                    return ins
                P.add("pe", mm, r=["QT"] + list(kkeys), w=[f"ps{b}"])
                P.add("act", lambda e: e.activation(out=pt[0:nkeys, 0:32], in_=ps[b][0:nkeys, 0:32], func=AF.Exp, scale=SCALE), r=[f"ps{b}"], w=[ptk])
                return pt, ptk

            def pv_s(pt, ptk, nkeys, V_ap, vkeys, first, bank=7):
                def mm(e):
                    if first:
                        e.matmul(ps[bank][0:8, 0:260], lhsT=zeros_bf[:, 0:8], rhs=zeros_bf[:, 0:260], start=True, stop=False, skip_group_check=True)
                    for g in range(4):
                        ins = e.matmul(ps[bank][0:8, g * 65:(g + 1) * 65], lhsT=pt[0:nkeys, g * 8:(g + 1) * 8], rhs=V_ap,
                                       start=False, stop=False, skip_group_check=True)
                    return ins
                P.add("pe", mm, r=[ptk, "zeros_bf"] + list(vkeys), w=[f"ps{bank}"])

            pm1k = pm1[:, 0, :].unsqueeze(1).to_broadcast([128, 2, 64])
            pm1v = pm1[:, 1, :].unsqueeze(1).to_broadcast([128, 2, 64])
            for p in range(128):
                col = sq * 128 + p
                ib_ = pg_state["n"] % 2
                pg_state["n"] += 1
                gather(pgk[ib_], f"pgk{ib_}", ccmpk, col)
                gather(pgv[ib_], f"pgv{ib_}", ccmpv, col)
                P.add("dve", lambda e, ib_=ib_: e.tensor_tensor(out=cmod[:, 0, :].rearrange("p (h d) -> p h d", h=2), in0=pgk[ib_][:, :].rearrange("p (h d) -> p h d", h=2),
                                                             in1=pm1k, op=ALU.mult), r=[f"pgk{ib_}", "pm1"], w=["cmod"])
                P.add("dve", lambda e, ib_=ib_: e.tensor_tensor(out=cmod[:, 1, :].rearrange("p (h d) -> p h d", h=2), in0=pgv[ib_][:, :].rearrange("p (h d) -> p h d", h=2),
                                                             in1=pm1v, op=ALU.mult), r=[f"pgv{ib_}", "pm1"], w=["cmod"])

                def mmc(e):
                    e.matmul(ps[4][:, 0:2], lhsT=cmod[:, 0, :], rhs=sel2[:, :], start=True, stop=True)
                    return e.matmul(ps[4][:, 2:4], lhsT=cmod[:, 1, :], rhs=sel2[:, :], start=True, stop=True)
                P.add("pe", mmc, r=["cmod", "sel2"], w=["ps4"])
                P.add("act", lambda e, p=p: e.copy(out=summTs[:, :, 2 * p:2 * p + 2], in_=ps[4][:, 0:4].rearrange("p (a b) -> p a b", a=2)),
                      r=["ps4"], w=["summTs"])

            def mmk(e):
                e.matmul(ps[4][:, 0:256], lhsT=wkbd[:], rhs=summTs[:, 0, :], start=True, stop=True)
                e.matmul(ps[3][:, 0:128], lhsT=summTs[:, 1, 0:128], rhs=wvbd[:], start=True, stop=True)
                return e.matmul(ps[3][:, 128:256], lhsT=summTs[:, 1, 128:256], rhs=wvbd[:], start=True, stop=True)
            P.add("pe", mmk, r=["summTs", "wkbd", "wvbd"], w=["ps4", "ps3"])
            P.add("act", lambda e: e.copy(out=kcTs[:], in_=ps[4][:, 0:256]), r=["ps4"], w=["kcTs"])
            P.add("act", lambda e: e.copy(out=Vcs[:, :, :, 0:64], in_=ps[3][:, 0:256].rearrange("p (c h d) -> p c h d", c=2, h=2)), r=["ps3"], w=["Vcs"])
            ld(wkb_t[:], cwk_d[sq, :, :].rearrange("(w p) c -> p w c", p=128), "wkb_t", "wkb_t", eng="pool")
            for w_ in range(4):
                ld(wvb_t[:, w_, :, 0:64], cwv_d[sq, w_ * 128:(w_ + 1) * 128, :].rearrange("p (h d) -> p h d", h=2), "wvb_t", "wvb_t", eng="pool")

            def trw(e):
                for w in range(4):
                    ins = e.transpose(out=psb(3)[:, 128 * w:128 * w + 128], in_=wkb_t[:, w, :], identity=identb[:])
                return ins
            P.add("pe", trw, r=["wkb_t", "identb"], w=["ps3"])
            P.add("act", lambda e: e.copy(out=wKT[:].rearrange("p a b -> p (a b)"), in_=psb(3)[:, 0:512]), r=["ps3"], w=["wKT"])

            for h in range(2):
                hp = slice(64 * h, 64 * h + 64)
                def mmA(e, hp=hp, cs0=cs0):
                    for g in range(4):
                        ins = e.matmul(ps[4 - g // 2][0:8, (g % 2) * 256:(g % 2) * 256 + 256], lhsT=QT[hp, g, cs0:cs0 + 8], rhs=kcTs[hp, :], start=True, stop=True)
                    return ins
                P.add("pe", mmA, r=["QT", "kcTs"], w=["ps4", "ps3"])

                def expA(e):
                    for g in range(4):
                        ins = e.activation(out=ecs[:, g, :], in_=ps[4 - g // 2][0:8, (g % 2) * 256:(g % 2) * 256 + 256], func=AF.Exp, scale=SCALE,
                                           accum_out=l4[0:8, g:g + 1])
                    return ins
                P.add("act", expA, r=["ps4", "ps3"], w=["xbuf1", "l4"])
                P.add("dve", lambda e: e.tensor_scalar(out=rl4[0:8, :], in0=l4[0:8, :], scalar1=1e-30, scalar2=None, op0=ALU.max), r=["l4"], w=["rl4"])
                P.add("dve", lambda e: e.reciprocal(out=rl4[0:8, :], in_=rl4[0:8, :]), r=["rl4"], w=["rl4"])
                P.add("dve", lambda e: e.scalar_tensor_tensor(out=scs[:, :], in0=ecs[:, 0, :], scalar=rl4[0:8, 0:1], in1=madds[:, :], op0=ALU.mult, op1=ALU.add),
                      r=["xbuf1", "rl4", "madds"], w=["scs"])
                for g in range(1, 4):
                    P.add("dve", lambda e, g=g: e.scalar_tensor_tensor(out=scs[:, :], in0=ecs[:, g, :], scalar=rl4[0:8, g:g + 1], in1=scs[:, :], op0=ALU.mult, op1=ALU.add),
                          r=["xbuf1", "rl4", "scs"], w=["scs"])
                P.add("dve", lambda e: e.max(out=mx8[0:8, 0:8], in_=scs[:, :]), r=["scs"], w=["mx8a"])
                P.add("dve", lambda e: e.match_replace(out=sc2s[:, :], in_to_replace=mx8[0:8, 0:8], in_values=scs[:, :], imm_value=-1e9), r=["scs", "mx8a"], w=["sc2s"])
                P.add("dve", lambda e: e.max(out=mx8[0:8, 8:16], in_=sc2s[:, :]), r=["sc2s"], w=["mx8b"])
                P.add("dve", lambda e: e.tensor_scalar(out=selbs[:, :], in0=scs[:, :], scalar1=mx8[0:8, 14:15], scalar2=-BIG, op0=ALU.is_lt, op1=ALU.mult),
                      r=["scs", "mx8b"], w=["selbs"])

                def trsb(e):
                    e.transpose(out=psb(3)[:, 0:8], in_=selbs[:, 0:128], identity=identb[0:8, 0:8])
                    return e.transpose(out=psb(3)[:, 8:16], in_=selbs[:, 128:256], identity=identb[0:8, 0:8])
                P.add("pe", trsb, r=["selbs", "identb"], w=["ps3"])
                P.add("act", lambda e, h=h: e.copy(out=selbT2[:, h].rearrange("p a b -> p (a b)"), in_=psb(3)[:, 0:16]), r=["ps3"], w=["selbT2"])
                for c in range(2):
                    pt, ptk = s_tile_s(h, 0, kcTs[hp, c * 128:(c + 1) * 128], 128, ["kcTs"], [])
                    pv_s(pt, ptk, 128, Vcs[:, c, h, :], ["Vcs"], c == 0)
                P.add("act", lambda e, h=h: e.copy(out=(Obr[0:8] if h == 0 else ObrS1[:])[:, 0].rearrange("p g c -> p (g c)"), in_=ps[7][0:8, 0:260]), r=["ps7"], w=["ObrS", "Obr"])
                for w in range(4):
                    bl = [(identb[:, :], mwin[:, :])] if w == 0 else []
                    pt, ptk = s_tile_s(h, 1, wKT[hp, w, :], 128, ["wKT", "mwin"], bl)
                    pv_s(pt, ptk, 128, wvb_t[:, w, h, :], ["wvb_t"], w == 0)
                pt, ptk = s_tile_s(h, 1, KTnew[hp, 1, :], 32, ["KTnew", "mnew"], [(identb[0:32, 0:32], mnew[:, cs0:cs0 + 8])])
                pv_s(pt, ptk, 32, Vnew[:, 1, h, :], ["Vnew"], False)
                P.add("act", lambda e, h=h: e.copy(out=(Obr[0:8] if h == 0 else ObrS1[:])[:, 2].rearrange("p g c -> p (g c)"), in_=ps[7][0:8, 0:260]), r=["ps7"], w=["ObrS", "Obr"])
            pend = [None]
            for p in range(128):
                col = sq * 128 + p
                ib_ = pg_state["n"] % 2
                pg_state["n"] += 1
                gather(pgk[ib_], f"pgk{ib_}", cselk, col)
                gather(pgv[ib_], f"pgv{ib_}", cselv, col)
                P.add("dve", lambda e, ib_=ib_: e.tensor_copy(out=pgkb[ib_][:], in_=pgk[ib_][:]), r=[f"pgk{ib_}"], w=[f"pgkb{ib_}"])
                P.add("pe", lambda e, ib_=ib_: e.transpose(out=psb(3)[:, 256 * ib_:256 * ib_ + 128], in_=pgkb[ib_][:], identity=identb[:]),
                      r=[f"pgkb{ib_}", "identb"], w=["ps3"])
                P.add("act", lambda e, ib_=ib_: e.copy(out=KTp[ib_][:], in_=psb(3)[:, 256 * ib_:256 * ib_ + 128]), r=["ps3"], w=[f"KTp{ib_}"])
                P.add("dve", lambda e, ib_=ib_: e.tensor_copy(out=pgvb[ib_][:, :, 0:64], in_=pgv[ib_][:, :].rearrange("p (h d) -> p h d", h=2)),
                      r=[f"pgv{ib_}"], w=[f"pgvb{ib_}"])
                for h in range(2):
                    hp = slice(64 * h, 64 * h + 64)
                    bl = [(eall[:, (p % 64) * 128:(p % 64 + 1) * 128], selbT2[:, h, p // 64, :])]
                    pt, ptk = s_tile_s(h, 1, KTp[ib_][hp, :], 128, [f"KTp{ib_}", "eall", "selbT2"], bl)
                    if pend[0] is not None:
                        pv_s(*pend[0][0], bank=pend[0][1])
                    pend[0] = ((pt, ptk, 128, pgvb[ib_][:, h, :], [f"pgvb{ib_}"], p == 0), (7 if h == 0 else 2))
            for h in range(2):
                hp = slice(64 * h, 64 * h + 64)
                pt, ptk = s_tile_s(h, 1, KTnew[hp, 0, :], 32, ["KTnew", "mnew"], [(identb[0:32, 0:32], mnew[:, cs0:cs0 + 8])])
                if pend[0] is not None:
                    pv_s(*pend[0][0], bank=pend[0][1])
                pend[0] = ((pt, ptk, 32, Vnew[:, 0, h, :], ["Vnew"], False), (7 if h == 0 else 2))
            pv_s(*pend[0][0], bank=pend[0][1])
            pend[0] = None
            for h in range(2):
                bk = 7 if h == 0 else 2
                P.add("act", lambda e, h=h, bk=bk: e.copy(out=(Obr[0:8] if h == 0 else ObrS1[:])[:, 1].rearrange("p g c -> p (g c)"), in_=ps[bk][0:8, 0:260]), r=[f"ps{bk}"], w=["ObrS", "Obr"])
            for h in range(2):
                combine_s(sq, h)
            P.add("sp", lambda e, sq=sq: e.dma_start(out=o_d[NT, 8 * sq:8 * sq + 8, :], in_=obf8[:, :]), r=["obf8"], w=[f"o_d{NT}"], sem="st_obf8")

    def combine_s(sq, h):
        g3 = gate8[:, sq, :].rearrange("p (k g b) -> p k b g", k=2, g=4, b=3)[:, h, :, :]
        Ob = Obr[0:8] if h == 0 else ObrS1[:]
        P.add("dve", lambda e: e.tensor_scalar(out=l3[0:8], in0=Ob[:, :, :, 64], scalar1=1e-30, scalar2=None, op0=ALU.max), r=["ObrS", "Obr"], w=["l3"])
        P.add("dve", lambda e: e.reciprocal(out=l3[0:8], in_=l3[0:8]), r=["l3"], w=["l3"])
        P.add("dve", lambda e: e.tensor_tensor(out=coef[0:8], in0=l3[0:8], in1=g3, op=ALU.mult), r=["l3", "gate8"], w=["coef"])
        oh = otok[0:8, 4 * h:4 * h + 4, :]
        P.add("dve", lambda e: e.tensor_tensor(out=oh, in0=Ob[:, 0, :, 0:64], in1=coef[0:8, 0, :].unsqueeze(2).to_broadcast([8, 4, 64]), op=ALU.mult),
              r=["ObrS", "Obr", "coef"], w=["otok"])
        for br in (1, 2):
            P.add("dve", lambda e, br=br: e.tensor_tensor(out=otmp[0:8], in0=Ob[:, br, :, 0:64], in1=coef[0:8, br, :].unsqueeze(2).to_broadcast([8, 4, 64]), op=ALU.mult),
                  r=["ObrS", "Obr", "coef"], w=["otmp"])
            P.add("dve", lambda e: e.tensor_tensor(out=oh, in0=oh, in1=otmp[0:8], op=ALU.add), r=["otok", "otmp"], w=["otok"])
        if h == 1:
            P.add("pool", lambda e: e.tensor_copy(out=obf8[:, :], in_=otok[0:8].rearrange("p h d -> p (h d)")), r=["otok"], w=["obf8"])

    f1(xs_d[:, :], cs_s[:, :], 32, A1s, "A1s", selSb, "selSb", None, None, skv_o[:, :], 0,
       glu_dst=[(sconv_o[sq, 22:30, :], 8 * sq, 8 * sq + 8) for sq in range(4)])
    for sq in range(4):
        ld(bnc_c[0:22, :], sconv_d[sq, 8:30, :], "ecmp", "bnc_c")
        P.add("sp", lambda e, sq=sq: e.dma_start(out=sconv_o[sq, 0:22, :], in_=bnc_c[0:22, :]), r=["ecmp"], w=[], sem="st_bc")
        for kv in range(2):
            src_c = (cwk_d, cwv_d)[kv]
            ld(bnc_w[0:126, :, :], src_c[sq, 8:512, :].rearrange("(p a) c -> p a c", a=4), "sbias", "bnc_w")
            P.add("sp", lambda e, sq=sq, kv=kv: e.dma_start(out=swin_o[sq, kv, 0:504, :].rearrange("(p a) c -> p a c", a=4), in_=bnc_w[0:126, :, :]),
                  r=["sbias"], w=[], sem="st_bw")
            P.add("sp", lambda e, sq=sq, kv=kv: e.dma_start(out=swin_o[sq, kv, 504:512, :], in_=kvout[0][8 * sq:8 * sq + 8, 512 + 128 * kv:640 + 128 * kv]),
                  r=["kvout0"], w=[], sem="st_kvout0")

    if do_sample:
        qfront(32, selSb, "selSb", csb[(state["xi"] - 1) % 2], f"csb{(state['xi'] - 1) % 2}")
        sample_attention()

    for i in range(n_own):
        f1(xt[i, :, :], cs_t[i, :, :], 128, A1, "A1", selPb, "selPb", 32 + i, 3 + (i % 3), None, 1)
        last = (i == NT - 1)
        ib = f1(xo[i, :, :], cs_o[i, :, :], 128, A1, "A1", selPb, "selPb", i, i % 3, kv_o[i, :, :], i,
                glu_dst=[(conv_o[:, :], 98, 128)] if last else None)
        update_cmp()
        qfront(128, selPb, "selPb", csb[ib], f"csb{ib}")
        attention(i)
        P.add("sp", lambda e, i=i: e.dma_start(out=o_d[i, :, :], in_=obf[:, :]), r=["obf"], w=[f"o_d{i}"], sem="st_obf")

    esA.close()
    P.barrier(lambda e: e.memset(dummy[:], 2.0))

    esB = ExitStack()

    def sbB(name, shape, dt=F32):
        return esB.enter_context(nc.sbuf_tensor("sB_" + name, list(shape), dt))
    w_gluB = sbB("w_gluB", [128, 8, 1024], BF)
    for kc in range(8):
        ld(w_gluB[:, kc, :], w_in_v[:, kc, 0:1024], "w_gluB", "wb4", eng="pool")
    w_mg_bf = sbB("w_mg_bf", [128, 8, 2048], BF)
    w_pw2_bf = sbB("w_pw2_bf", [128, 4, D], BF)
    w_nso_bf = sbB("w_nso_bf", [128, 4, D], BF)
    w_out_bf = sbB("w_out_bf", [128, 8, D], BF)
    for kc in range(8):
        ld(w_mg_bf[:, kc, :], w_in_v[:, kc, OFF_MG:D_IN], "w_mg_bf", "wb0", eng="pool")
        ld(w_out_bf[:, kc, :], w_out.rearrange("(kc p) n -> p kc n", p=128)[:, kc, :], "w_out_bf", "wb3", eng="pool")
    for kc in range(4):
        ld(w_pw2_bf[:, kc, :], w_pw2.rearrange("(kc p) n -> p kc n", p=128)[:, kc, :], "w_pw2_bf", "wb1", eng="pool")
        ld(w_nso_bf[:, kc, :], w_nsa_o.rearrange("(kc p) n -> p kc n", p=128)[:, kc, :], "w_nso_bf", "wb2", eng="pool")
    diag = sbB("diag", [128, 4, CONV_K, 128], BF)
    ones_bf = sbB("ones_bf", [128, 128], BF)
    P.add("pool", lambda e: e.memset(ones_bf[:], 1.0), w=["ones_bf"])
    for c in range(4):
        for k in range(CONV_K):
            P.add("pool", lambda e, c=c, k=k: e.tensor_scalar(out=diag[:, c, k, :], in0=identb[:, :], scalar1=wdwT[:, c, k:k + 1], scalar2=1.0,
                                                            op0=ALU.mult, op1=ALU.mult), r=["identb", "wdwT"], w=["diag"])
    xhb = sbB("xhb", [32, D])
    ssh = sbB("ssh", [32, 1])
    rrh = sbB("rrh", [32, 1])
    hbfh = sbB("hbfh", [32, D], BF)
    hTh = sbB("hTh", [128, 8, 32], BF)
    sgh = sbB("sgh", [128, 128])
    uext = sbB("uext", [128, 4, 160], BF)
    sg = sbB("sg", [128, 512])
    ycv = sbB("ycv", [128, 4, 128])
    ybf = sbB("ybf", [128, 4, 128], BF)
    ysq = sbB("ysq", [128, 4, 128], BF)
    mu = sbB("mu", [128, 128])
    msq = sbB("msq", [128, 128])
    rstd = sbB("rstd", [128, 128])
    tn = sbB("tn", [128, 4, 128])
    zf = sbB("zf", [128, 4, 128])
    ysl = sbB("ysl", [128, 4, 128], BF)
    ob = sbB("ob", [128, 512], BF)
    oT = sbB("oT", [128, 4, 128], BF)
    gcn = sbB("gcn", [128, 8, 128])
    m1 = sbB("m1", [128, 8, 128])
    mT = sbB("mT", [128, 8, 128], BF)
    x1t = sbB("x1t", [128, D])
    sct = utok

    def featproj(bank0, nchunk, N, wt, wk, wcol0, zcol0, selb_, selk, rhsT, rkey, nk=8, bias=True):
        for b0 in range(0, nchunk, 4):
            bank = bank0 + b0 // 4

            def mm(e, b0=b0, bank=bank):
                for c in range(b0, min(b0 + 4, nchunk)):
                    out = ps[bank][:, (c % 4) * N:(c % 4 + 1) * N]
                    for kc in range(nk):
                        ins = e.matmul(out, lhsT=wt[:, kc, wcol0 + c * 128:wcol0 + (c + 1) * 128], rhs=rhsT[:, kc, 0:N], start=(kc == 0),
                                       stop=(kc == nk - 1 and not bias))
                    if bias:
                        ins = e.matmul(out, lhsT=zb1[:, zcol0 + c * 128:zcol0 + (c + 1) * 128], rhs=selb_[:, 0:N], start=False, stop=True)
                return ins
            P.add("pe", mm, r=[wk, rkey, "zb1", selk], w=[f"ps{bank}"])

    def sigmoid_from(dst_ap, src_ap, rk_, wk_):
        P.add("act", lambda e: e.activation(out=dst_ap, in_=src_ap, func=AF.Tanh, scale=0.5), r=[rk_], w=[wk_])
        P.add("dve", lambda e: e.tensor_scalar(out=dst_ap, in0=dst_ap, scalar1=0.5, scalar2=0.5, op0=ALU.mult, op1=ALU.add), r=[wk_], w=[wk_])

    def passB(idx, N, xsrc, At, Ak, Gt, Gk, selb_, selk, segs, halo_kind, o_src, x1_dst):
        ib = front_norm(xsrc, cs_o[0, 0:N, :], N, At, Ak)
        xb = xbuf[ib]
        xk = f"xbuf{ib}"
        nseg = len(segs)
        L = segs[0][1]
        uv = uext[:, :, 0:nseg * (30 + L)].rearrange("p c (s t) -> p c s t", s=nseg)
        if halo_kind == "x":
            ld(xhb[:], xh[idx, :, :], "xhb", "xhb")
            P.add("act", lambda e: e.activation(out=junk[0:32, :], in_=xhb[:], func=AF.Square, accum_out=ssh[:]), r=["xhb"], w=["junk", "ssh"])
            rsqrt_pool(rrh[:], ssh[:], 1.0 / D, 1, "ssh", "rrh")
            P.add("dve", lambda e: e.scalar_tensor_tensor(out=hbfh[:], in0=xhb[:], scalar=rrh[:, 0:1], in1=At[0:32, :], op0=ALU.mult, op1=ALU.mult),
                  r=["xhb", "rrh", Ak], w=["hbfh"])

            def trh(e):
                for kc in range(8):
                    ins = e.transpose(out=psb(4)[:, kc * 32:(kc + 1) * 32], in_=hbfh[:, kc * 128:(kc + 1) * 128], identity=identb[0:32, 0:32])
                return ins
            P.add("pe", trh, r=["hbfh", "identb"], w=["ps4"])
            P.add("act", lambda e: e.copy(out=hTh[:].rearrange("p a b -> p (a b)"), in_=psb(4)[:, 0:256]), r=["ps4"], w=["hTh"])
            featproj(3, 4, 32, w_gluB, "w_gluB", 0, 0, selb_, selk, hTh, "hTh")
            featproj(4, 4, 32, w_gluB, "w_gluB", 512, 512, selb_, selk, hTh, "hTh")
            sigmoid_from(sgh[:, :], ps[4][:, 0:128], "ps4", "sgh")
            P.add("dve", lambda e: e.tensor_tensor(out=sgh[:, :], in0=sgh[:, :], in1=ps[3][:, 0:128], op=ALU.mult), r=["sgh", "ps3"], w=["sgh"])
            P.add("dve", lambda e: e.tensor_scalar(out=uv[:, :, 0, 0:30], in0=sgh[:, :].rearrange("p (c t) -> p c t", c=4)[:, :, 2:32],
                                                  scalar1=hv_sb[:, idx:idx + 1], scalar2=None, op0=ALU.mult), r=["sgh", "hv"], w=["uext"])
        else:
            for sq in range(nseg):
                ld(sct[0:30, :], sconv_d[sq, :, :], "utok", "utok")

                def trs(e):
                    for c in range(4):
                        ins = e.transpose(out=ps[4][:, c * 32:c * 32 + 30], in_=sct[0:30, c * 128:(c + 1) * 128], identity=identf[0:30, 0:30])
                    return ins
                P.add("pe", trs, r=["utok", "identf"], w=["ps4"])
                P.add("act", lambda e, sq=sq: e.copy(out=uv[:, :, sq, 0:30], in_=ps[4][:, 0:128].rearrange("p (c t) -> p c t", c=4)[:, :, 0:30]),
                      r=["ps4"], w=["uext"])
        featproj(1, 4, N, w_gluB, "w_gluB", 0, 0, selb_, selk, hT, "hT")
        featproj(2, 4, N, w_gluB, "w_gluB", 512, 512, selb_, selk, hT, "hT")
        sigmoid_from(sg[:, 0:4 * N], ps[2][:, 0:4 * N], "ps2", "sg")
        for sq, (t0, Ls) in enumerate(segs):
            P.add("dve", lambda e, sq=sq, t0=t0, Ls=Ls: e.tensor_tensor(
                out=uv[:, :, sq, 30:30 + Ls], in0=sg[:, 0:4 * N].rearrange("p (c t) -> p c t", c=4)[:, :, t0:t0 + Ls],
                in1=ps[1][:, 0:4 * N].rearrange("p (c t) -> p c t", c=4)[:, :, t0:t0 + Ls], op=ALU.mult), r=["sg", "ps1"], w=["uext"])

        def conv(e):
            for c in range(4):
                for sq, (t0, Ls) in enumerate(segs):
                    for k in range(CONV_K):
                        ins = e.matmul(ps[3][:, c * N + t0:c * N + t0 + Ls], lhsT=diag[:, c, k, :], rhs=uv[:, c, sq, k:k + Ls],
                                       start=(k == 0), stop=(k == CONV_K - 1))
            return ins
        P.add("pe", conv, r=["diag", "uext"], w=["ps3"])
        for c in range(4):
            P.add("act", lambda e, c=c: e.activation(out=ycv[:, c, 0:N], in_=ps[3][:, c * N:(c + 1) * N], func=AF.Identity, bias=cvec[:, 0, c:c + 1]),
                  r=["ps3", "cvec"], w=["ycv"])
        P.add("pool", lambda e: e.tensor_copy(out=ybf[:, :, 0:N], in_=ycv[:, :, 0:N]), r=["ycv"], w=["ybf"])
        P.add("act", lambda e: e.activation(out=ysq[:, :, 0:N], in_=ycv[:, :, 0:N], func=AF.Square), r=["ycv"], w=["ysq"])

        def stats(e):
            for c in range(4):
                e.matmul(ps[4][:, 0:N], lhsT=ones_bf[:, :], rhs=ybf[:, c, 0:N], start=(c == 0), stop=(c == 3))
            for c in range(4):
                ins = e.matmul(ps[4][:, N:2 * N], lhsT=ones_bf[:, :], rhs=ysq[:, c, 0:N], start=(c == 0), stop=(c == 3))
            return ins
        P.add("pe", stats, r=["ones_bf", "ybf", "ysq"], w=["ps4"])
        P.add("act", lambda e: e.activation(out=mu[:, 0:N], in_=ps[4][:, 0:N], func=AF.Copy, scale=1.0 / 512), r=["ps4"], w=["mu"])
        P.add("act", lambda e: e.activation(out=msq[:, 0:N], in_=ps[4][:, N:2 * N], func=AF.Copy, scale=1.0 / 512), r=["ps4"], w=["msq"])
        P.add("dve", lambda e: e.tensor_tensor(out=rstd[:, 0:N], in0=mu[:, 0:N], in1=mu[:, 0:N], op=ALU.mult), r=["mu"], w=["rstd"])
        P.add("dve", lambda e: e.tensor_tensor(out=rstd[:, 0:N], in0=msq[:, 0:N], in1=rstd[:, 0:N], op=ALU.subtract), r=["msq", "rstd"], w=["rstd"])
        rsqrt_pool(rstd[:, 0:N], rstd[:, 0:N], 1.0, N, "rstd", "rstd")
        P.add("dve", lambda e: e.tensor_tensor(out=tn[:, :, 0:N], in0=ycv[:, :, 0:N], in1=mu[:, 0:N].unsqueeze(1).to_broadcast([128, 4, N]), op=ALU.subtract),
              r=["ycv", "mu"], w=["tn"])
        P.add("dve", lambda e: e.tensor_tensor(out=tn[:, :, 0:N], in0=tn[:, :, 0:N], in1=rstd[:, 0:N].unsqueeze(1).to_broadcast([128, 4, N]), op=ALU.mult),
              r=["tn", "rstd"], w=["tn"])
        for c in range(4):
            P.add("act", lambda e, c=c: e.activation(out=zf[:, c, 0:N], in_=tn[:, c, 0:N], func=AF.Identity, scale=cvec[:, 1, c:c + 1], bias=cvec[:, 2, c:c + 1]),
                  r=["tn", "cvec"], w=["zf"])
        sigmoid_from(tn[:, :, 0:N], zf[:, :, 0:N], "zf", "tn")
        P.add("dve", lambda e: e.tensor_tensor(out=ysl[:, :, 0:N], in0=tn[:, :, 0:N], in1=zf[:, :, 0:N], op=ALU.mult), r=["tn", "zf"], w=["ysl"])
        featproj(5, 8, N, w_pw2_bf, "w_pw2_bf", 0, 0, selb_, selk, ysl, "ysl", nk=4, bias=False)
        featproj(1, 8, N, w_mg_bf, "w_mg_bf", 0, OFF_MG, selb_, selk, hT, "hT")
        for hb in range(2):
            sigmoid_from(gcn[:, 4 * hb:4 * hb + 4, 0:N], ps[1 + hb][:, 0:4 * N].rearrange("p (c t) -> p c t", c=4), f"ps{1 + hb}", "gcn")
            P.add("dve", lambda e, hb=hb: e.tensor_tensor(out=m1[:, 4 * hb:4 * hb + 4, 0:N], in0=gcn[:, 4 * hb:4 * hb + 4, 0:N],
                                                       in1=ps[5 + hb][:, 0:4 * N].rearrange("p (c t) -> p c t", c=4), op=ALU.mult),
                  r=["gcn", f"ps{5 + hb}"], w=["m1"])
        P.add("sp", lambda e: e.dma_start(out=ob[0:N, :], in_=o_src), r=[f"o_d{idx}"], w=["ob"], sem="ob")

        def tro(e):
            for kc in range(4):
                ins = e.transpose(out=psb(3)[:, kc * 128:kc * 128 + N], in_=ob[0:N, kc * 128:(kc + 1) * 128], identity=identb[0:N, 0:N])
            return ins
        P.add("pe", tro, r=["ob", "identb"], w=["ps3"])
        P.add("act", lambda e: e.copy(out=oT[:, :, 0:N], in_=psb(3)[:, 0:512].rearrange("p (a b) -> p a b", a=4)[:, :, 0:N]), r=["ps3"], w=["oT"])
        featproj(5, 8, N, w_nso_bf, "w_nso_bf", 0, 0, selb_, selk, oT, "oT", nk=4, bias=False)
        featproj(1, 8, N, w_mg_bf, "w_mg_bf", 1024, OFF_MG + 1024, selb_, selk, hT, "hT")
        for hb in range(2):
            sigmoid_from(gcn[:, 4 * hb:4 * hb + 4, 0:N], ps[1 + hb][:, 0:4 * N].rearrange("p (c t) -> p c t", c=4), f"ps{1 + hb}", "gcn")
            P.add("dve", lambda e, hb=hb: e.tensor_tensor(out=gcn[:, 4 * hb:4 * hb + 4, 0:N], in0=gcn[:, 4 * hb:4 * hb + 4, 0:N],
                                                       in1=ps[5 + hb][:, 0:4 * N].rearrange("p (c t) -> p c t", c=4), op=ALU.mult),
                  r=["gcn", f"ps{5 + hb}"], w=["gcn"])
        P.add("dve", lambda e: e.tensor_tensor(out=mT[:, :, 0:N], in0=m1[:, :, 0:N], in1=gcn[:, :, 0:N], op=ALU.add), r=["m1", "gcn"], w=["mT"])
        for half in range(2):
            def mmo(e, half=half):
                for kc in range(8):
                    ins = e.matmul(ps[5 + half][0:N, :], lhsT=mT[:, kc, 0:N], rhs=w_out_bf[:, kc, half * 512:(half + 1) * 512], start=(kc == 0), stop=(kc == 7))
                return ins
            P.add("pe", mmo, r=["mT", "w_out_bf"], w=[f"ps{5 + half}"])
            P.add("dve", lambda e, half=half: e.tensor_tensor(out=x1t[0:N, half * 512:(half + 1) * 512], in0=ps[5 + half][0:N, :],
                                                           in1=Gt[0:N, half * 512:(half + 1) * 512], op=ALU.mult), r=[f"ps{5 + half}", Gk], w=["x1t"])
        P.add("dve", lambda e: e.tensor_tensor(out=x1t[0:N, :], in0=x1t[0:N, :], in1=xb[0:N, :], op=ALU.add), r=["x1t", xk], w=["x1t"])
        P.add("sp", lambda e: e.dma_start(out=x1_dst, in_=x1t[0:N, :]), r=["x1t"], w=[f"x1d{idx}"], sem="st_x1")

    for i in range(n_own):
        passB(i, 128, xo[i, :, :], A1, "A1", G1, "G1", selPb, "selPb", [(0, 128)], "x", o_d[i, :, :], x1_d[i, :, :])
    if do_sample:
        passB(NT, 32, xs_d[:, :], A1s, "A1s", G1s, "G1s", selSb, "selSb", [(8 * q_, 8) for q_ in range(4)], "s", o_d[NT, 0:32, :], x1_d[NT, 0:32, :])
    esB.close()
    P.barrier(lambda e: e.memset(dummy[:], 3.0))

    esC = ExitStack()

    def sbC(name, shape, dt=F32):
        return esC.enter_context(nc.sbuf_tensor("sC_" + name, list(shape), dt))
    w_g_bf = sbC("w_g_bf", [128, 8, D_FF], BF)
    w_u_bf = sbC("w_u_bf", [128, 8, D_FF], BF)
    w_d_bf = sbC("w_d_bf", [128, NFF, D], BF)
    for kc in range(8):
        for c0, c1 in ((0, 2048), (2048, D_FF)):
            ld(w_g_bf[:, kc, c0:c1], w_gate.rearrange("(kc p) n -> p kc n", p=128)[:, kc, c0:c1], "w_g_bf", "wc0", eng="pool")
            ld(w_u_bf[:, kc, c0:c1], w_up.rearrange("(kc p) n -> p kc n", p=128)[:, kc, c0:c1], "w_u_bf", "wc1", eng="pool")
    for fc in range(NFF):
        ld(w_d_bf[:, fc, :], w_down.rearrange("(fc p) n -> p fc n", p=128)[:, fc, :], "w_d_bf", "wc2", eng="pool")
    zbg = zb1
    zbu = sbC("zbu", [5, D_FF], BF)
    A2, G2, A2s, G2s = A1, G1, A1s, G1s
    sgc = utok
    actT = sbC("actT", [128, NFF, 128], BF)
    ybuf = sbC("ybuf", [128, D])
    for wt, wk, zb, zk in ((w_g_bf, "w_g_bf", zbg, "zb1"), (w_u_bf, "w_u_bf", zbu, "zbu")):
        for c0 in range(0, D_FF, 512):
            c1 = min(c0 + 512, D_FF)

            def mm(e, wt=wt, c0=c0, c1=c1):
                for kc in range(8):
                    ins = e.matmul(ps[3][0:5, 0:c1 - c0], lhsT=sh2T[:, kc, :], rhs=wt[:, kc, c0:c1], start=(kc == 0), stop=(kc == 7))
                return ins
            P.add("pe", mm, r=["sh2T", wk], w=["ps3"])
            P.add("act", lambda e, zb=zb, c0=c0, c1=c1: e.copy(out=zb[:, c0:c1], in_=ps[3][0:5, 0:c1 - c0]), r=["ps3"], w=[zk])
    for dst, key, src, skey, sl, slk, n in ((A2, "A1", a2row, "a2row", selP, "selP", 128), (G2, "G1", g2row, "g2row", selP, "selP", 128),
                                            (A2s, "A1s", a2row, "a2row", selS, "selS", 32), (G2s, "G1s", g2row, "g2row", selS, "selS", 32)):
        for half in range(2):
            P.add("pe", lambda e, src=src, half=half, sl=sl, n=n: e.matmul(ps[1][0:n, :], lhsT=sl[:, 0:n], rhs=src[:, half * 512:(half + 1) * 512], start=True, stop=True),
                  r=[slk, skey], w=["ps1"])
            P.add("act", lambda e, dst=dst, half=half, n=n: e.copy(out=dst[:, half * 512:(half + 1) * 512], in_=ps[1][0:n, :]), r=["ps1"], w=[key])

    def passC(idx, N, xsrc, At, Ak, Gt, Gk, selb_, selk, y_dst):
        ib = front_norm(xsrc, cs_o[0, 0:N, :], N, At, Ak, src_key=f"x1d{idx}")
        xb = xbuf[ib]
        xk = f"xbuf{ib}"
        for gi, f0 in enumerate(range(0, NFF, 4)):
            nf = min(4, NFF - f0)
            bg, bu = (1, 2) if gi % 2 == 0 else (3, 4)
            for bank, wt, wk, zb, zk in ((bg, w_g_bf, "w_g_bf", zbg, "zb1"), (bu, w_u_bf, "w_u_bf", zbu, "zbu")):
                def mm(e, bank=bank, wt=wt, zb=zb, f0=f0, nf=nf):
                    for j in range(nf):
                        out = ps[bank][:, j * N:(j + 1) * N]
                        col = (f0 + j) * 128
                        for kc in range(8):
                            e.matmul(out, lhsT=wt[:, kc, col:col + 128], rhs=hT[:, kc, 0:N], start=(kc == 0), stop=False)
                        ins = e.matmul(out, lhsT=zb[:, col:col + 128], rhs=selb_[:, 0:N], start=False, stop=True)
                    return ins
                P.add("pe", mm, r=[wk, zk, "hT", selk], w=[f"ps{bank}"])
            sigmoid_from(sgc[:, 0:nf * N], ps[bg][:, 0:nf * N], f"ps{bg}", "utok")
            P.add("dve", lambda e, bg=bg, nf=nf: e.tensor_tensor(out=sgc[:, 0:nf * N], in0=sgc[:, 0:nf * N], in1=ps[bg][:, 0:nf * N], op=ALU.mult),
                  r=["utok", f"ps{bg}"], w=["utok"])
            P.add("dve", lambda e, bu=bu, nf=nf, f0=f0: e.tensor_tensor(out=actT[:, f0:f0 + nf, 0:N], in0=sgc[:, 0:nf * N].rearrange("p (c t) -> p c t", c=nf),
                                                                      in1=ps[bu][:, 0:nf * N].rearrange("p (c t) -> p c t", c=nf), op=ALU.mult),
                  r=["utok", f"ps{bu}"], w=["actT"])
        for half in range(2):
            def mmd(e, half=half):
                for fc in range(NFF):
                    ins = e.matmul(ps[5 + half][0:N, :], lhsT=actT[:, fc, 0:N], rhs=w_d_bf[:, fc, half * 512:(half + 1) * 512], start=(fc == 0), stop=(fc == NFF - 1))
                return ins
            P.add("pe", mmd, r=["actT", "w_d_bf"], w=[f"ps{5 + half}"])
            P.add("dve", lambda e, half=half: e.tensor_tensor(out=ybuf[0:N, half * 512:(half + 1) * 512], in0=ps[5 + half][0:N, :],
                                                           in1=Gt[0:N, half * 512:(half + 1) * 512], op=ALU.mult), r=[f"ps{5 + half}", Gk], w=["ybuf"])
        P.add("dve", lambda e: e.tensor_tensor(out=ybuf[0:N, :], in0=ybuf[0:N, :], in1=xb[0:N, :], op=ALU.add), r=["ybuf", xk], w=["ybuf"])
        P.add("sp", lambda e: e.dma_start(out=y_dst, in_=ybuf[0:N, :]), r=["ybuf"], w=[], sem="st_y")

    for i in range(n_own):
        ld(dummy[:], dummy[:], "dummy", "dmy") if False else None
        passC(i, 128, x1_d[i, :, :], A2, "A1", G2, "G1", selPb, "selPb", y_o[i, :, :])
    if do_sample:
        passC(NT, 32, x1_d[NT, 0:32, :], A2s, "A1s", G2s, "G1s", selSb, "selSb", ys_o[:, :])
    esC.close()

    final = [k for k in P.dma_sems if k.startswith("st_")]
    P.emit(final_sems=final)
    es.close()
    return nc


def _rope_table(pos):
    half = 32
    inv = np.power(np.float32(10000.0), -np.arange(half, dtype=np.float32) / np.float32(half)).astype(np.float32)
    ang = pos.astype(np.float32)[:, None] * inv[None, :]
    return np.concatenate([np.cos(ang), np.sin(ang)], axis=-1).astype(np.float32)


def _host_prep(inp, n_own=NT):
    bf = ml_dtypes.bfloat16
    f32 = np.float32
    common = {}
    common["identf"] = np.eye(128, dtype=f32)
    e = np.zeros((128, 64, 128), f32)
    for kt in range(64):
        e[2 * kt, kt, 0:64] = 1.0
        e[2 * kt + 1, kt, 64:128] = 1.0
    common["eall"] = e.reshape(128, 64 * 128)
    s2 = np.zeros((128, 2), f32)
    s2[0:64, 0] = 1.0 / 64
    s2[64:128, 1] = 1.0 / 64
    common["sel2"] = s2
    sp = np.zeros((5, 128), f32)
    sp[0, :] = 1.0
    common["selP"] = sp
    ssel = np.zeros((5, 32), f32)
    for q in range(4):
        ssel[1 + q, 8 * q:8 * q + 8] = 1.0
    common["selS"] = ssel
    for k in ("cache_cmp_k", "cache_cmp_v", "cache_sel_k", "cache_sel_v"):
        common[k] = inp[k].reshape(5120 * 128, 128)
    mas = np.zeros((1, 256), f32)
    mas[0, 0] = 8.0
    mas[0, 255] = 8.0
    common["madd_s"] = mas
    rr_ = np.arange(128)
    common["mwin_s"] = np.where(rr_[:, None] > np.arange(8)[None, :], 0.0, -BIG).astype(f32)
    mn = np.full((32, 32), -BIG, f32)
    for q in range(4):
        for t in range(8):
            mn[8 * q:8 * q + t + 1, 8 * q + t] = 0.0
    common["mnew_s"] = mn
    common["cs_s"] = np.tile(_rope_table(16384 + np.arange(8)), (4, 1))
    for k in ("w_ada", "b_ada", "mix_norm_g", "ffn_norm_g", "w_in", "q_norm_g", "w_pw2", "w_nsa_o", "w_out", "w_gate", "w_up", "w_down"):
        common[k] = np.ascontiguousarray(inp[k][0]) if inp[k].ndim == 3 else np.ascontiguousarray(inp[k])
    common["k_norm_g"] = np.ascontiguousarray(inp["k_norm_g"][0].reshape(1, 192))
    common["wdwT"] = np.ascontiguousarray(inp["w_dw"][0].T.reshape(4, 128, CONV_K).transpose(1, 0, 2))
    cv = np.stack([inp["b_dw"][0], inp["conv_ln_g"][0], inp["conv_ln_b"][0]], 0)
    common["cvec"] = np.ascontiguousarray(cv.reshape(3, 4, 128).transpose(2, 0, 1))
    common["modk"] = np.ascontiguousarray(np.tile(inp["cmp_mod_k"][0], (2, 1)))
    common["modv"] = np.ascontiguousarray(np.tile(inp["cmp_mod_v"][0], (2, 1)))
    for nm, src in (("wkbd", "cmp_w_k"), ("wvbd", "cmp_w_v")):
        m = np.zeros((128, 128), f32)
        m[0:64, 0:64] = inp[src][0]
        m[64:128, 64:128] = inp[src][0]
        common[nm] = m
    r = np.arange(128)
    tri = np.where(r[:, None] <= r[None, :], 0.0, -BIG).astype(f32)
    winlo = np.where(r[:, None] > r[None, :], 0.0, -BIG).astype(f32)
    maps = []
    for c in range(8):
        b, par = c // 2, c % 2
        m = dict(common)
        xp = inp["x_prompt"][b].reshape(64, 128, D)
        own = np.arange(NT) * 2 + par
        oth = np.arange(NT) * 2 + (1 - par)
        m["xo"] = np.ascontiguousarray(xp[own])
        m["xt"] = np.ascontiguousarray(xp[oth])
        xflat = inp["x_prompt"][b]
        xhh = np.zeros((NT, 32, D), f32)
        hvv = np.zeros((128, NT), f32)
        for i in range(NT):
            s0 = own[i] * 128
            if s0 > 0:
                xhh[i] = xflat[s0 - 32:s0]
                hvv[:, i] = 1.0
        m["xh"] = xhh
        m["hv"] = hvv
        pos_o = (own[:, None] * 128 + r[None, :]).reshape(-1)
        pos_t = (oth[:, None] * 128 + r[None, :]).reshape(-1)
        m["cs_o"] = _rope_table(pos_o).reshape(NT, 128, 64)
        m["cs_t"] = _rope_table(pos_t).reshape(NT, 128, 64)
        gb = np.zeros(128, np.int64)
        for i in range(NT):
            gb[2 * i], gb[2 * i + 1] = 2 * own[i], 2 * own[i] + 1
            gb[64 + 2 * i], gb[64 + 2 * i + 1] = 2 * oth[i], 2 * oth[i] + 1
        mc = np.zeros((NT, 128, 128), f32)
        ma = np.zeros((NT, 128, 128), f32)
        for i in range(NT):
            qpos = own[i] * 128 + r
            cur = qpos // 64
            ok = (gb[None, :] + 1) * 64 <= qpos[:, None] + 1
            mc[i] = np.where(ok, 0.0, -BIG)
            vis = gb[None, :] <= cur[:, None]
            forced = (gb[None, :] == 0) | (gb[None, :] == cur[:, None]) | (gb[None, :] == cur[:, None] - 1)
            ma[i] = np.where(vis, np.where(forced, 8.0, 0.0), -1.0)
        m["m_cmp"] = mc
        m["m_add"] = ma
        m["m_cmpT"] = np.ascontiguousarray(mc.transpose(0, 2, 1))
        full = np.zeros((128, 128), f32)
        none = np.full((128, 128), -BIG, f32)
        m["masks"] = np.ascontiguousarray(np.stack([tri, winlo, none if par == 0 else full, full if par == 0 else none], 1))
        m["xs"] = np.ascontiguousarray(inp["x_sample"][4 * c:4 * c + 4].reshape(32, D))
        m["cwk"] = np.ascontiguousarray(inp["cache_win_k"][0, 4 * c:4 * c + 4].reshape(4, 512, 128))
        m["cwv"] = np.ascontiguousarray(inp["cache_win_v"][0, 4 * c:4 * c + 4].reshape(4, 512, 128))
        m["sconv"] = np.ascontiguousarray(inp["state_conv"][0, 4 * c:4 * c + 4])
        m["pt"] = np.ascontiguousarray(inp["page_table"][4 * c:4 * c + 4].reshape(1, 512).astype(np.int32))
        cs = inp["c_sample"][4 * c:4 * c + 4]
        c5 = np.concatenate([inp["c_prompt"][b:b + 1], cs], 0)
        m["c5T"] = np.ascontiguousarray(c5.T.reshape(8, 128, 5).transpose(1, 0, 2).reshape(128, 40))
        maps.append(m)
    return maps


_CACHE = {}


def kernel(**inp):
    inp = {k: np.asarray(v) for k, v in inp.items()}
    if "nc" not in _CACHE:
        _CACHE["nc"] = build()
    nc = _CACHE["nc"]
    maps = _host_prep(inp)
    res = run_bass_kernel_spmd(nc, maps, core_ids=list(range(8)))
    R = res.results
    B, T = 4, 8192
    yp = np.zeros((B, T, D), np.float32)
    kv = np.zeros((B, T, 768), np.float32)
    for c in range(8):
        b, par = c // 2, c % 2
        yp[b].reshape(64, 128, D)[par::2] = R[c]["y_o"]
        kv[b].reshape(64, 128, 768)[par::2] = R[c]["kv_o"]
    kv6 = kv.reshape(B, T, 6, 2, 64)
    outs = [yp, np.concatenate([R[c]["ys_o"] for c in range(8)], 0).reshape(32, 8, D)]
    for j in range(4):
        outs.append(np.ascontiguousarray(kv6[None, :, :, j]))
    for j in (4, 5):
        outs.append(np.ascontiguousarray(kv6[None, :, -512:, j]))
    pconv = np.stack([R[2 * b + 1]["conv_o"] for b in range(B)], 0)[None]
    outs.append(pconv)
    skv = np.concatenate([R[c]["skv_o"].reshape(4, 8, 6, 2, 64) for c in range(8)], 0)
    for j in range(4):
        outs.append(np.ascontiguousarray(skv[None, :, :, j]))
    swin = np.concatenate([R[c]["swin_o"] for c in range(8)], 0)
    for j in range(2):
        outs.append(np.ascontiguousarray(swin[None, :, j].reshape(1, 32, 512, 2, 64)))
    outs.append(np.concatenate([R[c]["sconv_o"] for c in range(8)], 0)[None])
    return tuple(outs)
```
